# Optimizing a Trainium2 kernel written in Bass

```python
import math
import jax
import jax.numpy as jnp
from jax import lax
import numpy as np


D_MODEL = 1024
BATCH = 16
SEQ = 4096
DEPTH = 2

MEM_LEN = 256
RMS_EPS = 1e-6
ROPE_THETA = 10000.0
NEG_INF = -1e30
FORCE_BONUS = 1e3

NSA_HEADS = 8
NSA_KV_GROUPS = 2
NSA_HPG = NSA_HEADS // NSA_KV_GROUPS
NSA_HD = 64
CMP_LEN = 32
CMP_STRIDE = 16
CMP_HID = 2 * NSA_HD
SLC_BLK = 64
SLC_TOPK = 16
WINDOW = 512
NSA_QBLK = 32

GDN_HEADS = 8
GDN_HD = 64
GDN_CONV = 4
GDN_CHUNK = 64

SC_WIDTH = 3

XA_HEADS = 4
XA_HD = 128

D_FF = 4 * D_MODEL

NSA_WIDTH = NSA_HEADS * NSA_HD
NSA_KV_WIDTH = NSA_KV_GROUPS * NSA_HD
GDN_WIDTH = GDN_HEADS * GDN_HD
MIX_WIDTH = NSA_WIDTH + GDN_WIDTH
XA_WIDTH = XA_HEADS * XA_HD
IN_SIZES = (NSA_WIDTH, NSA_KV_WIDTH, NSA_KV_WIDTH, NSA_KV_WIDTH, NSA_KV_WIDTH, NSA_KV_WIDTH, NSA_KV_WIDTH, 3 * NSA_HEADS, GDN_WIDTH, GDN_WIDTH, GDN_WIDTH, GDN_HEADS, GDN_HEADS, GDN_WIDTH)
IN_COLS = sum(IN_SIZES)
N_HYB = (DEPTH + 1) // 2
N_SC = DEPTH // 2

kernel_name = 'hybrid_nsa_gdn_shortconv_decoder'


def rms_norm(x, w):
    xf = x.astype(jnp.float32)
    y = xf * lax.rsqrt(jnp.mean(xf * xf, axis=-1, keepdims=True) + RMS_EPS)
    return (y * w.astype(jnp.float32)).astype(x.dtype)


def l2_norm(x):
    xf = x.astype(jnp.float32)
    return xf * lax.rsqrt(jnp.sum(xf * xf, axis=-1, keepdims=True) + 1e-6)


def rope(x, positions):
    d = x.shape[-1]
    inv_freq = ROPE_THETA ** (-jnp.arange(0, d, 2, dtype=jnp.float32) / d)
    ang = positions.astype(jnp.float32)[..., None] * inv_freq
    cos = jnp.cos(ang)[:, :, None, :]
    sin = jnp.sin(ang)[:, :, None, :]
    xf = x.astype(jnp.float32)
    x1, x2 = xf[..., : d // 2], xf[..., d // 2:]
    return jnp.concatenate([x1 * cos - x2 * sin, x2 * cos + x1 * sin], axis=-1).astype(x.dtype)


def causal_dwconv(x, w):
    k_len, ch = w.shape
    return lax.conv_general_dilated(x, w[:, None, :].astype(x.dtype), window_strides=(1,), padding=[(k_len - 1, 0)], dimension_numbers=('NWC', 'WIO', 'NWC'), feature_group_count=ch)


def compress_blocks(t, pos_emb, w1, w2):
    b_, s_, g_, d = t.shape
    n_cmp = (s_ - CMP_LEN) // CMP_STRIDE + 1
    idx = np.arange(n_cmp)[:, None] * CMP_STRIDE + np.arange(CMP_LEN)[None, :]
    blocks = t[:, idx] + pos_emb[None, None, :, None, :]
    flat = blocks.transpose(0, 3, 1, 2, 4).reshape(b_, g_, n_cmp, CMP_LEN * d)
    return jax.nn.silu(flat @ w1) @ w2


def cmp_to_slc_matrix(n_cmp, n_slc):
    cs = np.arange(n_cmp)[:, None] * CMP_STRIDE
    js = np.arange(n_slc)[None, :] * SLC_BLK
    ov = np.clip(np.minimum(cs + CMP_LEN, js + SLC_BLK) - np.maximum(cs, js), 0, None)
    return jnp.asarray(ov / CMP_LEN, dtype=jnp.float32)


def nsa_attention(q, k_cmp, v_cmp, k_slc, v_slc, k_win, v_win, gates, ck_pos, ck_w1, ck_w2, cv_pos, cv_w1, cv_w2):
    b_, s_ = q.shape[:2]
    g_, hpg, d = NSA_KV_GROUPS, NSA_HPG, NSA_HD
    scale = d ** -0.5
    kc = compress_blocks(k_cmp, ck_pos, ck_w1, ck_w2)
    vc = compress_blocks(v_cmp, cv_pos, cv_w1, cv_w2)
    n_cmp = kc.shape[2]
    cmp_end = jnp.arange(n_cmp) * CMP_STRIDE + CMP_LEN - 1
    n_slc = s_ // SLC_BLK
    n_sel = min(SLC_TOPK, n_slc)
    overlap = cmp_to_slc_matrix(n_cmp, n_slc)
    ks_blk = k_slc.reshape(b_, n_slc, SLC_BLK, g_, d).transpose(0, 3, 1, 2, 4)
    vs_blk = v_slc.reshape(b_, n_slc, SLC_BLK, g_, d).transpose(0, 3, 1, 2, 4)
    kw = jnp.pad(k_win, ((0, 0), (WINDOW, 0), (0, 0), (0, 0))).transpose(0, 2, 1, 3)
    vw = jnp.pad(v_win, ((0, 0), (WINDOW, 0), (0, 0), (0, 0))).transpose(0, 2, 1, 3)
    n_qb = s_ // NSA_QBLK
    qb = q.reshape(b_, n_qb, NSA_QBLK, g_, hpg, d).transpose(1, 0, 3, 4, 2, 5)
    gb = gates.reshape(b_, n_qb, NSA_QBLK, g_, hpg, 3).transpose(1, 0, 3, 4, 2, 5)
    b_ix = jnp.arange(b_)[:, None, None, None]
    g_ix = jnp.arange(g_)[None, :, None, None]
    slc_j = jnp.arange(n_slc)

    def query_block(args):
        qc, gc, c = args
        t = c * NSA_QBLK + jnp.arange(NSA_QBLK)
        s = jnp.einsum('bghqd,bgnd->bghqn', qc, kc).astype(jnp.float32) * scale
        valid = cmp_end[None, :] <= t[:, None]
        p_cmp = jax.nn.softmax(jnp.where(valid, s, NEG_INF), axis=-1) * valid
        o_cmp = jnp.einsum('bghqn,bgnd->bghqd', p_cmp.astype(vc.dtype), vc)
        imp = jnp.einsum('bghqn,nj->bgqj', p_cmp, overlap)
        cur = t // SLC_BLK
        forced = (slc_j[None, :] == 0) | (slc_j[None, :] == cur[:, None]) | (slc_j[None, :] == cur[:, None] - 1)
        causal_blk = slc_j[None, :] * SLC_BLK <= t[:, None]
        imp = jnp.where(causal_blk, jnp.where(forced, imp + FORCE_BONUS, imp), -1.0)
        _, sel = lax.top_k(imp, n_sel)
        kg = ks_blk[b_ix, g_ix, sel]
        vg = vs_blk[b_ix, g_ix, sel]
        tok = sel[..., None] * SLC_BLK + jnp.arange(SLC_BLK)
        tok_ok = tok <= t[None, None, :, None, None]
        s = jnp.einsum('bghqd,bgqnld->bghqnl', qc, kg).astype(jnp.float32) * scale
        s = jnp.where(tok_ok[:, :, None], s, NEG_INF).reshape(b_, g_, hpg, NSA_QBLK, n_sel * SLC_BLK)
        p = jax.nn.softmax(s, axis=-1).reshape(b_, g_, hpg, NSA_QBLK, n_sel, SLC_BLK)
        o_slc = jnp.einsum('bghqnl,bgqnld->bghqd', p.astype(vg.dtype), vg)
        kwc = lax.dynamic_slice_in_dim(kw, c * NSA_QBLK, NSA_QBLK + WINDOW, axis=2)
        vwc = lax.dynamic_slice_in_dim(vw, c * NSA_QBLK, NSA_QBLK + WINDOW, axis=2)
        kpos = c * NSA_QBLK - WINDOW + jnp.arange(NSA_QBLK + WINDOW)
        win_ok = (kpos[None, :] <= t[:, None]) & (kpos[None, :] > t[:, None] - WINDOW) & (kpos[None, :] >= 0)
        s = jnp.einsum('bghqd,bgkd->bghqk', qc, kwc).astype(jnp.float32) * scale
        p = jax.nn.softmax(jnp.where(win_ok, s, NEG_INF), axis=-1)
        o_win = jnp.einsum('bghqk,bgkd->bghqd', p.astype(vwc.dtype), vwc)
        return gc[..., 0:1] * o_cmp + gc[..., 1:2] * o_slc + gc[..., 2:3] * o_win

    o = lax.map(query_block, (qb, gb, jnp.arange(n_qb)))
    return o.transpose(1, 0, 4, 2, 3, 5).reshape(b_, s_, NSA_HEADS * d)


def chunk_gated_delta_rule(q, k, v, g, beta):
    b_, s_, h_, dk = q.shape
    dv = v.shape[-1]
    c_len = GDN_CHUNK
    n_ch = s_ // c_len

    def to_chunks(t):
        t = t.reshape(b_, n_ch, c_len, h_, *t.shape[3:])
        return jnp.moveaxis(t, (1, 3), (0, 2))

    qc, kc, vc, bc = to_chunks(q), to_chunks(k), to_chunks(v), to_chunks(beta)
    gcs = jnp.cumsum(to_chunks(g), axis=-1)
    incl = jnp.tril(jnp.ones((c_len, c_len), dtype=bool))
    strict = jnp.tril(jnp.ones((c_len, c_len), dtype=bool), -1)
    decay = jnp.exp(jnp.where(incl, gcs[..., :, None] - gcs[..., None, :], NEG_INF))
    kb = kc * bc[..., None]
    a_mat = jnp.where(strict, jnp.einsum('nbhcd,nbhed->nbhce', kb, kc) * decay, 0.0)
    eye = jnp.eye(c_len, dtype=jnp.float32)
    t_inv = lax.linalg.triangular_solve(eye + a_mat, jnp.broadcast_to(eye, a_mat.shape), left_side=True, lower=True, unit_diagonal=True)
    u = t_inv @ (vc * bc[..., None])
    w = t_inv @ (kb * jnp.exp(gcs)[..., None])
    qk = jnp.einsum('nbhcd,nbhed->nbhce', qc, kc) * decay

    def step(state, xs):
        q_i, k_i, u_i, w_i, qk_i, g_i = xs
        v_new = u_i - w_i @ state
        o_i = (q_i * jnp.exp(g_i)[..., None]) @ state + qk_i @ v_new
        g_last = g_i[..., -1:]
        state = state * jnp.exp(g_last)[..., None] + jnp.einsum('bhcd,bhce->bhde', k_i * jnp.exp(g_last - g_i)[..., None], v_new)
        return state, o_i

    state0 = jnp.zeros((b_, h_, dk, dv), jnp.float32)
    _, o = lax.scan(step, state0, (qc, kc, u, w, qk, gcs))
    return jnp.moveaxis(o, (0, 2), (1, 3)).reshape(b_, s_, h_, dv)


def gated_deltanet(q, k, v, a, b, z, conv_w, a_log, dt_bias, norm_w):
    b_, s_ = q.shape[:2]
    h_, d = GDN_HEADS, GDN_HD
    qkv = jax.nn.silu(causal_dwconv(jnp.concatenate([q, k, v], axis=-1), conv_w))
    q, k, v = jnp.split(qkv, 3, axis=-1)
    q = l2_norm(q.reshape(b_, s_, h_, d)) * (d ** -0.5)
    k = l2_norm(k.reshape(b_, s_, h_, d))
    v = v.reshape(b_, s_, h_, d).astype(jnp.float32)
    beta = jax.nn.sigmoid(b.astype(jnp.float32))
    g = -jnp.exp(a_log.astype(jnp.float32)) * jax.nn.softplus(a.astype(jnp.float32) + dt_bias.astype(jnp.float32))
    o = chunk_gated_delta_rule(q, k, v, g, beta)
    o = rms_norm(o, norm_w) * jax.nn.silu(z.reshape(b_, s_, h_, d).astype(jnp.float32))
    return o.reshape(b_, s_, h_ * d).astype(z.dtype)


def hybrid_mixer(h, positions, w_in, ck_pos, ck_w1, ck_w2, cv_pos, cv_w1, cv_w2, gdn_conv, gdn_a_log, gdn_dt_bias, gdn_norm, w_out):
    b_, s_, _ = h.shape
    splits = [int(p) for p in np.cumsum(IN_SIZES)[:-1]]
    (nq, kcmp, vcmp, kslc, vslc, kwin, vwin, ngate, gq, gk, gv, ga, gb, gz) = jnp.split(h @ w_in, splits, axis=-1)

    def heads(t, n):
        return t.reshape(b_, s_, n, -1)

    g_ = NSA_KV_GROUPS
    o_nsa = nsa_attention(rope(heads(nq, NSA_HEADS), positions), heads(kcmp, g_), heads(vcmp, g_), rope(heads(kslc, g_), positions), heads(vslc, g_), rope(heads(kwin, g_), positions), heads(vwin, g_), jax.nn.sigmoid(ngate).reshape(b_, s_, NSA_HEADS, 3), ck_pos, ck_w1, ck_w2, cv_pos, cv_w1, cv_w2)
    o_gdn = gated_deltanet(gq, gk, gv, ga, gb, gz, gdn_conv, gdn_a_log, gdn_dt_bias, gdn_norm)
    return jnp.concatenate([o_nsa, o_gdn], axis=-1) @ w_out


def short_conv_mixer(h, w_in, conv_w, w_out):
    b_gate, c_gate, u = jnp.split(h @ w_in, 3, axis=-1)
    return (b_gate * causal_dwconv(c_gate * u, conv_w)) @ w_out


def cross_attention(h, mem_n, wq, wkv, wo):
    b_, s_, _ = h.shape
    q = (h @ wq).reshape(b_, s_, XA_HEADS, XA_HD)
    k, v = jnp.split(mem_n @ wkv, 2, axis=-1)
    k = k.reshape(b_, -1, XA_HEADS, XA_HD)
    v = v.reshape(b_, -1, XA_HEADS, XA_HD)
    s = jnp.einsum('bshd,bmhd->bhsm', q, k).astype(jnp.float32) * (XA_HD ** -0.5)
    p = jax.nn.softmax(s, axis=-1).astype(v.dtype)
    o = jnp.einsum('bhsm,bmhd->bshd', p, v).reshape(b_, s_, XA_WIDTH)
    return o @ wo


def squared_relu_mlp(h, w1, w2):
    return jnp.square(jax.nn.relu(h @ w1)) @ w2


def setup_inputs(seed: int = 0) -> dict:
    key = jax.random.key(seed)
    ks = list(jax.random.split(key, 32))
    f32 = jnp.float32

    def dense(shape, fan_in, gain=1.0):
        return jax.random.normal(ks.pop(), shape, f32) * (gain * fan_in ** -0.5)

    def norm_gain(shape):
        return 1.0 + 0.05 * jax.random.normal(ks.pop(), shape, f32)

    x = jax.random.normal(ks.pop(), (BATCH, SEQ, D_MODEL), f32)
    mem = jax.random.normal(ks.pop(), (BATCH, MEM_LEN, D_MODEL), f32)
    offs = jax.random.randint(ks.pop(), (BATCH, 1), 0, 1024, dtype=jnp.int32)
    positions = offs + jnp.arange(SEQ, dtype=jnp.int32)[None, :]
    dt = jnp.exp(jax.random.uniform(ks.pop(), (N_HYB, GDN_HEADS), f32, math.log(1e-3), math.log(1e-1)))
    return {
        'x': x,
        'mem': mem,
        'positions': positions,
        'norm_mix': norm_gain((DEPTH, D_MODEL)),
        'norm_xattn': norm_gain((DEPTH, D_MODEL)),
        'norm_mlp': norm_gain((DEPTH, D_MODEL)),
        'hyb_w_in': dense((N_HYB, D_MODEL, IN_COLS), D_MODEL),
        'hyb_cmp_k_pos': 0.02 * jax.random.normal(ks.pop(), (N_HYB, CMP_LEN, NSA_HD), f32),
        'hyb_cmp_k_w1': dense((N_HYB, CMP_LEN * NSA_HD, CMP_HID), CMP_LEN * NSA_HD),
        'hyb_cmp_k_w2': dense((N_HYB, CMP_HID, NSA_HD), CMP_HID),
        'hyb_cmp_v_pos': 0.02 * jax.random.normal(ks.pop(), (N_HYB, CMP_LEN, NSA_HD), f32),
        'hyb_cmp_v_w1': dense((N_HYB, CMP_LEN * NSA_HD, CMP_HID), CMP_LEN * NSA_HD),
        'hyb_cmp_v_w2': dense((N_HYB, CMP_HID, NSA_HD), CMP_HID),
        'hyb_gdn_conv': dense((N_HYB, GDN_CONV, 3 * GDN_WIDTH), GDN_CONV),
        'hyb_gdn_a_log': jnp.log(jax.random.uniform(ks.pop(), (N_HYB, GDN_HEADS), f32, 1.0, 16.0)),
        'hyb_gdn_dt_bias': dt + jnp.log(-jnp.expm1(-dt)),
        'hyb_gdn_norm': norm_gain((N_HYB, GDN_HD)),
        'hyb_w_out': dense((N_HYB, MIX_WIDTH, D_MODEL), MIX_WIDTH, 0.5),
        'sc_w_in': dense((N_SC, D_MODEL, 3 * D_MODEL), D_MODEL),
        'sc_conv': dense((N_SC, SC_WIDTH, D_MODEL), SC_WIDTH),
        'sc_w_out': dense((N_SC, D_MODEL, D_MODEL), D_MODEL, 0.5),
        'mem_norm': norm_gain((D_MODEL,)),
        'xa_wq': dense((DEPTH, D_MODEL, XA_WIDTH), D_MODEL),
        'xa_wkv': dense((DEPTH, D_MODEL, 2 * XA_WIDTH), D_MODEL),
        'xa_wo': dense((DEPTH, XA_WIDTH, D_MODEL), XA_WIDTH, 0.5),
        'mlp_w1': dense((DEPTH, D_MODEL, D_FF), D_MODEL),
        'mlp_w2': dense((DEPTH, D_FF, D_MODEL), D_FF, 0.5),
        'final_norm': norm_gain((D_MODEL,)),
    }


def reference(x, mem, positions, norm_mix, norm_xattn, norm_mlp, hyb_w_in, hyb_cmp_k_pos, hyb_cmp_k_w1, hyb_cmp_k_w2, hyb_cmp_v_pos, hyb_cmp_v_w1, hyb_cmp_v_w2, hyb_gdn_conv, hyb_gdn_a_log, hyb_gdn_dt_bias, hyb_gdn_norm, hyb_w_out, sc_w_in, sc_conv, sc_w_out, mem_norm, xa_wq, xa_wkv, xa_wo, mlp_w1, mlp_w2, final_norm):
    mem_n = rms_norm(mem, mem_norm)
    for layer in range(DEPTH):
        j = layer // 2
        hn = rms_norm(x, norm_mix[layer])
        if layer % 2 == 0:
            mix = hybrid_mixer(hn, positions, hyb_w_in[j], hyb_cmp_k_pos[j], hyb_cmp_k_w1[j], hyb_cmp_k_w2[j], hyb_cmp_v_pos[j], hyb_cmp_v_w1[j], hyb_cmp_v_w2[j], hyb_gdn_conv[j], hyb_gdn_a_log[j], hyb_gdn_dt_bias[j], hyb_gdn_norm[j], hyb_w_out[j])
        else:
            mix = short_conv_mixer(hn, sc_w_in[j], sc_conv[j], sc_w_out[j])
        x = x + mix
        x = x + cross_attention(rms_norm(x, norm_xattn[layer]), mem_n, xa_wq[layer], xa_wkv[layer], xa_wo[layer])
        x = x + squared_relu_mlp(rms_norm(x, norm_mlp[layer]), mlp_w1[layer], mlp_w2[layer])
    return rms_norm(x, final_norm)
```

```python
import numpy as np
from contextlib import ExitStack
import ml_dtypes
import concourse.bass as bass
import concourse.mybir as mybir
from concourse.bass_utils import run_bass_kernel_spmd

F32 = mybir.dt.float32
BF16 = mybir.dt.bfloat16
I32 = mybir.dt.int32
AF = mybir.ActivationFunctionType
ALU = mybir.AluOpType
AX = mybir.AxisListType

NCORES = 8
SEQ = 4096
DM = 1024
NSEQ = 2
NTOK = NSEQ * SEQ
MEM = 256
EPS = 1e-6
NDS = 8


class Tile:
    __slots__ = ("t", "w", "r", "name")

    def __init__(self, t, name=""):
        self.t = t
        self.w = None
        self.r = {}
        self.name = name

    def __getitem__(self, idx):
        return self.t[idx]


class Eng:
    def __init__(self, name, h, semid):
        self.name, self.h, self.semid = name, h, semid
        self.count = 0
        self.known = {}


class KB:
    def __init__(self):
        self.nc = bass.Bass("TRN2", target_bir_lowering=False)
        self.es = ExitStack()
        nc = self.nc
        self.sems = []
        self.E = {}
        for name, h in [("pe", nc.tensor), ("act", nc.scalar), ("dve", nc.vector),
                        ("pool", nc.gpsimd), ("sp", nc.sync)]:
            self.sems.append(self.es.enter_context(nc.semaphore("s_" + name)))
            self.E[name] = Eng(name, h, len(self.sems) - 1)
        self.dq = {}
        for q in ("sp", "pool", "act"):
            ids = []
            for i in range(NDS):
                self.sems.append(self.es.enter_context(nc.semaphore(f"d_{q}{i}")))
                ids.append(len(self.sems) - 1)
            self.dq[q] = dict(ids=ids, vals=[0] * NDS, idx=0)
        self.n_ins = 0
        self.n_wait = 0
        self.uid = 0

    def sb(self, stack, shape, dt, name=None):
        self.uid += 1
        name = f"{name or 't'}_{self.uid}"
        return Tile(stack.enter_context(self.nc.sbuf_tensor(name, list(shape), dt)), name)

    def ps(self, stack, shape, dt, name=None):
        self.uid += 1
        name = f"{name or 'p'}_{self.uid}"
        return Tile(stack.enter_context(self.nc.psum_tensor(name, list(shape), dt)), name)

    def dram(self, name, shape, dt, kind="Internal"):
        return Tile(self.nc.dram_tensor(name, list(shape), dt, kind=kind).ap(), name)

    def _wait(self, eng, s, v):
        if s == eng.semid or eng.known.get(s, 0) >= v:
            return
        eng.h.wait_ge(self.sems[s], v)
        eng.known[s] = v
        self.n_wait += 1

    def _deps(self, eng, reads, writes):
        deps = {}
        for t in reads:
            if t.w is not None and deps.get(t.w[0], 0) < t.w[1]:
                deps[t.w[0]] = t.w[1]
            if t.w is not None and t.w[0] == eng.semid and eng.name != "pe" and eng.known.get(eng.semid, 0) < t.w[1]:
                eng.h.wait_ge(self.sems[eng.semid], t.w[1])
                eng.known[eng.semid] = t.w[1]
                self.n_wait += 1
        for t in writes:
            if t.w is not None and deps.get(t.w[0], 0) < t.w[1]:
                deps[t.w[0]] = t.w[1]
            for s, v in t.r.items():
                if deps.get(s, 0) < v:
                    deps[s] = v
        for s, v in deps.items():
            self._wait(eng, s, v)

    def _mark(self, ev, reads, writes):
        s, v = ev
        for t in reads:
            if t.r.get(s, 0) < v:
                t.r[s] = v
        for t in writes:
            t.w = ev
            t.r = {}

    def op(self, en, fn, reads=(), writes=()):
        eng = self.E[en]
        self._deps(eng, reads, writes)
        ins = fn(eng.h)
        eng.count += 1
        ins.then_inc(self.sems[eng.semid], 1)
        self._mark((eng.semid, eng.count), reads, writes)
        self.n_ins += 1

    def dma(self, q, out, in_, reads=(), writes=(), **kw):
        eng = self.E[q]
        dq = self.dq[q]
        i = dq["idx"]
        dq["idx"] = (i + 1) % NDS
        semid = dq["ids"][i]
        prev = dq["vals"][i]
        self._deps(eng, reads, writes)
        if prev > 0:
            self._wait(eng, semid, prev)
        eng.h.dma_start(out=out, in_=in_, **kw).then_inc(self.sems[semid], 16)
        dq["vals"][i] = prev + 16
        self._mark((semid, prev + 16), reads, writes)
        self.n_ins += 1

    def barrier(self):
        cur = {}
        for e in self.E.values():
            cur[e.semid] = e.count
        for dq in self.dq.values():
            for sid, v in zip(dq["ids"], dq["vals"]):
                cur[sid] = v
        for e in self.E.values():
            for s, v in cur.items():
                if v > 0:
                    self._wait(e, s, v)

    def finish(self):
        self.barrier()
        self.es.close()


def load_w(kb, dst, src, src_ap, KC, cols, gain, stg, dst_col0=0, src_col0=0, scale=None, cnt=[0]):
    SC = stg[0].t.shape[1]
    for kc in range(KC):
        for c0 in range(0, cols, SC):
            cw = min(SC, cols - c0)
            st = stg[cnt[0] % len(stg)]
            kb.dma("sp", st[:, :cw], src_ap[kc * 128:(kc + 1) * 128, src_col0 + c0:src_col0 + c0 + cw],
                   reads=[src], writes=[st])
            o = dst[:, kc, dst_col0 + c0:dst_col0 + c0 + cw]
            if gain is not None:
                g = gain[:, kc:kc + 1]
                if scale is None:
                    if cnt[0] % 2 == 0:
                        kb.op("dve", lambda e: e.tensor_scalar(out=o, in0=st[:, :cw], scalar1=g, scalar2=None, op0=ALU.mult),
                              reads=[st, gain], writes=[dst])
                    else:
                        kb.op("act", lambda e: e.activation(out=o, in_=st[:, :cw], func=AF.Copy, scale=g),
                              reads=[st, gain], writes=[dst])
                else:
                    kb.op("dve", lambda e: e.tensor_scalar(out=o, in0=st[:, :cw], scalar1=g, scalar2=float(scale), op0=ALU.mult, op1=ALU.mult),
                          reads=[st, gain], writes=[dst])
            else:
                sc = 1.0 if scale is None else float(scale)
                if cnt[0] % 2 == 0:
                    kb.op("dve", lambda e: e.tensor_scalar(out=o, in0=st[:, :cw], scalar1=sc, scalar2=None, op0=ALU.mult),
                          reads=[st], writes=[dst])
                else:
                    kb.op("act", lambda e: e.activation(out=o, in_=st[:, :cw], func=AF.Copy, scale=sc),
                          reads=[st], writes=[dst])
            cnt[0] += 1


def load_gain_col(kb, dst, src, row):
    ap = src[row] if row is not None else src.t
    kb.dma("sp", dst[:, :], ap.rearrange("(k p) -> p k", p=128), reads=[src], writes=[dst],
           allow_slow_non_contiguous=True)


def rms_rstd(kb, x_ap, xt, junk, ss, rs, D):
    kb.op("act", lambda e: e.activation(out=junk[:, :D], in_=x_ap, func=AF.Square), reads=[xt], writes=[junk])
    kb.op("dve", lambda e: e.tensor_reduce(out=ss[:, 0:1], in_=junk[:, :D], axis=AX.X, op=ALU.add), reads=[junk], writes=[ss])
    kb.op("dve", lambda e: e.tensor_scalar(out=ss[:, 0:1], in0=ss[:, 0:1], scalar1=float(D * EPS), scalar2=None,
                                           op0=ALU.add), reads=[ss], writes=[ss])
    kb.op("dve", lambda e: e.reciprocal(out=ss[:, 0:1], in_=ss[:, 0:1]), reads=[ss], writes=[ss])
    kb.op("act", lambda e: e.activation(out=rs[:, 0:1], in_=ss[:, 0:1], func=AF.Sqrt), reads=[ss], writes=[rs])


def phase_mlp(kb, x_in, x_out, w1, w2, gain, layer, ident_d, final_gain=None, ntok=NTOK):
    nc = kb.nc
    G = 256
    NG = ntok // G
    with ExitStack() as st:
        W1 = kb.sb(st, [128, 8, 4096], BF16, "W1")
        W2 = kb.sb(st, [128, 32, 1024], BF16, "W2")
        gcol = kb.sb(st, [128, 8], F32, "gcol")
        ident = kb.sb(st, [128, 128], BF16, "ident")
        kb.dma("pool", ident[:, :], ident_d[:, :], reads=[ident_d], writes=[ident])
        load_gain_col(kb, gcol, gain, layer)
        with ExitStack() as st2:
            stg = [kb.sb(st2, [128, 2048], F32, "stg") for _ in range(2)]
            load_w(kb, W1, w1, w1[layer], 8, 4096, gcol, stg)
            load_w(kb, W2, w2, w2[layer], 32, 1024, None, stg)
            kb.barrier()
        xt = [kb.sb(st, [128, 2, 1024], F32, "xt") for _ in range(3)]
        xn = kb.sb(st, [128, 2, 1024], BF16, "xn")
        xnT = [kb.sb(st, [128, 8, G], BF16, "xnT") for _ in range(2)]
        hT = [kb.sb(st, [128, 8, G], BF16, "hT") for _ in range(4)]
        rr = [kb.sb(st, [128, 2, G], BF16, "rr") for _ in range(2)]
        junk = kb.sb(st, [128, 1024], F32, "junk")
        ss = [kb.sb(st, [128, 2], F32, "ss") for _ in range(2)]
        rs = [kb.sb(st, [128, 1], F32, "rs") for _ in range(2)]
        pT = [kb.ps(st, [128, 8, 128], BF16, "pT") for _ in range(2)]
        pH = [kb.ps(st, [128, 2, G], F32, "pH") for _ in range(2)]
        pO = [kb.ps(st, [128, 512], F32, "pO") for _ in range(2)]
        fg = None
        if final_gain is not None:
            fg = kb.sb(st, [128, 1024], F32, "fg")
            kb.dma("pool", fg[:, :], final_gain.t.partition_broadcast(128), reads=[final_gain], writes=[fg])
        x_in_v = x_in.t.rearrange("(g t p) d -> g p t d", t=2, p=128)
        x_out_v = x_out.t.rearrange("(g t p) d -> g p t d", t=2, p=128)
        cnt = [0]

        def prep(g):
            b = g % 2
            kb.dma("sp", xt[g % 3][:, :, :], x_in_v[g], reads=[x_in], writes=[xt[g % 3]])
            for t in range(2):
                i = cnt[0] % 2
                cnt[0] += 1
                rms_rstd(kb, xt[g % 3][:, t, :], xt[g % 3], junk, ss[i], rs[i], 1024)
                kb.op("dve", lambda e: e.tensor_scalar(out=xn[:, t, :], in0=xt[g % 3][:, t, :], scalar1=rs[i][:, 0:1], scalar2=32.0,
                                                       op0=ALU.mult, op1=ALU.mult), reads=[xt[g % 3], rs[i]], writes=[xn])
                p = pT[i]
                for kc in range(8):
                    kb.op("pe", lambda e: e.transpose(out=p[:, kc, :], in_=xn[:, t, kc * 128:(kc + 1) * 128], identity=ident[:, :]),
                          reads=[xn, ident], writes=[p])
                kb.op("act", lambda e: e.activation(out=xnT[b][:, :, t * 128:(t + 1) * 128], in_=p[:, :, :], func=AF.Copy),
                      reads=[p], writes=[xnT[b]])

        prep(0)
        hcnt = 0
        ocnt = 0
        for g in range(NG):
            b = g % 2
            for fp in range(16):
                ph = pH[hcnt % 2]
                r = rr[hcnt % 2]
                hcnt += 1
                for j in range(2):
                    f = fp * 2 + j
                    for kc in range(8):
                        kb.op("pe", lambda e: e.matmul(ph[:, j, :], lhsT=W1[:, kc, f * 128:(f + 1) * 128], rhs=xnT[b][:, kc, :],
                                                       start=(kc == 0), stop=(kc == 7)),
                              reads=[W1, xnT[b]], writes=[ph])
                kb.op("act", lambda e: e.activation(out=r[:, :, :], in_=ph[:, :, :], func=AF.Relu), reads=[ph], writes=[r])
                h = hT[fp // 4]
                kb.op("dve", lambda e: e.tensor_tensor(out=h[:, (fp % 4) * 2:(fp % 4) * 2 + 2, :], in0=r[:, :, :], in1=r[:, :, :], op=ALU.mult),
                      reads=[r], writes=[h])
            if g + 1 < NG:
                prep(g + 1)
            for t in range(2):
                for half in range(2):
                    po = pO[ocnt % 2]
                    ocnt += 1
                    for f in range(32):
                        h = hT[f // 8]
                        kb.op("pe", lambda e: e.matmul(po[:, :], lhsT=h[:, f % 8, t * 128:(t + 1) * 128], rhs=W2[:, f, half * 512:(half + 1) * 512],
                                                       start=(f == 0), stop=(f == 31)),
                              reads=[h, W2], writes=[po])
                    xs = xt[g % 3][:, t, half * 512:(half + 1) * 512]
                    kb.op("dve", lambda e: e.tensor_tensor(out=xs, in0=xs, in1=po[:, :], op=ALU.add), reads=[po, xt[g % 3]], writes=[xt[g % 3]])
                if fg is not None:
                    i = cnt[0] % 2
                    cnt[0] += 1
                    rms_rstd(kb, xt[g % 3][:, t, :], xt[g % 3], junk, ss[i], rs[i], 1024)
                    kb.op("dve", lambda e: e.tensor_scalar(out=xt[g % 3][:, t, :], in0=xt[g % 3][:, t, :], scalar1=rs[i][:, 0:1], scalar2=32.0,
                                                           op0=ALU.mult, op1=ALU.mult), reads=[xt[g % 3], rs[i]], writes=[xt[g % 3]])
                    kb.op("dve", lambda e: e.tensor_tensor(out=xt[g % 3][:, t, :], in0=xt[g % 3][:, t, :], in1=fg[:, :], op=ALU.mult),
                          reads=[xt[g % 3], fg], writes=[xt[g % 3]])
            kb.dma("sp", x_out_v[g], xt[g % 3][:, :, :], reads=[xt[g % 3]], writes=[x_out])
        kb.barrier()


C_Q, C_KCMP, C_VCMP, C_KSLC, C_VSLC, C_KWIN, C_VWIN, C_GATE = 0, 512, 640, 768, 896, 1024, 1152, 1280
C_GQ, C_GK, C_GV, C_GA, C_GB, C_GZ = 1304, 1816, 2328, 2840, 2848, 2856
IN_COLS = 3368
TM0 = 26 * 128
TMA = 296
WCOLS = TM0 + TMA + 512
PI = float(np.pi)


def make_scratch(kb, dbg=False):
    kind = "ExternalOutput" if dbg else "Internal"
    D = {}
    for s in range(NSEQ):
        D[f"QT{s}"] = kb.dram(f"QT{s}", [4, 128, SEQ], BF16, kind)
        D[f"KST{s}"] = kb.dram(f"KST{s}", [128, SEQ], BF16, kind)
        D[f"KWT{s}"] = kb.dram(f"KWT{s}", [128, SEQ], BF16, kind)
        D[f"KCT{s}"] = kb.dram(f"KCT{s}", [128, SEQ], BF16, kind)
        D[f"VCT{s}"] = kb.dram(f"VCT{s}", [128, SEQ], BF16, kind)
        D[f"VS{s}"] = kb.dram(f"VS{s}", [SEQ, 128], BF16, kind)
        D[f"VW{s}"] = kb.dram(f"VW{s}", [SEQ, 128], BF16, kind)
        D[f"GATE{s}"] = kb.dram(f"GATE{s}", [SEQ, 24], F32, kind)
        D[f"GQKV{s}"] = kb.dram(f"GQKV{s}", [12, 128, 4 + SEQ], F32, kind)
        D[f"GG{s}"] = kb.dram(f"GG{s}", [SEQ, 8], F32, kind)
        D[f"GB{s}"] = kb.dram(f"GB{s}", [SEQ, 8], F32, kind)
        D[f"GZ{s}"] = kb.dram(f"GZ{s}", [SEQ, 512], BF16, kind)
        D[f"MIX{s}"] = kb.dram(f"MIX{s}", [SEQ, 1024], BF16, kind)
    return D


def phase_a(kb, I, D, C, dbg_stop=99, dbg_groups=99):
    x, pos, w_in = I["x"], I["positions"], I["hyb_w_in"]
    with ExitStack() as st:
        W = kb.sb(st, [128, 8, WCOLS], BF16, "Win")
        gcol = kb.sb(st, [128, 8], F32, "gcol")
        ident = kb.sb(st, [128, 128], BF16, "ident")
        invf = kb.sb(st, [128, 1], F32, "invf")
        dtb = kb.sb(st, [128, 8], F32, "dtb")
        nea = kb.sb(st, [128, 8], F32, "nea")
        zero = kb.sb(st, [128, 4], F32, "zero")
        kb.dma("pool", ident[:, :], C["ident"][:, :], reads=[C["ident"]], writes=[ident])
        kb.dma("pool", invf[:, :], C["invf"][:, :], reads=[C["invf"]], writes=[invf])
        kb.dma("pool", dtb[:, :], I["hyb_gdn_dt_bias"][0].partition_broadcast(128), reads=[I["hyb_gdn_dt_bias"]], writes=[dtb])
        kb.dma("pool", nea[:, :], I["hyb_gdn_a_log"][0].partition_broadcast(128), reads=[I["hyb_gdn_a_log"]], writes=[nea])
        kb.op("act", lambda e: e.activation(out=nea[:, :], in_=nea[:, :], func=AF.Exp), reads=[nea], writes=[nea])
        kb.op("dve", lambda e: e.tensor_scalar(out=nea[:, :], in0=nea[:, :], scalar1=-1.0, scalar2=None, op0=ALU.mult), reads=[nea], writes=[nea])
        kb.op("dve", lambda e: e.memset(zero[:, :], 0.0), writes=[zero])
        for s in range(NSEQ):
            for c in range(12):
                kb.dma("pool", D[f"GQKV{s}"][c, :, 0:4], zero[:, :], reads=[zero], writes=[D[f"GQKV{s}"]])
        load_gain_col(kb, gcol, I["norm_mix"], 0)
        wi = w_in[0]
        with ExitStack() as st2:
            stg = [kb.sb(st2, [128, 512], F32, "stg") for _ in range(3)]
            load_w(kb, W, w_in, wi, 8, 512, gcol, stg, dst_col0=0, src_col0=C_Q, scale=0.125)
            load_w(kb, W, w_in, wi, 8, 128, gcol, stg, dst_col0=8 * 128, src_col0=C_KSLC)
            load_w(kb, W, w_in, wi, 8, 128, gcol, stg, dst_col0=10 * 128, src_col0=C_KWIN)
            load_w(kb, W, w_in, wi, 8, 128, gcol, stg, dst_col0=12 * 128, src_col0=C_KCMP)
            load_w(kb, W, w_in, wi, 8, 128, gcol, stg, dst_col0=13 * 128, src_col0=C_VCMP)
            load_w(kb, W, w_in, wi, 8, 512, gcol, stg, dst_col0=14 * 128, src_col0=C_GQ)
            load_w(kb, W, w_in, wi, 8, 512, gcol, stg, dst_col0=18 * 128, src_col0=C_GK)
            load_w(kb, W, w_in, wi, 8, 512, gcol, stg, dst_col0=22 * 128, src_col0=C_GV)
            load_w(kb, W, w_in, wi, 8, 128, gcol, stg, dst_col0=TM0, src_col0=C_VSLC)
            load_w(kb, W, w_in, wi, 8, 128, gcol, stg, dst_col0=TM0 + 128, src_col0=C_VWIN)
            load_w(kb, W, w_in, wi, 8, 24, gcol, stg, dst_col0=TM0 + 256, src_col0=C_GATE)
            load_w(kb, W, w_in, wi, 8, 16, gcol, stg, dst_col0=TM0 + 280, src_col0=C_GA)
            load_w(kb, W, w_in, wi, 8, 512, gcol, stg, dst_col0=TM0 + TMA, src_col0=C_GZ)
            k = 0
            for (src0, ncol, dst0, sc) in [(C_Q, 512, 4 * 128, 0.125), (C_KSLC, 128, 9 * 128, 1.0), (C_KWIN, 128, 11 * 128, 1.0)]:
                nh = ncol // 64
                for kc in range(8):
                    sg = stg[k % 3]
                    k += 1
                    kb.dma("sp", sg[:, :ncol], wi[kc * 128:(kc + 1) * 128, src0:src0 + ncol], reads=[w_in], writes=[sg])
                    sv = sg[:, :ncol].rearrange("p (h t c) -> p h t c", h=nh, t=2)
                    dv = W[:, kc, dst0:dst0 + ncol].rearrange("p (h t c) -> p h t c", h=nh, t=2)
                    g = gcol[:, kc:kc + 1]
                    kb.op("dve", lambda e: e.tensor_scalar(out=dv[:, :, 0, :], in0=sv[:, :, 1, :], scalar1=g, scalar2=-sc, op0=ALU.mult, op1=ALU.mult),
                          reads=[sg, gcol], writes=[W])
                    kb.op("dve", lambda e: e.tensor_scalar(out=dv[:, :, 1, :], in0=sv[:, :, 0, :], scalar1=g, scalar2=sc, op0=ALU.mult, op1=ALU.mult),
                          reads=[sg, gcol], writes=[W])
            kb.barrier()
        G = 512
        if dbg_stop <= 0:
            return
        xt = [kb.sb(st, [128, 4, 1024], F32, "xt") for _ in range(2)]
        xn = kb.sb(st, [128, 1024], BF16, "xn")
        xnT = [kb.sb(st, [128, 8, G], BF16, "xnT") for _ in range(2)]
        junk = kb.sb(st, [128, 1024], F32, "junk")
        ss = [kb.sb(st, [128, 2], F32, "ss") for _ in range(2)]
        rs = [kb.sb(st, [128, 1], F32, "rs") for _ in range(2)]
        posi = kb.sb(st, [128, G], I32, "posi")
        posl = [kb.sb(st, [128, G], I32, "posl") for _ in range(2)]
        posf = kb.sb(st, [128, G], F32, "posf")
        ang = kb.sb(st, [128, G], F32, "ang")
        kf = kb.sb(st, [128, G], F32, "kf")
        sinT = kb.sb(st, [128, G], F32, "sinT")
        cosT = kb.sb(st, [128, G], F32, "cosT")
        t1 = [kb.sb(st, [128, G], F32, "t1") for _ in range(2)]
        t2 = [kb.sb(st, [128, G], F32, "t2") for _ in range(2)]
        ob = [kb.sb(st, [128, G], BF16, "ob") for _ in range(3)]
        of = [kb.sb(st, [128, G], F32, "of") for _ in range(3)]
        tma = [kb.sb(st, [128, 256], BF16, "tma") for _ in range(2)]
        tmg = [kb.sb(st, [128, 40], F32, "tmg") for _ in range(2)]
        tmz = [kb.sb(st, [128, 512], BF16, "tmz") for _ in range(2)]
        pT = [kb.ps(st, [128, 8, 128], BF16, "pT") for _ in range(2)]
        pA = [kb.ps(st, [128, 512], F32, "pA") for _ in range(4)]
        pB = [kb.ps(st, [128, 512], F32, "pB") for _ in range(2)]
        cnt = [0]
        ca = [0]
        cb = [0]
        co = [0]
        cf = [0]

        def fm(c, b):
            p = pA[ca[0] % 4]
            ca[0] += 1
            for kc in range(8):
                kb.op("pe", lambda e: e.matmul(p[:, :], lhsT=W[:, kc, c * 128:(c + 1) * 128], rhs=xnT[b][:, kc, :], start=(kc == 0), stop=(kc == 7)),
                      reads=[W, xnT[b]], writes=[p])
            return p

        def load_a(g_):
            s_, gi_ = divmod(g_, SEQ // G)
            kb.dma("sp", xt[g_ % 2][:, :, :], x.t[s_, gi_ * G:(gi_ + 1) * G, :].rearrange("(t p) d -> p t d", p=128), reads=[x], writes=[xt[g_ % 2]])
            kb.dma("sp", posl[g_ % 2][:, :], pos.t[s_, gi_ * G:(gi_ + 1) * G].partition_broadcast(128), reads=[pos], writes=[posl[g_ % 2]])

        for s in range(NSEQ):
            for gi in range(SEQ // G):
                g = s * (SEQ // G) + gi
                b = g % 2
                tok0 = gi * G
                if g >= dbg_groups:
                    continue
                if g == 0:
                    load_a(0)
                if g + 1 < min(dbg_groups, NSEQ * (SEQ // G)):
                    load_a(g + 1)
                for t in range(4):
                    i = cnt[0] % 2
                    cnt[0] += 1
                    rms_rstd(kb, xt[b][:, t, :], xt[b], junk, ss[i], rs[i], 1024)
                    kb.op("dve", lambda e: e.tensor_scalar(out=xn[:, :], in0=xt[b][:, t, :], scalar1=rs[i][:, 0:1], scalar2=32.0,
                                                           op0=ALU.mult, op1=ALU.mult), reads=[xt[b], rs[i]], writes=[xn])
                    p = pT[i]
                    for kc in range(8):
                        kb.op("pe", lambda e: e.transpose(out=p[:, kc, :], in_=xn[:, kc * 128:(kc + 1) * 128], identity=ident[:, :]),
                              reads=[xn, ident], writes=[p])
                    kb.op("act", lambda e: e.activation(out=xnT[b][:, :, t * 128:(t + 1) * 128], in_=p[:, :, :], func=AF.Copy),
                          reads=[p], writes=[xnT[b]])
                if dbg_stop <= 1:
                    continue
                kb.op("dve", lambda e: e.tensor_copy(out=posf[:, :], in_=posl[b][:, :]), reads=[posl[b]], writes=[posf])
                kb.op("dve", lambda e: e.tensor_scalar(out=ang[:, :], in0=posf[:, :], scalar1=invf[:, 0:1], scalar2=None, op0=ALU.mult),
                      reads=[posf, invf], writes=[ang])
                for (dst, shift) in [(sinT, 0.0), (cosT, 0.5 * PI)]:
                    kb.op("dve", lambda e: e.tensor_scalar(out=posf[:, :], in0=ang[:, :], scalar1=float(shift), scalar2=None, op0=ALU.add),
                          reads=[ang], writes=[posf])
                    kb.op("dve", lambda e: e.tensor_scalar(out=posi[:, :], in0=posf[:, :], scalar1=float(1.0 / (2 * PI)), scalar2=None, op0=ALU.mult),
                          reads=[posf], writes=[posi])
                    kb.op("dve", lambda e: e.tensor_copy(out=kf[:, :], in_=posi[:, :]), reads=[posi], writes=[kf])
                    kb.op("dve", lambda e: e.scalar_tensor_tensor(out=posf[:, :], in0=kf[:, :], scalar=float(-2 * PI), in1=posf[:, :], op0=ALU.mult, op1=ALU.add),
                          reads=[kf, posf], writes=[posf])
                    kb.op("dve", lambda e: e.tensor_scalar(out=kf[:, :], in0=posf[:, :], scalar1=PI, scalar2=float(-2 * PI), op0=ALU.is_gt, op1=ALU.mult),
                          reads=[posf], writes=[kf])
                    kb.op("dve", lambda e: e.tensor_tensor(out=posf[:, :], in0=posf[:, :], in1=kf[:, :], op=ALU.add), reads=[posf, kf], writes=[posf])
                    kb.op("dve", lambda e: e.tensor_scalar(out=posf[:, :], in0=posf[:, :], scalar1=PI, scalar2=-PI, op0=ALU.min, op1=ALU.max),
                          reads=[posf], writes=[posf])
                    kb.op("act", lambda e: e.activation(out=dst[:, :], in_=posf[:, :], func=AF.Sin), reads=[posf], writes=[dst])
                if dbg_stop <= 2:
                    continue
                ropes = [(c, c + 4, D[f"QT{s}"], D[f"QT{s}"][c, :, tok0:tok0 + G]) for c in range(4)]
                ropes += [(8, 9, D[f"KST{s}"], D[f"KST{s}"][:, tok0:tok0 + G]), (10, 11, D[f"KWT{s}"], D[f"KWT{s}"][:, tok0:tok0 + G])]
                for (c1, c2, dtile, dap) in ropes:
                    p1 = fm(c1, b)
                    p2 = fm(c2, b)
                    j = cb[0] % 2
                    cb[0] += 1
                    o = ob[co[0] % 3]
                    co[0] += 1
                    kb.op("dve", lambda e: e.tensor_tensor(out=t1[j][:, :], in0=p1[:, :], in1=cosT[:, :], op=ALU.mult), reads=[p1, cosT], writes=[t1[j]])
                    kb.op("dve", lambda e: e.tensor_tensor(out=t2[j][:, :], in0=p2[:, :], in1=sinT[:, :], op=ALU.mult), reads=[p2, sinT], writes=[t2[j]])
                    kb.op("pool", lambda e: e.tensor_tensor(out=o[:, :], in0=t1[j][:, :], in1=t2[j][:, :], op=ALU.add), reads=[t1[j], t2[j]], writes=[o])
                    kb.dma("sp", dap, o[:, :], reads=[o], writes=[dtile])
                for (c, dname) in [(12, "KCT"), (13, "VCT")]:
                    p1 = fm(c, b)
                    o = ob[co[0] % 3]
                    co[0] += 1
                    kb.op("act", lambda e: e.activation(out=o[:, :], in_=p1[:, :], func=AF.Copy), reads=[p1], writes=[o])
                    kb.dma("sp", D[f"{dname}{s}"][:, tok0:tok0 + G], o[:, :], reads=[o], writes=[D[f"{dname}{s}"]])
                for c in range(12):
                    p1 = fm(14 + c, b)
                    o = of[cf[0] % 3]
                    cf[0] += 1
                    kb.op("act", lambda e: e.activation(out=o[:, :], in_=p1[:, :], func=AF.Copy), reads=[p1], writes=[o])
                    kb.dma("sp", D[f"GQKV{s}"][c, :, 4 + tok0:4 + tok0 + G], o[:, :], reads=[o], writes=[D[f"GQKV{s}"]])
                if dbg_stop <= 3:
                    continue
                for t in range(4):
                    r0 = tok0 + t * 128
                    j = t % 2
                    pa = pB[0]
                    pz = pB[1]
                    for kc in range(8):
                        kb.op("pe", lambda e: e.matmul(pa[:, :TMA], lhsT=xnT[b][:, kc, t * 128:(t + 1) * 128], rhs=W[:, kc, TM0:TM0 + TMA], start=(kc == 0), stop=(kc == 7)),
                              reads=[W, xnT[b]], writes=[pa])
                    for kc in range(8):
                        kb.op("pe", lambda e: e.matmul(pz[:, :], lhsT=xnT[b][:, kc, t * 128:(t + 1) * 128], rhs=W[:, kc, TM0 + TMA:TM0 + TMA + 512], start=(kc == 0), stop=(kc == 7)),
                              reads=[W, xnT[b]], writes=[pz])
                    kb.op("act", lambda e: e.activation(out=tma[j][:, :], in_=pa[:, 0:256], func=AF.Copy), reads=[pa], writes=[tma[j]])
                    kb.dma("sp", D[f"VS{s}"][r0:r0 + 128, :], tma[j][:, 0:128], reads=[tma[j]], writes=[D[f"VS{s}"]])
                    kb.dma("sp", D[f"VW{s}"][r0:r0 + 128, :], tma[j][:, 128:256], reads=[tma[j]], writes=[D[f"VW{s}"]])
                    tg = tmg[j]
                    kb.op("act", lambda e: e.activation(out=tg[:, 0:24], in_=pa[:, 256:280], func=AF.Sigmoid), reads=[pa], writes=[tg])
                    kb.op("act", lambda e: e.activation(out=tg[:, 24:32], in_=pa[:, 288:296], func=AF.Sigmoid), reads=[pa], writes=[tg])
                    kb.op("dve", lambda e: e.tensor_tensor(out=tg[:, 32:40], in0=pa[:, 280:288], in1=dtb[:, :], op=ALU.add), reads=[pa, dtb], writes=[tg])
                    kb.op("act", lambda e: e.activation(out=tg[:, 32:40], in_=tg[:, 32:40], func=AF.Exp), reads=[tg], writes=[tg])
                    kb.op("dve", lambda e: e.tensor_scalar(out=tg[:, 32:40], in0=tg[:, 32:40], scalar1=1.0, scalar2=None, op0=ALU.add), reads=[tg], writes=[tg])
                    kb.op("act", lambda e: e.activation(out=tg[:, 32:40], in_=tg[:, 32:40], func=AF.Ln), reads=[tg], writes=[tg])
                    kb.op("dve", lambda e: e.tensor_tensor(out=tg[:, 32:40], in0=tg[:, 32:40], in1=nea[:, :], op=ALU.mult), reads=[tg, nea], writes=[tg])
                    kb.dma("sp", D[f"GATE{s}"][r0:r0 + 128, :], tg[:, 0:24], reads=[tg], writes=[D[f"GATE{s}"]])
                    kb.dma("sp", D[f"GB{s}"][r0:r0 + 128, :], tg[:, 24:32], reads=[tg], writes=[D[f"GB{s}"]])
                    kb.dma("sp", D[f"GG{s}"][r0:r0 + 128, :], tg[:, 32:40], reads=[tg], writes=[D[f"GG{s}"]])
                    kb.op("act", lambda e: e.activation(out=tmz[j][:, :], in_=pz[:, :], func=AF.Silu), reads=[pz], writes=[tmz[j]])
                    kb.dma("sp", D[f"GZ{s}"][r0:r0 + 128, :], tmz[j][:, :], reads=[tmz[j]], writes=[D[f"GZ{s}"]])
        kb.barrier()


IN_SHAPES = {
    "x": ([NSEQ, SEQ, DM], F32), "mem": ([NSEQ, MEM, DM], F32), "positions": ([NSEQ, SEQ], I32),
    "norm_mix": ([2, DM], F32), "norm_xattn": ([2, DM], F32), "norm_mlp": ([2, DM], F32),
    "hyb_w_in": ([1, DM, IN_COLS], F32), "hyb_cmp_k_pos": ([1, 32, 64], F32), "hyb_cmp_k_w1": ([1, 2048, 128], F32),
    "hyb_cmp_k_w2": ([1, 128, 64], F32), "hyb_cmp_v_pos": ([1, 32, 64], F32), "hyb_cmp_v_w1": ([1, 2048, 128], F32),
    "hyb_cmp_v_w2": ([1, 128, 64], F32), "hyb_gdn_conv": ([1, 4, 1536], F32), "hyb_gdn_a_log": ([1, 8], F32),
    "hyb_gdn_dt_bias": ([1, 8], F32), "hyb_gdn_norm": ([1, 64], F32), "hyb_w_out": ([1, 1024, 1024], F32),
    "sc_w_in": ([1, 1024, 3072], F32), "sc_conv": ([1, 3, 1024], F32), "sc_w_out": ([1, 1024, 1024], F32),
    "mem_norm": ([DM], F32), "xa_wq": ([2, 1024, 512], F32), "xa_wkv": ([2, 1024, 1024], F32), "xa_wo": ([2, 512, 1024], F32),
    "mlp_w1": ([2, 1024, 4096], F32), "mlp_w2": ([2, 4096, 1024], F32), "final_norm": ([DM], F32),
}


def make_consts():
    C = {}
    C["ident"] = np.eye(128, dtype=np.float32).astype(ml_dtypes.bfloat16)
    C["identf"] = np.eye(128, dtype=np.float32)
    inv = (10000.0 ** (-np.arange(0, 64, 2, dtype=np.float32) / 64)).astype(np.float32)
    C["invf"] = np.tile(inv, 4).reshape(128, 1).astype(np.float32)
    C.update(nsa_consts())
    C.update(gdn_consts())
    return C


CONST_DT = {"ident": BF16, "identf": F32, "invf": F32, "winmask": BF16, "cmpmask": BF16, "EE": BF16, "ovaug": BF16, "c01": F32, "addc": F32, "LT": F32, "ones64": F32, "trilI8": F32, "trilS8": F32, "I8": F32, "Bones": BF16}


def declare(kb):
    I = {k: kb.dram(k, shp, dt, "ExternalInput") for k, (shp, dt) in IN_SHAPES.items()}
    C = {k: kb.dram("c_" + k, list(v.shape), CONST_DT[k], "ExternalInput") for k, v in make_consts().items()}
    return I, C


def core_inputs(inputs, core):
    m = {}
    for k in IN_SHAPES:
        v = np.asarray(inputs[k])
        if k in ("x", "mem", "positions"):
            v = v[core * NSEQ:(core + 1) * NSEQ]
        m[k] = np.ascontiguousarray(v)
    for k, v in make_consts().items():
        m["c_" + k] = v
    return m


def phase_memkv(kb, I, C, KT, VV):
    with ExitStack() as st:
        W = kb.sb(st, [128, 8, 1024], BF16, "Wkv")
        gcol = kb.sb(st, [128, 8], F32, "gcol")
        ident = kb.sb(st, [128, 128], BF16, "ident")
        stg = [kb.sb(st, [128, 1024], F32, "stg") for _ in range(2)]
        mt = kb.sb(st, [128, 2, 1024], F32, "mt")
        mn = kb.sb(st, [128, 1024], BF16, "mn")
        mT = kb.sb(st, [128, 8, 256], BF16, "mT")
        junk = kb.sb(st, [128, 1024], F32, "junk")
        ss = kb.sb(st, [128, 2], F32, "ss")
        rs = kb.sb(st, [128, 1], F32, "rs")
        pT = kb.ps(st, [128, 8, 128], BF16, "pT")
        pK = [kb.ps(st, [128, 512], F32, "pK") for _ in range(2)]
        kb.dma("sp", ident[:, :], C["ident"][:, :], reads=[C["ident"]], writes=[ident])
        load_gain_col(kb, gcol, I["mem_norm"], None)
        k = 0
        for l in range(2):
            load_w(kb, W, I["xa_wkv"], I["xa_wkv"][l], 8, 1024, gcol, stg)
            for s in range(NSEQ):
                kb.dma("sp", mt[:, :, :], I["mem"].t[s].rearrange("(t p) d -> p t d", p=128), reads=[I["mem"]], writes=[mt])
                for t in range(2):
                    rms_rstd(kb, mt[:, t, :], mt, junk, ss, rs, 1024)
                    kb.op("dve", lambda e: e.tensor_scalar(out=mn[:, :], in0=mt[:, t, :], scalar1=rs[:, 0:1], scalar2=32.0, op0=ALU.mult, op1=ALU.mult),
                          reads=[mt, rs], writes=[mn])
                    for kc in range(8):
                        kb.op("pe", lambda e: e.transpose(out=pT[:, kc, :], in_=mn[:, kc * 128:(kc + 1) * 128], identity=ident[:, :]), reads=[mn, ident], writes=[pT])
                    kb.op("act", lambda e: e.activation(out=mT[:, :, t * 128:(t + 1) * 128], in_=pT[:, :, :], func=AF.Copy), reads=[pT], writes=[mT])
                for h in range(4):
                    p = pK[k % 2]
                    k += 1
                    for kc in range(8):
                        kb.op("pe", lambda e: e.matmul(p[:, :256], lhsT=W[:, kc, h * 128:(h + 1) * 128], rhs=mT[:, kc, :], start=(kc == 0), stop=(kc == 7)),
                              reads=[W, mT], writes=[p])
                    kb.op("act", lambda e: e.activation(out=KT[l][s][:, h, :], in_=p[:, :256], func=AF.Copy), reads=[p], writes=[KT[l][s]])
                for t in range(2):
                    p = pK[k % 2]
                    k += 1
                    for kc in range(8):
                        kb.op("pe", lambda e: e.matmul(p[:, :], lhsT=mT[:, kc, t * 128:(t + 1) * 128], rhs=W[:, kc, 512:1024], start=(kc == 0), stop=(kc == 7)),
                              reads=[W, mT], writes=[p])
                    kb.op("act", lambda e: e.activation(out=VV[l][s][:, t, :], in_=p[:, :], func=AF.Copy), reads=[p], writes=[VV[l][s]])
        kb.barrier()


def phase_b(kb, I, C, D, layer, x_in_of, x_out, KT, VV, dbg_groups=99):
    G = 512
    NGS = SEQ // G
    with ExitStack() as st:
        ident = kb.sb(st, [128, 128], BF16, "ident")
        ones = kb.sb(st, [128, 128], BF16, "ones")
        kb.dma("sp", ident[:, :], C["ident"][:, :], reads=[C["ident"]], writes=[ident])
        kb.op("dve", lambda e: e.memset(ones[:, :], 1.0), writes=[ones])
        Wout = kb.sb(st, [128, 8, 1024], BF16, "Wout")
        Wq = kb.sb(st, [128, 8, 512], BF16, "Wq")
        Wo = kb.sb(st, [128, 4, 1024], BF16, "Wo")
        gx = kb.sb(st, [128, 8], F32, "gx")
        load_gain_col(kb, gx, I["norm_xattn"], layer)
        if layer == 1:
            Wsc = kb.sb(st, [128, 8, 3072], BF16, "Wsc")
            gm = kb.sb(st, [128, 8], F32, "gm")
            cw = kb.sb(st, [128, 8, 3], F32, "cw")
            load_gain_col(kb, gm, I["norm_mix"], 1)
            for j in range(3):
                kb.dma("sp", cw[:, :, j], I["sc_conv"][0, j].rearrange("(k p) -> p k", p=128), reads=[I["sc_conv"]], writes=[cw], allow_slow_non_contiguous=True)
        with ExitStack() as st2:
            stg = [kb.sb(st2, [128, 1024], F32, "stg") for _ in range(2)]
            if layer == 0:
                load_w(kb, Wout, I["hyb_w_out"], I["hyb_w_out"][0], 8, 1024, None, stg)
            else:
                load_w(kb, Wout, I["sc_w_out"], I["sc_w_out"][0], 8, 1024, None, stg)
                load_w(kb, Wsc, I["sc_w_in"], I["sc_w_in"][0], 8, 3072, gm, stg)
            load_w(kb, Wq, I["xa_wq"], I["xa_wq"][layer], 8, 512, gx, stg, scale=float(128 ** -0.5))
            load_w(kb, Wo, I["xa_wo"], I["xa_wo"][layer], 4, 1024, None, stg)
            kb.barrier()
        xt = [kb.sb(st, [128, 4, 1024], F32, "xt") for _ in range(2)]
        xn = kb.sb(st, [128, 1024], BF16, "xn")
        xnT = kb.sb(st, [128, 8, G], BF16, "xnT")
        junk = kb.sb(st, [128, 1024], F32, "junk")
        ss = [kb.sb(st, [128, 2], F32, "ss") for _ in range(2)]
        rs = [kb.sb(st, [128, 1], F32, "rs") for _ in range(2)]
        qT = kb.sb(st, [128, 4, G], BF16, "qT")
        eT = [kb.sb(st, [128, 2, G], BF16, "eT") for _ in range(2)]
        rden = kb.sb(st, [128, G], F32, "rden")
        oT = kb.sb(st, [128, 4, G], BF16, "oT")
        pT = [kb.ps(st, [128, 8, 128], BF16, "pT") for _ in range(2)]
        pA = [kb.ps(st, [128, 512], F32, "pA") for _ in range(4)]
        pO = [kb.ps(st, [128, 512], F32, "pO") for _ in range(2)]
        if layer == 0:
            mx = [kb.sb(st, [128, 4, 1024], BF16, "mx") for _ in range(2)]
            mT = kb.sb(st, [128, 8, G], BF16, "mT")
        else:
            bT = [kb.sb(st, [128, G], F32, "bT") for _ in range(2)]
            halo = kb.sb(st, [128, 8, 2], F32, "halo")
            cu = [kb.sb(st, [128, G + 2], F32, "cu") for _ in range(2)]
            cT = kb.sb(st, [128, G], F32, "cT")
            acc = [kb.sb(st, [128, G], F32, "acc") for _ in range(2)]
            mT = kb.sb(st, [128, 8, G], BF16, "mT")
        cnt = [0]
        ca = [0]
        co = [0]

        def load_b(g_):
            s_, gi_ = divmod(g_, NGS)
            xin, xtile = x_in_of(s_)
            kb.dma("sp", xt[g_ % 2][:, :, :], xin[gi_ * G:(gi_ + 1) * G, :].rearrange("(t p) d -> p t d", p=128), reads=[xtile], writes=[xt[g_ % 2]])
            if layer == 0:
                kb.dma("sp", mx[g_ % 2][:, :, :], D[f"MIX{s_}"][gi_ * G:(gi_ + 1) * G, :].rearrange("(t p) d -> p t d", p=128), reads=[D[f"MIX{s_}"]], writes=[mx[g_ % 2]])

        def norm_T(b, dstT):
            for t in range(4):
                i = cnt[0] % 2
                cnt[0] += 1
                rms_rstd(kb, xt[b][:, t, :], xt[b], junk, ss[i], rs[i], 1024)
                kb.op("dve", lambda e: e.tensor_scalar(out=xn[:, :], in0=xt[b][:, t, :], scalar1=rs[i][:, 0:1], scalar2=32.0, op0=ALU.mult, op1=ALU.mult),
                      reads=[xt[b], rs[i]], writes=[xn])
                p = pT[i]
                for kc in range(8):
                    kb.op("pe", lambda e: e.transpose(out=p[:, kc, :], in_=xn[:, kc * 128:(kc + 1) * 128], identity=ident[:, :]), reads=[xn, ident], writes=[p])
                kb.op("act", lambda e: e.activation(out=dstT[:, :, t * 128:(t + 1) * 128], in_=p[:, :, :], func=AF.Copy), reads=[p], writes=[dstT])

        def proj_add(b, lT, nk, Wt):
            for t in range(4):
                for half in range(2):
                    po = pO[co[0] % 2]
                    co[0] += 1
                    for kc in range(nk):
                        kb.op("pe", lambda e: e.matmul(po[:, :], lhsT=lT[:, kc, t * 128:(t + 1) * 128], rhs=Wt[:, kc, half * 512:(half + 1) * 512], start=(kc == 0), stop=(kc == nk - 1)),
                              reads=[lT, Wt], writes=[po])
                    xs = xt[b][:, t, half * 512:(half + 1) * 512]
                    kb.op("dve", lambda e: e.tensor_tensor(out=xs, in0=xs, in1=po[:, :], op=ALU.add), reads=[po, xt[b]], writes=[xt[b]])

        ng = min(dbg_groups, NSEQ * NGS)
        for g in range(ng):
            s, gi = divmod(g, NGS)
            b = g % 2
            if g == 0:
                load_b(0)
            if g + 1 < ng:
                load_b(g + 1)
            if layer == 0:
                for t in range(4):
                    p = pT[t % 2]
                    for kc in range(8):
                        kb.op("pe", lambda e: e.transpose(out=p[:, kc, :], in_=mx[b][:, t, kc * 128:(kc + 1) * 128], identity=ident[:, :]), reads=[mx[b], ident], writes=[p])
                    kb.op("act", lambda e: e.activation(out=mT[:, :, t * 128:(t + 1) * 128], in_=p[:, :, :], func=AF.Copy), reads=[p], writes=[mT])
            else:
                norm_T(b, xnT)
                if gi == 0:
                    kb.op("dve", lambda e: e.memset(halo[:, :, :], 0.0), writes=[halo])
                for c in range(8):
                    pb = pA[ca[0] % 4]
                    pc = pA[(ca[0] + 1) % 4]
                    pu = pA[(ca[0] + 2) % 4]
                    ca[0] += 3
                    cc = cu[c % 2]
                    ac = acc[c % 2]
                    bb = bT[c % 2]
                    for (pp, off) in [(pb, 0), (pc, 1024), (pu, 2048)]:
                        for kc in range(8):
                            kb.op("pe", lambda e: e.matmul(pp[:, :], lhsT=Wsc[:, kc, off + c * 128:off + (c + 1) * 128], rhs=xnT[:, kc, :], start=(kc == 0), stop=(kc == 7)),
                                  reads=[Wsc, xnT], writes=[pp])
                    kb.op("act", lambda e: e.activation(out=bb[:, :], in_=pb[:, :], func=AF.Copy), reads=[pb], writes=[bb])
                    kb.op("act", lambda e: e.activation(out=cT[:, :], in_=pc[:, :], func=AF.Copy), reads=[pc], writes=[cT])
                    kb.op("dve", lambda e: e.tensor_copy(out=cc[:, 0:2], in_=halo[:, c, :]), reads=[halo], writes=[cc])
                    kb.op("dve", lambda e: e.tensor_tensor(out=cc[:, 2:G + 2], in0=cT[:, :], in1=pu[:, :], op=ALU.mult), reads=[cT, pu], writes=[cc])
                    kb.op("dve", lambda e: e.tensor_copy(out=halo[:, c, :], in_=cc[:, G:G + 2]), reads=[cc], writes=[halo])
                    kb.op("dve", lambda e: e.tensor_scalar(out=ac[:, :], in0=cc[:, 0:G], scalar1=cw[:, c, 0:1], scalar2=None, op0=ALU.mult), reads=[cc, cw], writes=[ac])
                    kb.op("dve", lambda e: e.scalar_tensor_tensor(out=ac[:, :], in0=cc[:, 1:G + 1], scalar=cw[:, c, 1:2], in1=ac[:, :], op0=ALU.mult, op1=ALU.add),
                          reads=[cc, cw, ac], writes=[ac])
                    kb.op("dve", lambda e: e.scalar_tensor_tensor(out=ac[:, :], in0=cc[:, 2:G + 2], scalar=cw[:, c, 2:3], in1=ac[:, :], op0=ALU.mult, op1=ALU.add),
                          reads=[cc, cw, ac], writes=[ac])
                    kb.op("pool", lambda e: e.tensor_tensor(out=mT[:, c, :], in0=ac[:, :], in1=bb[:, :], op=ALU.mult), reads=[ac, bb], writes=[mT])
            proj_add(b, mT, 8, Wout)
            norm_T(b, xnT)
            for h in range(4):
                p = pA[ca[0] % 4]
                ca[0] += 1
                for kc in range(8):
                    kb.op("pe", lambda e: e.matmul(p[:, :], lhsT=Wq[:, kc, h * 128:(h + 1) * 128], rhs=xnT[:, kc, :], start=(kc == 0), stop=(kc == 7)), reads=[Wq, xnT], writes=[p])
                kb.op("act", lambda e: e.activation(out=qT[:, h, :], in_=p[:, :], func=AF.Copy), reads=[p], writes=[qT])
            for h in range(4):
                e_ = eT[h % 2]
                for mc in range(2):
                    p = pA[ca[0] % 4]
                    ca[0] += 1
                    kb.op("pe", lambda e: e.matmul(p[:, :], lhsT=KT[layer][s][:, h, mc * 128:(mc + 1) * 128], rhs=qT[:, h, :], start=True, stop=True), reads=[KT[layer][s], qT], writes=[p])
                    kb.op("act", lambda e: e.activation(out=e_[:, mc, :], in_=p[:, :], func=AF.Exp), reads=[p], writes=[e_])
                pd = pA[ca[0] % 4]
                po = pA[(ca[0] + 1) % 4]
                ca[0] += 2
                for mc in range(2):
                    kb.op("pe", lambda e: e.matmul(pd[:, :], lhsT=ones[:, :], rhs=e_[:, mc, :], start=(mc == 0), stop=(mc == 1)), reads=[ones, e_], writes=[pd])
                for mc in range(2):
                    kb.op("pe", lambda e: e.matmul(po[:, :], lhsT=VV[layer][s][:, mc, h * 128:(h + 1) * 128], rhs=e_[:, mc, :], start=(mc == 0), stop=(mc == 1)), reads=[VV[layer][s], e_], writes=[po])
                kb.op("dve", lambda e: e.reciprocal(out=rden[:, :], in_=pd[:, :]), reads=[pd], writes=[rden])
                kb.op("dve", lambda e: e.tensor_tensor(out=oT[:, h, :], in0=po[:, :], in1=rden[:, :], op=ALU.mult), reads=[po, rden], writes=[oT])
            proj_add(b, oT, 4, Wo)
            kb.dma("sp", x_out.t[s * SEQ + gi * G:s * SEQ + (gi + 1) * G, :].rearrange("(t p) d -> p t d", p=128), xt[b][:, :, :], reads=[xt[b]], writes=[x_out])
        kb.barrier()


NEGB = -30000.0


def nsa_consts():
    C = {}
    k = np.arange(128)[:, None]
    q = np.arange(512)[None, :]
    wm = np.zeros((8, 128, 512), np.float32)
    for r in range(8):
        d = q - k + 512 - 128 * r
        wm[r] = np.where((d >= 0) & (d <= 511), 0.0, NEGB)
    C["winmask"] = wm.astype(ml_dtypes.bfloat16)
    cm = np.zeros((4, 128, 512), np.float32)
    for g in range(4):
        cm[g] = np.where(q - 16 * k >= 31 - 512 * g, 0.0, NEGB)
    C["cmpmask"] = cm.astype(ml_dtypes.bfloat16)
    C["EE"] = (np.arange(4096)[None, :] // 64 == np.arange(64)[:, None]).astype(np.float32).astype(ml_dtypes.bfloat16)
    cs = np.arange(255)[:, None] * 16
    js = np.arange(64)[None, :] * 64
    ov = np.clip(np.minimum(cs + 32, js + 64) - np.maximum(cs, js), 0, None) / 32.0
    oa = np.zeros((2, 128, 65), np.float32)
    oa[0, :, :64] = ov[:128]
    oa[1, :127, :64] = ov[128:]
    oa[:, :, 64] = 1.0
    C["ovaug"] = oa.astype(ml_dtypes.bfloat16)
    t = (np.arange(32)[:, None, None] * 128 + np.arange(128)[None, :, None])
    j = np.arange(64)[None, None, :]
    cur = t // 64
    forced = (j == 0) | (j == cur) | (j == cur - 1)
    causal = (j * 64 <= t)
    c01 = causal.astype(np.float32)
    addc = np.where(causal, np.where(forced, 1000.0, 0.0), -1.0).astype(np.float32)
    C["c01"] = np.ascontiguousarray(c01.transpose(1, 0, 2))
    C["addc"] = np.ascontiguousarray(addc.transpose(1, 0, 2))
    return C


def phase_nsa(kb, I, C, D, dbg_sg=99, dbg_G=99):
    with ExitStack() as st:
        identb = kb.sb(st, [128, 128], BF16, "identb")
        winm = kb.sb(st, [128, 8, 512], BF16, "winm")
        cmpm = kb.sb(st, [128, 4, 512], BF16, "cmpm")
        EE = kb.sb(st, [64, 4096], BF16, "EE")
        ovaug = kb.sb(st, [128, 2, 65], BF16, "ovaug")
        c01 = kb.sb(st, [128, 32, 64], F32, "c01")
        addc = kb.sb(st, [128, 32, 64], F32, "addc")
        kb.dma("sp", identb[:, :], C["ident"][:, :], reads=[C["ident"]], writes=[identb])
        kb.dma("sp", winm[:, :, :], C["winmask"].t.rearrange("r p q -> p r q"), reads=[C["winmask"]], writes=[winm])
        kb.dma("sp", cmpm[:, :, :], C["cmpmask"].t.rearrange("r p q -> p r q"), reads=[C["cmpmask"]], writes=[cmpm])
        kb.dma("sp", EE[:, :], C["EE"][:, :], reads=[C["EE"]], writes=[EE])
        kb.dma("sp", ovaug[:, :, :], C["ovaug"].t.rearrange("c p j -> p c j"), reads=[C["ovaug"]], writes=[ovaug])
        kb.dma("sp", c01[:, :, :], C["c01"][:, :, :], reads=[C["c01"]], writes=[c01])
        kb.dma("sp", addc[:, :, :], C["addc"][:, :, :], reads=[C["addc"]], writes=[addc])
        W1 = [kb.sb(st, [64, 32, 128], BF16, "cW1") for _ in range(2)]
        W2 = [kb.sb(st, [128, 64], BF16, "cW2") for _ in range(2)]
        posT = [kb.sb(st, [64, 32], BF16, "cpos") for _ in range(2)]
        with ExitStack() as st2:
            sg1 = kb.sb(st2, [64, 32, 128], F32, "sg1")
            sg2 = kb.sb(st2, [128, 64], F32, "sg2")
            sg3 = kb.sb(st2, [64, 32], F32, "sg3")
            for i, nm in enumerate(["k", "v"]):
                w1 = I[f"hyb_cmp_{nm}_w1"]
                kb.dma("sp", sg1[:, :, :], w1[0].rearrange("(l d) h -> d l h", d=64), reads=[w1], writes=[sg1])
                kb.op("dve", lambda e: e.tensor_copy(out=W1[i][:, :, :], in_=sg1[:, :, :]), reads=[sg1], writes=[W1[i]])
                w2 = I[f"hyb_cmp_{nm}_w2"]
                kb.dma("sp", sg2[:, :], w2[0], reads=[w2], writes=[sg2])
                kb.op("dve", lambda e: e.tensor_copy(out=W2[i][:, :], in_=sg2[:, :]), reads=[sg2], writes=[W2[i]])
                pp = I[f"hyb_cmp_{nm}_pos"]
                kb.dma("sp", sg3[:, :], pp[0].rearrange("l d -> d l"), reads=[pp], writes=[sg3], allow_slow_non_contiguous=True)
                kb.op("dve", lambda e: e.tensor_copy(out=posT[i][:, :], in_=sg3[:, :]), reads=[sg3], writes=[posT[i]])
            kb.barrier()
        kct = kb.sb(st, [64, SEQ], BF16, "kct")
        kst = kb.sb(st, [64, SEQ], BF16, "kst")
        kwt = kb.sb(st, [64, SEQ], BF16, "kwt")
        vsa = kb.sb(st, [128, 32, 65], BF16, "vsa")
        vwa = kb.sb(st, [128, 32, 65], BF16, "vwa")
        vca = kb.sb(st, [128, 2, 65], BF16, "vca")
        kcT = kb.sb(st, [64, 256], BF16, "kcT")
        hsil = kb.sb(st, [128, 256], BF16, "hsil")
        posc = kb.sb(st, [128, 1], F32, "posc")
        qTh = [kb.sb(st, [64, 512], BF16, "qTh") for _ in range(4)]
        gat = kb.sb(st, [128, 4, 24], F32, "gat")
        eT = [kb.sb(st, [128, 512], BF16, "eT") for _ in range(3)]
        ocmp = kb.sb(st, [128, 4, 4, 65], F32, "ocmp")
        eall = [kb.sb(st, [128, 8, 512], BF16, "eall") for _ in range(4)]
        ewin = kb.sb(st, [128, 8, 512], BF16, "ewin")
        imp = kb.sb(st, [128, 4, 64], F32, "imp")
        imp2 = kb.sb(st, [128, 64], F32, "imp2")
        m8 = kb.sb(st, [128, 16], F32, "m8")
        rd = kb.sb(st, [128, 8], F32, "rd")
        nsel = kb.sb(st, [128, 64], BF16, "nsel")
        nselT = kb.sb(st, [64, 512], BF16, "nselT")
        oacc = kb.sb(st, [128, 64], F32, "oacc")
        onsa = [kb.sb(st, [128, 4, 256], BF16, "onsa") for _ in range(2)]
        pS = [kb.ps(st, [128, 512], F32, "pS") for _ in range(2)]
        pOs = kb.ps(st, [128, 4, 128], F32, "pOs")
        pOw = kb.ps(st, [128, 4, 128], F32, "pOw")
        pOc = kb.ps(st, [128, 4, 128], F32, "pOc")
        pU = kb.ps(st, [128, 4, 128], F32, "pU")
        pTr = kb.ps(st, [64, 512], BF16, "pTr")
        cs = [0]
        ce = [0]
        cg = 0
        for s in range(NSEQ):
            for g in range(2):
                sg = s * 2 + g
                if sg >= dbg_sg:
                    continue
                r64 = slice(g * 64, (g + 1) * 64)
                kb.dma("sp", kst[:, :], D[f"KST{s}"][r64, :], reads=[D[f"KST{s}"]], writes=[kst])
                kb.dma("sp", kwt[:, :], D[f"KWT{s}"][r64, :], reads=[D[f"KWT{s}"]], writes=[kwt])
                kb.op("dve", lambda e: e.memset(vsa[:, :, 64:65], 1.0), writes=[vsa])
                kb.op("dve", lambda e: e.memset(vwa[:, :, 64:65], 1.0), writes=[vwa])
                kb.op("dve", lambda e: e.memset(vca[:, :, :], 0.0), writes=[vca])
                kb.op("dve", lambda e: e.memset(vca[:, :, 64:65], 1.0), writes=[vca])
                kb.dma("sp", vsa[:, :, 0:64], D[f"VS{s}"][:, r64].rearrange("(t p) d -> p t d", p=128), reads=[D[f"VS{s}"]], writes=[vsa])
                kb.dma("sp", vwa[:, :, 0:64], D[f"VW{s}"][:, r64].rearrange("(t p) d -> p t d", p=128), reads=[D[f"VW{s}"]], writes=[vwa])
                for i, nm in enumerate(["KCT", "VCT"]):
                    kb.dma("sp", kct[:, :], D[f"{nm}{s}"][r64, :], reads=[D[f"{nm}{s}"]], writes=[kct])
                    ph = pS[cs[0] % 2]
                    cs[0] += 1
                    pp_ = pS[cs[0] % 2]
                    cs[0] += 1
                    for l in range(32):
                        kb.op("pe", lambda e: e.matmul(ph[:, 0:255], lhsT=W1[i][:, l, :], rhs=kct[:, l:l + 16 * 254 + 1:16], start=(l == 0), stop=(l == 31)),
                              reads=[W1[i], kct], writes=[ph])
                    for l in range(32):
                        kb.op("pe", lambda e: e.matmul(pp_[:, 0:1], lhsT=W1[i][:, l, :], rhs=posT[i][:, l:l + 1], start=(l == 0), stop=(l == 31)),
                              reads=[W1[i], posT[i]], writes=[pp_])
                    kb.op("dve", lambda e: e.tensor_copy(out=posc[:, :], in_=pp_[:, 0:1]), reads=[pp_], writes=[posc])
                    kb.op("act", lambda e: e.activation(out=hsil[:, 0:255], in_=ph[:, 0:255], func=AF.Silu, bias=posc[:, 0:1]), reads=[ph, posc], writes=[hsil])
                    if i == 0:
                        p2 = pS[cs[0] % 2]
                        cs[0] += 1
                        kb.op("pe", lambda e: e.matmul(p2[0:64, 0:255], lhsT=W2[0][:, :], rhs=hsil[:, 0:255], start=True, stop=True), reads=[W2[0], hsil], writes=[p2])
                        kb.op("act", lambda e: e.activation(out=kcT[:, 0:255], in_=p2[0:64, 0:255], func=AF.Copy), reads=[p2], writes=[kcT])
                    else:
                        for c, nn in [(0, 128), (1, 127)]:
                            p2 = pS[cs[0] % 2]
                            cs[0] += 1
                            kb.op("pe", lambda e: e.matmul(p2[0:nn, 0:64], lhsT=hsil[:, c * 128:c * 128 + nn], rhs=W2[1][:, :], start=True, stop=True), reads=[W2[1], hsil], writes=[p2])
                            kb.op("act", lambda e: e.activation(out=vca[0:nn, c, 0:64], in_=p2[0:nn, 0:64], func=AF.Copy), reads=[p2], writes=[vca])
                for G in range(8):
                    if G >= dbg_G:
                        continue
                    q0 = G * 512
                    on = onsa[cg % 2]
                    cg += 1
                    for h in range(4):
                        hh = g * 4 + h
                        kb.dma("sp", qTh[h][:, :], D[f"QT{s}"][hh // 2, (hh % 2) * 64:(hh % 2) * 64 + 64, q0:q0 + 512], reads=[D[f"QT{s}"]], writes=[qTh[h]])
                    kb.dma("sp", gat[:, :, :], D[f"GATE{s}"][q0:q0 + 512, :].rearrange("(t p) c -> p t c", p=128), reads=[D[f"GATE{s}"]], writes=[gat])
                    chunks = [(0, 128, G)] if G < 4 else [(0, 128, None), (1, 127, G - 4)]
                    for h in range(4):
                        ets = []
                        for (c, nn, mi) in chunks:
                            p = pS[cs[0] % 2]
                            cs[0] += 1
                            kb.op("pe", lambda e: e.matmul(p[0:nn, :], lhsT=kcT[:, c * 128:c * 128 + nn], rhs=qTh[h][:, :], start=True, stop=(mi is None)), reads=[kcT, qTh[h]], writes=[p])
                            if mi is not None:
                                kb.op("pe", lambda e: e.matmul(p[0:nn, :], lhsT=identb[0:nn, 0:nn], rhs=cmpm[0:nn, mi, :], start=False, stop=True), reads=[identb, cmpm], writes=[p])
                            e_ = eT[ce[0] % 3]
                            ce[0] += 1
                            kb.op("act", lambda e: e.activation(out=e_[0:nn, :], in_=p[0:nn, :], func=AF.Exp), reads=[p], writes=[e_])
                            ets.append((e_, c, nn))
                        for qt in range(4):
                            for k_, (e_, c, nn) in enumerate(ets):
                                kb.op("pe", lambda e: e.matmul(pOc[:, qt, 0:65], lhsT=e_[0:nn, qt * 128:(qt + 1) * 128], rhs=vca[0:nn, c, :], start=(k_ == 0), stop=(k_ == len(ets) - 1)),
                                      reads=[e_, vca], writes=[pOc])
                            for k_, (e_, c, nn) in enumerate(ets):
                                kb.op("pe", lambda e: e.matmul(pU[:, qt, 0:65], lhsT=e_[0:nn, qt * 128:(qt + 1) * 128], rhs=ovaug[0:nn, c, :], start=(k_ == 0), stop=(k_ == len(ets) - 1)),
                                      reads=[e_, ovaug], writes=[pU])
                        kb.op("act", lambda e: e.activation(out=ocmp[:, h, :, :], in_=pOc[:, :, 0:65], func=AF.Copy), reads=[pOc], writes=[ocmp])
                        kb.op("dve", lambda e: e.tensor_scalar(out=rd[:, 0:4], in0=pU[:, :, 64], scalar1=1e-30, scalar2=None, op0=ALU.max), reads=[pU], writes=[rd])
                        kb.op("dve", lambda e: e.reciprocal(out=rd[:, 4:8], in_=rd[:, 0:4]), reads=[rd], writes=[rd])
                        for qt in range(4):
                            if h == 0:
                                kb.op("dve", lambda e: e.tensor_scalar(out=imp[:, qt, :], in0=pU[:, qt, 0:64], scalar1=rd[:, 4 + qt:5 + qt], scalar2=None, op0=ALU.mult),
                                      reads=[pU, rd], writes=[imp])
                            else:
                                kb.op("dve", lambda e: e.scalar_tensor_tensor(out=imp[:, qt, :], in0=pU[:, qt, 0:64], scalar=rd[:, 4 + qt:5 + qt], in1=imp[:, qt, :], op0=ALU.mult, op1=ALU.add),
                                      reads=[pU, rd, imp], writes=[imp])
                    for qt in range(4):
                        qg = G * 4 + qt
                        kb.op("dve", lambda e: e.tensor_tensor(out=imp[:, qt, :], in0=imp[:, qt, :], in1=c01[:, qg, :], op=ALU.mult), reads=[imp, c01], writes=[imp])
                        kb.op("dve", lambda e: e.tensor_tensor(out=imp[:, qt, :], in0=imp[:, qt, :], in1=addc[:, qg, :], op=ALU.add), reads=[imp, addc], writes=[imp])
                        kb.op("dve", lambda e: e.max(out=m8[:, 0:8], in_=imp[:, qt, :]), reads=[imp], writes=[m8])
                        kb.op("dve", lambda e: e.match_replace(out=imp2[:, :], in_to_replace=m8[:, 0:8], in_values=imp[:, qt, :], imm_value=-1e9), reads=[imp, m8], writes=[imp2])
                        kb.op("dve", lambda e: e.max(out=m8[:, 8:16], in_=imp2[:, :]), reads=[imp2], writes=[m8])
                        kb.op("dve", lambda e: e.tensor_scalar(out=imp2[:, :], in0=imp[:, qt, :], scalar1=m8[:, 15:16], scalar2=-NEGB, op0=ALU.is_ge, op1=ALU.mult), reads=[imp, m8], writes=[imp2])
                        kb.op("dve", lambda e: e.tensor_scalar(out=nsel[:, :], in0=imp2[:, :], scalar1=NEGB, scalar2=None, op0=ALU.add), reads=[imp2], writes=[nsel])
                        kb.op("pe", lambda e: e.transpose(out=pTr[:, qt * 128:(qt + 1) * 128], in_=nsel[:, :], identity=identb[:, :]), reads=[nsel, identb], writes=[pTr])
                    kb.op("act", lambda e: e.activation(out=nselT[:, :], in_=pTr[:, :], func=AF.Copy), reads=[pTr], writes=[nselT])
                    for h in range(4):
                        nk = 4 * G + 4
                        for kt in range(nk):
                            r = kt - 4 * G
                            qlo = max(r, 0)
                            qs = slice(qlo * 128, 512)
                            p = pS[cs[0] % 2]
                            cs[0] += 1
                            kb.op("pe", lambda e: e.matmul(p[:, qs], lhsT=kst[:, kt * 128:(kt + 1) * 128], rhs=qTh[h][:, qs], start=True, stop=False), reads=[kst, qTh[h]], writes=[p])
                            kb.op("pe", lambda e: e.matmul(p[:, qs], lhsT=EE[:, kt * 128:(kt + 1) * 128], rhs=nselT[:, qs], start=False, stop=(r < 0)), reads=[EE, nselT], writes=[p])
                            if r >= 0:
                                kb.op("pe", lambda e: e.matmul(p[:, qs], lhsT=identb[:, :], rhs=winm[:, r + 4, qs], start=False, stop=True), reads=[identb, winm], writes=[p])
                            kb.op("act", lambda e: e.activation(out=eall[kt // 8][:, kt % 8, qs], in_=p[:, qs], func=AF.Exp), reads=[p], writes=[eall[kt // 8]])
                        for qt in range(4):
                            last = 4 * G + qt
                            for kt in range(last + 1):
                                kb.op("pe", lambda e: e.matmul(pOs[:, qt, 0:65], lhsT=eall[kt // 8][:, kt % 8, qt * 128:(qt + 1) * 128], rhs=vsa[:, kt, :], start=(kt == 0), stop=(kt == last)),
                                      reads=[eall[kt // 8], vsa], writes=[pOs])
                        for r in range(8):
                            kt = 4 * G - 4 + r
                            if kt < 0:
                                continue
                            lo, hi = max(0, r - 4), min(3, r)
                            qs = slice(lo * 128, (hi + 1) * 128)
                            p = pS[cs[0] % 2]
                            cs[0] += 1
                            kb.op("pe", lambda e: e.matmul(p[:, qs], lhsT=kwt[:, kt * 128:(kt + 1) * 128], rhs=qTh[h][:, qs], start=True, stop=False), reads=[kwt, qTh[h]], writes=[p])
                            kb.op("pe", lambda e: e.matmul(p[:, qs], lhsT=identb[:, :], rhs=winm[:, r, qs], start=False, stop=True), reads=[identb, winm], writes=[p])
                            kb.op("act", lambda e: e.activation(out=ewin[:, r, qs], in_=p[:, qs], func=AF.Exp), reads=[p], writes=[ewin])
                        for qt in range(4):
                            rr_ = [r for r in range(qt, qt + 5) if 4 * G - 4 + r >= 0]
                            for r in rr_:
                                kt = 4 * G - 4 + r
                                kb.op("pe", lambda e: e.matmul(pOw[:, qt, 0:65], lhsT=ewin[:, r, qt * 128:(qt + 1) * 128], rhs=vwa[:, kt, :], start=(r == rr_[0]), stop=(r == rr_[-1])),
                                      reads=[ewin, vwa], writes=[pOw])
                        for qt in range(4):
                            kb.op("dve", lambda e: e.tensor_copy(out=rd[:, 0:1], in_=ocmp[:, h, qt, 64:65]), reads=[ocmp], writes=[rd])
                            kb.op("dve", lambda e: e.tensor_copy(out=rd[:, 1:2], in_=pOs[:, qt, 64:65]), reads=[pOs], writes=[rd])
                            kb.op("dve", lambda e: e.tensor_copy(out=rd[:, 2:3], in_=pOw[:, qt, 64:65]), reads=[pOw], writes=[rd])
                            kb.op("dve", lambda e: e.tensor_scalar(out=rd[:, 0:3], in0=rd[:, 0:3], scalar1=1e-30, scalar2=None, op0=ALU.max), reads=[rd], writes=[rd])
                            kb.op("dve", lambda e: e.reciprocal(out=rd[:, 4:7], in_=rd[:, 0:3]), reads=[rd], writes=[rd])
                            kb.op("dve", lambda e: e.tensor_tensor(out=rd[:, 4:7], in0=rd[:, 4:7], in1=gat[:, qt, h * 3 + g * 12:h * 3 + g * 12 + 3], op=ALU.mult), reads=[rd, gat], writes=[rd])
                            kb.op("dve", lambda e: e.tensor_scalar(out=oacc[:, :], in0=ocmp[:, h, qt, 0:64], scalar1=rd[:, 4:5], scalar2=None, op0=ALU.mult), reads=[ocmp, rd], writes=[oacc])
                            kb.op("dve", lambda e: e.scalar_tensor_tensor(out=oacc[:, :], in0=pOs[:, qt, 0:64], scalar=rd[:, 5:6], in1=oacc[:, :], op0=ALU.mult, op1=ALU.add), reads=[pOs, rd, oacc], writes=[oacc])
                            kb.op("dve", lambda e: e.scalar_tensor_tensor(out=on[:, qt, h * 64:(h + 1) * 64], in0=pOw[:, qt, 0:64], scalar=rd[:, 6:7], in1=oacc[:, :], op0=ALU.mult, op1=ALU.add), reads=[pOw, rd, oacc], writes=[on])
                    kb.dma("sp", D[f"MIX{s}"][q0:q0 + 512, g * 256:(g + 1) * 256].rearrange("(t p) c -> p t c", p=128), on[:, :, :], reads=[on], writes=[D[f"MIX{s}"]])
        kb.barrier()


def phase_zero_gdn(kb, D):
    with ExitStack() as st:
        z = kb.sb(st, [128, 512], BF16, "z")
        kb.op("dve", lambda e: e.memset(z[:, :], 0.0), writes=[z])
        for s in range(NSEQ):
            for t in range(SEQ // 128):
                kb.dma("sp", D[f"MIX{s}"][t * 128:(t + 1) * 128, 512:1024], z[:, :], reads=[z], writes=[D[f"MIX{s}"]])
        kb.barrier()


def build_all(dbg=False):
    kb = KB()
    I, C = declare(kb)
    D = make_scratch(kb, dbg=dbg)
    kind = "ExternalOutput" if dbg else "Internal"
    xa = kb.dram("xa", [NTOK, DM], F32, kind)
    xb = kb.dram("xb", [NTOK, DM], F32, kind)
    xc = kb.dram("xc", [NTOK, DM], F32, kind)
    out = kb.dram("out", [NTOK, DM], F32, "ExternalOutput")
    with ExitStack() as top:
        KT = [[kb.sb(top, [128, 4, 256], BF16, "KT") for s in range(NSEQ)] for l in range(2)]
        VV = [[kb.sb(top, [128, 2, 512], BF16, "VV") for s in range(NSEQ)] for l in range(2)]
        phase_memkv(kb, I, C, KT, VV)
        phase_a(kb, I, D, C)
        phase_nsa(kb, I, C, D)
        phase_gdn(kb, I, C, D)
        phase_b(kb, I, C, D, 0, lambda s: (I["x"].t[s], I["x"]), xa, KT, VV)
        phase_mlp(kb, xa, xb, I["mlp_w1"], I["mlp_w2"], I["norm_mlp"], 0, C["ident"])
        phase_b(kb, I, C, D, 1, lambda s: (xb.t[s * SEQ:(s + 1) * SEQ], xb), xc, KT, VV)
        phase_mlp(kb, xc, out, I["mlp_w1"], I["mlp_w2"], I["norm_mlp"], 1, C["ident"], final_gain=I["final_norm"])
        kb.finish()
    return kb


def kernel(**inputs):
    kb = build_all()
    in_maps = [core_inputs(inputs, c) for c in range(NCORES)]
    res = run_bass_kernel_spmd(kb.nc, in_maps, core_ids=list(range(NCORES)))
    outs = [np.asarray(r["out"], dtype=np.float32).reshape(NSEQ, SEQ, DM) for r in res.results]
    return np.concatenate(outs, axis=0)


def gdn_consts():
    C = {}
    i = np.arange(64)
    lt = (i[:, None] <= i[None, :]).astype(np.float32)
    C["LT"] = lt
    C["ones64"] = np.ones((64, 128), np.float32)
    tri = (i[None, :] <= i[:, None]).astype(np.float32)
    tris = (i[None, :] < i[:, None]).astype(np.float32)
    C["trilI8"] = np.tile(tri[:, None, :], (1, 8, 1)).reshape(64, 512)
    C["trilS8"] = np.tile(tris[:, None, :], (1, 8, 1)).reshape(64, 512)
    C["I8"] = np.tile(np.eye(64, dtype=np.float32)[:, None, :], (1, 8, 1)).reshape(64, 512)
    bo = np.zeros((128, 128), np.float32)
    bo[:64, :64] = 1
    bo[64:, 64:] = 1
    C["Bones"] = bo.astype(ml_dtypes.bfloat16)
    return C


def phase_gdn(kb, I, C, D, dbg_seq=99, dbg_groups=99, dbg_stop=99):
    G = 512
    with ExitStack() as st:
        identb = kb.sb(st, [128, 128], BF16, "identb")
        LT = kb.sb(st, [64, 64], F32, "LT")
        ones64 = kb.sb(st, [64, 128], F32, "ones64")
        trilI = kb.sb(st, [64, 512], F32, "trilI")
        trilS = kb.sb(st, [64, 512], F32, "trilS")
        I8 = kb.sb(st, [64, 512], F32, "I8")
        Bones = kb.sb(st, [128, 128], BF16, "Bones")
        nw8 = kb.sb(st, [64, 8, 64], F32, "nw8")
        cwg = kb.sb(st, [128, 12, 4], F32, "cwg")
        for (t_, nm) in [(identb, "ident"), (LT, "LT"), (ones64, "ones64"), (trilI, "trilI8"), (trilS, "trilS8"), (I8, "I8"), (Bones, "Bones")]:
            kb.dma("sp", t_[:, :], C[nm][:, :], reads=[C[nm]], writes=[t_])
        for h in range(8):
            kb.dma("sp", nw8[:, h, :], I["hyb_gdn_norm"][0].partition_broadcast(64), reads=[I["hyb_gdn_norm"]], writes=[nw8])
        for j in range(4):
            kb.dma("sp", cwg[:, :, j], I["hyb_gdn_conv"][0, j].rearrange("(k p) -> p k", p=128), reads=[I["hyb_gdn_conv"]], writes=[cwg], allow_slow_non_contiguous=True)
        pre = kb.sb(st, [128, 12, G + 4], F32, "pre")
        acc = [kb.sb(st, [128, G], F32, "acc") for _ in range(2)]
        sl = [kb.sb(st, [128, G], F32, "sl") for _ in range(2)]
        sqb = [kb.sb(st, [128, G], BF16, "sqb") for _ in range(2)]
        rn = [kb.sb(st, [128, G], F32, "rn") for _ in range(2)]
        qkn = kb.sb(st, [128, 8, G], BF16, "qkn")
        vn = kb.sb(st, [128, 4, G], BF16, "vn")
        qodd = kb.sb(st, [64, 8, G], BF16, "qodd")
        gtl = kb.sb(st, [64, 8, 8], F32, "gtl")
        btl = kb.sb(st, [64, 8, 8], F32, "btl")
        gzl = kb.sb(st, [64, 8, 512], BF16, "gzl")
        kvtm = kb.sb(st, [64, 2, 512], BF16, "kvtm")
        gcs = kb.sb(st, [64, 8], F32, "gcs")
        eg = kb.sb(st, [64, 8], F32, "eg")
        ngcs = kb.sb(st, [64, 8], F32, "ngcs")
        egl = kb.sb(st, [64, 8], F32, "egl")
        eglast = kb.sb(st, [64, 8], F32, "eglast")
        gU = kb.sb(st, [64, 512], F32, "gU")
        arg = kb.sb(st, [64, 512], F32, "arg")
        Dm = kb.sb(st, [64, 512], F32, "Dm")
        Ds = kb.sb(st, [64, 512], F32, "Ds")
        Dib = kb.sb(st, [64, 512], BF16, "Dib")
        DiT = kb.sb(st, [64, 512], BF16, "DiT")
        A32 = kb.sb(st, [64, 512], F32, "A32")
        Y = [kb.sb(st, [64, 512], BF16, "Y") for _ in range(2)]
        YT = [kb.sb(st, [64, 512], BF16, "YT") for _ in range(2)]
        PT32 = kb.sb(st, [64, 512], F32, "PT32")
        PTb = kb.sb(st, [64, 512], BF16, "PTb")
        Vb = kb.sb(st, [64, 512], BF16, "Vb")
        Kbg = kb.sb(st, [64, 512], BF16, "Kbg")
        ktil = kb.sb(st, [64, 512], BF16, "ktil")
        u32 = kb.sb(st, [64, 512], F32, "u32")
        wTb = kb.sb(st, [64, 512], BF16, "wTb")
        vnew = kb.sb(st, [64, 512], BF16, "vnew")
        qkT = kb.sb(st, [64, 512], BF16, "qkT")
        S32 = kb.sb(st, [64, 512], F32, "S32")
        Sb = kb.sb(st, [64, 512], BF16, "Sb")
        o2 = kb.sb(st, [64, 512], F32, "o2")
        o32 = kb.sb(st, [64, 512], F32, "o32")
        osq = kb.sb(st, [64, 512], F32, "osq")
        oss = kb.sb(st, [64, 8], F32, "oss")
        ors = kb.sb(st, [64, 8], F32, "ors")
        ob = [kb.sb(st, [64, 512], BF16, "ob") for _ in range(2)]
        pA = [kb.ps(st, [128, 512], F32, "pgA") for _ in range(5)]
        pT = [kb.ps(st, [64, 1024], BF16, "pgT") for _ in range(2)]
        ca = [0]
        ct = [0]

        def psA():
            p = pA[ca[0] % 5]
            ca[0] += 1
            return p

        def psT():
            p = pT[ct[0] % 2]
            ct[0] += 1
            return p

        def hs(h):
            return slice(h * 64, (h + 1) * 64)

        def perhead_scale(out_t, in_t, in_tile, sc1, sc1_tile, sc2=None, sc2_tile=None, eng="dve"):
            for h in range(8):
                if sc2 is None:
                    kb.op(eng, lambda e: e.tensor_scalar(out=out_t[:, hs(h)], in0=in_t[:, hs(h)], scalar1=sc1[:, h:h + 1], scalar2=None, op0=ALU.mult),
                          reads=[in_tile, sc1_tile], writes=[out_t])
                elif isinstance(sc2, float):
                    kb.op(eng, lambda e: e.tensor_scalar(out=out_t[:, hs(h)], in0=in_t[:, hs(h)], scalar1=sc1[:, h:h + 1], scalar2=sc2, op0=ALU.mult, op1=ALU.mult),
                          reads=[in_tile, sc1_tile], writes=[out_t])
                else:
                    kb.op(eng, lambda e: e.tensor_scalar(out=out_t[:, hs(h)], in0=in_t[:, hs(h)], scalar1=sc1[:, h:h + 1], scalar2=sc2[:, h:h + 1], op0=ALU.mult, op1=ALU.mult),
                          reads=[in_tile, sc1_tile, sc2_tile], writes=[out_t])

        yi = 0
        for s in range(NSEQ):
            if s >= dbg_seq:
                continue
            kb.op("dve", lambda e: e.memset(S32[:, :], 0.0), writes=[S32])
            kb.op("dve", lambda e: e.memset(Sb[:, :], 0.0), writes=[Sb])
            for gi in range(SEQ // G):
                if gi >= dbg_groups:
                    continue
                tok0 = gi * G
                for c in range(12):
                    kb.dma("sp", pre[:, c, :], D[f"GQKV{s}"][c, :, tok0:tok0 + G + 4], reads=[D[f"GQKV{s}"]], writes=[pre])
                kb.dma("sp", gtl[:, :, :], D[f"GG{s}"][tok0:tok0 + G, :].rearrange("(c p) h -> p c h", p=64), reads=[D[f"GG{s}"]], writes=[gtl])
                kb.dma("sp", btl[:, :, :], D[f"GB{s}"][tok0:tok0 + G, :].rearrange("(c p) h -> p c h", p=64), reads=[D[f"GB{s}"]], writes=[btl])
                kb.dma("sp", gzl[:, :, :], D[f"GZ{s}"][tok0:tok0 + G, :].rearrange("(c p) z -> p c z", p=64), reads=[D[f"GZ{s}"]], writes=[gzl])
                for c in range(12):
                    a_ = acc[c % 2]
                    kb.op("dve", lambda e: e.tensor_scalar(out=a_[:, :], in0=pre[:, c, 1:G + 1], scalar1=cwg[:, c, 0:1], scalar2=None, op0=ALU.mult), reads=[pre, cwg], writes=[a_])
                    for j in range(1, 4):
                        kb.op("dve", lambda e: e.scalar_tensor_tensor(out=a_[:, :], in0=pre[:, c, 1 + j:1 + j + G], scalar=cwg[:, c, j:j + 1], in1=a_[:, :], op0=ALU.mult, op1=ALU.add),
                              reads=[pre, cwg, a_], writes=[a_])
                    if c >= 8:
                        kb.op("act", lambda e: e.activation(out=vn[:, c - 8, :], in_=a_[:, :], func=AF.Silu), reads=[a_], writes=[vn])
                        continue
                    s_ = sl[c % 2]
                    q_ = sqb[c % 2]
                    r_ = rn[c % 2]
                    kb.op("act", lambda e: e.activation(out=s_[:, :], in_=a_[:, :], func=AF.Silu), reads=[a_], writes=[s_])
                    kb.op("pool", lambda e: e.tensor_tensor(out=q_[:, :], in0=s_[:, :], in1=s_[:, :], op=ALU.mult), reads=[s_], writes=[q_])
                    p = psA()
                    kb.op("pe", lambda e: e.matmul(p[:, :], lhsT=Bones[:, :], rhs=q_[:, :], start=True, stop=True), reads=[Bones, q_], writes=[p])
                    kb.op("dve", lambda e: e.tensor_scalar(out=r_[:, :], in0=p[:, :], scalar1=1e-6, scalar2=None, op0=ALU.add), reads=[p], writes=[r_])
                    kb.op("dve", lambda e: e.reciprocal(out=r_[:, :], in_=r_[:, :]), reads=[r_], writes=[r_])
                    kb.op("act", lambda e: e.activation(out=r_[:, :], in_=r_[:, :], func=AF.Sqrt), reads=[r_], writes=[r_])
                    kb.op("dve", lambda e: e.scalar_tensor_tensor(out=qkn[:, c, :], in0=s_[:, :], scalar=(0.125 if c < 4 else 1.0), in1=r_[:, :], op0=ALU.mult, op1=ALU.mult),
                          reads=[s_, r_], writes=[qkn])
                if dbg_stop <= 1:
                    continue
                kb.dma("sp", qodd[:, :, :], qkn[64:128, :, :], reads=[qkn], writes=[qodd])
                for cc in range(8):
                    cs_ = slice(cc * 64, (cc + 1) * 64)
                    gt = gtl[:, cc, :]
                    bt = btl[:, cc, :]

                    def kT(h):
                        return qkn[0:64, 4 + h // 2, cs_] if h % 2 == 0 else qodd[:, 4 + h // 2, cs_]

                    def qT(h):
                        return qkn[0:64, h // 2, cs_] if h % 2 == 0 else qodd[:, h // 2, cs_]

                    p = psT()
                    for hp in range(4):
                        kb.op("pe", lambda e: e.transpose(out=p[:, hp * 128:(hp + 1) * 128], in_=qkn[:, 4 + hp, cs_], identity=identb[:, :]), reads=[qkn, identb], writes=[p])
                        kb.op("pe", lambda e: e.transpose(out=p[:, 512 + hp * 128:512 + (hp + 1) * 128], in_=vn[:, hp, cs_], identity=identb[:, :]), reads=[vn, identb], writes=[p])
                    kb.op("act", lambda e: e.activation(out=kvtm[:, :, :], in_=p[:, :].rearrange("p (a b) -> p a b", a=2), func=AF.Copy), reads=[p], writes=[kvtm])
                    Ktm = kvtm[:, 0, :]
                    Vtm = kvtm[:, 1, :]
                    if dbg_stop <= 2:
                        continue
                    p = psA()
                    kb.op("pe", lambda e: e.matmul(p[0:64, 0:8], lhsT=LT[:, :], rhs=gt, start=True, stop=True), reads=[LT, gtl], writes=[p])
                    kb.op("dve", lambda e: e.tensor_copy(out=gcs[:, :], in_=p[0:64, 0:8]), reads=[p], writes=[gcs])
                    for h in range(8):
                        kb.op("pool", lambda e: e.tensor_scalar(out=gU[:, hs(h)], in0=LT[:, :], scalar1=gt[:, h:h + 1], scalar2=None, op0=ALU.mult), reads=[LT, gtl], writes=[gU])
                    pR = psA()
                    kb.op("pe", lambda e: e.matmul(pR[0:64, :], lhsT=ones64[:, 0:64], rhs=gU[:, :], start=True, stop=True), reads=[ones64, gU], writes=[pR])
                    kb.op("act", lambda e: e.activation(out=eg[:, :], in_=gcs[:, :], func=AF.Exp), reads=[gcs], writes=[eg])
                    kb.op("act", lambda e: e.activation(out=eglast[:, :], in_=pR[0:64, 63:512:64], func=AF.Exp), reads=[pR], writes=[eglast])
                    kb.op("dve", lambda e: e.tensor_tensor(out=egl[:, :], in0=pR[0:64, 63:512:64], in1=gcs[:, :], op=ALU.subtract), reads=[pR, gcs], writes=[egl])
                    kb.op("act", lambda e: e.activation(out=egl[:, :], in_=egl[:, :], func=AF.Exp), reads=[egl], writes=[egl])
                    if dbg_stop <= 3:
                        continue
                    kb.op("dve", lambda e: e.tensor_scalar(out=ngcs[:, :], in0=gcs[:, :], scalar1=-1.0, scalar2=None, op0=ALU.mult), reads=[gcs], writes=[ngcs])
                    for h in range(8):
                        kb.op("dve", lambda e: e.tensor_scalar(out=arg[:, hs(h)], in0=pR[0:64, hs(h)], scalar1=ngcs[:, h:h + 1], scalar2=-1.0, op0=ALU.add, op1=ALU.mult),
                              reads=[pR, ngcs], writes=[arg])
                    if dbg_stop <= 3.2:
                        continue
                    kb.op("dve", lambda e: e.tensor_scalar(out=arg[:, :], in0=arg[:, :], scalar1=0.0, scalar2=None, op0=ALU.min), reads=[arg], writes=[arg])
                    if dbg_stop <= 3.4:
                        continue
                    kb.op("act", lambda e: e.activation(out=Dm[:, :], in_=arg[:, :], func=AF.Exp), reads=[arg], writes=[Dm])
                    if dbg_stop <= 3.6:
                        continue
                    kb.op("dve", lambda e: e.tensor_tensor(out=Ds[:, :], in0=Dm[:, :], in1=trilS[:, :], op=ALU.mult), reads=[Dm, trilS], writes=[Ds])
                    kb.op("dve", lambda e: e.tensor_tensor(out=Dib[:, :], in0=Dm[:, :], in1=trilI[:, :], op=ALU.mult), reads=[Dm, trilI], writes=[Dib])
                    if dbg_stop <= 4:
                        continue
                    pM1 = psA()
                    pM2 = psA()
                    for h in range(8):
                        kb.op("pe", lambda e: e.matmul(pM1[0:64, hs(h)], lhsT=kT(h), rhs=kT(h), start=True, stop=True), reads=[qkn, qodd], writes=[pM1])
                    for h in range(8):
                        kb.op("pe", lambda e: e.matmul(pM2[0:64, hs(h)], lhsT=kT(h), rhs=qT(h), start=True, stop=True), reads=[qkn, qodd], writes=[pM2])
                    kb.op("dve", lambda e: e.tensor_tensor(out=A32[:, :], in0=pM1[0:64, :], in1=Ds[:, :], op=ALU.mult), reads=[pM1, Ds], writes=[A32])
                    if dbg_stop <= 4.5:
                        continue
                    y0 = Y[yi % 2]
                    yt0 = YT[yi % 2]
                    yi += 1
                    perhead_scale(y0, A32, A32, bt, btl, -1.0)
                    p = psT()
                    for h in range(8):
                        kb.op("pe", lambda e: e.transpose(out=p[:, hs(h)], in_=y0[:, hs(h)], identity=identb[0:64, 0:64]), reads=[y0, identb], writes=[p])
                    kb.op("act", lambda e: e.activation(out=yt0[:, :], in_=p[:, 0:512], func=AF.Copy), reads=[p], writes=[yt0])
                    p = psT()
                    for h in range(8):
                        kb.op("pe", lambda e: e.transpose(out=p[:, hs(h)], in_=Dib[:, hs(h)], identity=identb[0:64, 0:64]), reads=[Dib, identb], writes=[p])
                    kb.op("act", lambda e: e.activation(out=DiT[:, :], in_=p[:, 0:512], func=AF.Copy), reads=[p], writes=[DiT])
                    kb.op("dve", lambda e: e.tensor_tensor(out=qkT[:, :], in0=pM2[0:64, :], in1=DiT[:, :], op=ALU.mult), reads=[pM2, DiT], writes=[qkT])
                    if dbg_stop <= 5:
                        continue
                    kb.op("dve", lambda e: e.tensor_tensor(out=PT32[:, :], in0=yt0[:, :], in1=I8[:, :], op=ALU.add), reads=[yt0, I8], writes=[PT32])
                    kb.op("act", lambda e: e.activation(out=PTb[:, :], in_=PT32[:, :], func=AF.Copy), reads=[PT32], writes=[PTb])
                    yc, ytc = y0, yt0
                    for it in range(5):
                        yn = Y[yi % 2]
                        ytn = YT[yi % 2]
                        yi += 1
                        p1 = psA()
                        p2 = psA()
                        for h in range(8):
                            kb.op("pe", lambda e: e.matmul(p1[0:64, hs(h)], lhsT=ytc[:, hs(h)], rhs=yc[:, hs(h)], start=True, stop=True), reads=[ytc, yc], writes=[p1])
                        for h in range(8):
                            kb.op("pe", lambda e: e.matmul(p2[0:64, hs(h)], lhsT=yc[:, hs(h)], rhs=ytc[:, hs(h)], start=True, stop=True), reads=[ytc, yc], writes=[p2])
                        kb.op("act", lambda e: e.activation(out=yn[:, :], in_=p1[0:64, :], func=AF.Copy), reads=[p1], writes=[yn])
                        kb.op("dve", lambda e: e.tensor_copy(out=ytn[:, :], in_=p2[0:64, :]), reads=[p2], writes=[ytn])
                        p3 = psA()
                        for h in range(8):
                            kb.op("pe", lambda e: e.matmul(p3[0:64, hs(h)], lhsT=yn[:, hs(h)], rhs=PTb[:, hs(h)], start=True, stop=True), reads=[yn, PTb], writes=[p3])
                        kb.op("dve", lambda e: e.tensor_tensor(out=PT32[:, :], in0=PT32[:, :], in1=p3[0:64, :], op=ALU.add), reads=[PT32, p3], writes=[PT32])
                        kb.op("act", lambda e: e.activation(out=PTb[:, :], in_=PT32[:, :], func=AF.Copy), reads=[PT32], writes=[PTb])
                        yc, ytc = yn, ytn
                    if dbg_stop <= 6:
                        continue
                    perhead_scale(Vb, Vtm, kvtm, bt, btl, eng="pool")
                    perhead_scale(Kbg, Ktm, kvtm, bt, btl, eg, eg)
                    perhead_scale(ktil, Ktm, kvtm, egl, egl, eng="pool")
                    pu = psA()
                    pw = psA()
                    for h in range(8):
                        kb.op("pe", lambda e: e.matmul(pu[0:64, hs(h)], lhsT=PTb[:, hs(h)], rhs=Vb[:, hs(h)], start=True, stop=True), reads=[PTb, Vb], writes=[pu])
                    for h in range(8):
                        kb.op("pe", lambda e: e.matmul(pw[0:64, hs(h)], lhsT=Kbg[:, hs(h)], rhs=PTb[:, hs(h)], start=True, stop=True), reads=[PTb, Kbg], writes=[pw])
                    kb.op("act", lambda e: e.activation(out=u32[:, :], in_=pu[0:64, :], func=AF.Copy), reads=[pu], writes=[u32])
                    kb.op("act", lambda e: e.activation(out=wTb[:, :], in_=pw[0:64, :], func=AF.Copy), reads=[pw], writes=[wTb])
                    if dbg_stop <= 7:
                        continue
                    pws = psA()
                    for h in range(8):
                        kb.op("pe", lambda e: e.matmul(pws[0:64, hs(h)], lhsT=wTb[:, hs(h)], rhs=Sb[:, hs(h)], start=True, stop=True), reads=[wTb, Sb], writes=[pws])
                    kb.op("dve", lambda e: e.tensor_tensor(out=vnew[:, :], in0=u32[:, :], in1=pws[0:64, :], op=ALU.subtract), reads=[u32, pws], writes=[vnew])
                    po1 = psA()
                    po2 = psA()
                    for h in range(8):
                        kb.op("pe", lambda e: e.matmul(po1[0:64, hs(h)], lhsT=qT(h), rhs=Sb[:, hs(h)], start=True, stop=True), reads=[qkn, qodd, Sb], writes=[po1])
                    for h in range(8):
                        kb.op("pe", lambda e: e.matmul(po2[0:64, hs(h)], lhsT=qkT[:, hs(h)], rhs=vnew[:, hs(h)], start=True, stop=True), reads=[qkT, vnew], writes=[po2])
                    pS_ = psA()
                    for h in range(8):
                        kb.op("pe", lambda e: e.matmul(pS_[0:64, hs(h)], lhsT=ktil[:, hs(h)], rhs=vnew[:, hs(h)], start=True, stop=True), reads=[ktil, vnew], writes=[pS_])
                    kb.op("act", lambda e: e.activation(out=o2[:, :], in_=po2[0:64, :], func=AF.Copy), reads=[po2], writes=[o2])
                    for h in range(8):
                        kb.op("dve", lambda e: e.scalar_tensor_tensor(out=o32[:, hs(h)], in0=po1[0:64, hs(h)], scalar=eg[:, h:h + 1], in1=o2[:, hs(h)], op0=ALU.mult, op1=ALU.add),
                              reads=[po1, eg, o2], writes=[o32])
                    for h in range(8):
                        kb.op("dve", lambda e: e.scalar_tensor_tensor(out=S32[:, hs(h)], in0=S32[:, hs(h)], scalar=eglast[:, h:h + 1], in1=pS_[0:64, hs(h)], op0=ALU.mult, op1=ALU.add),
                              reads=[S32, eglast, pS_], writes=[S32])
                    kb.op("act", lambda e: e.activation(out=Sb[:, :], in_=S32[:, :], func=AF.Copy), reads=[S32], writes=[Sb])
                    kb.op("pool", lambda e: e.tensor_tensor(out=osq[:, :], in0=o32[:, :], in1=o32[:, :], op=ALU.mult), reads=[o32], writes=[osq])
                    kb.op("dve", lambda e: e.tensor_reduce(out=oss[:, :], in_=osq[:, :].rearrange("p (h d) -> p h d", h=8), axis=AX.X, op=ALU.add), reads=[osq], writes=[oss])
                    kb.op("dve", lambda e: e.tensor_scalar(out=oss[:, :], in0=oss[:, :], scalar1=1.0 / 64, scalar2=EPS, op0=ALU.mult, op1=ALU.add), reads=[oss], writes=[oss])
                    kb.op("dve", lambda e: e.reciprocal(out=oss[:, :], in_=oss[:, :]), reads=[oss], writes=[oss])
                    kb.op("act", lambda e: e.activation(out=ors[:, :], in_=oss[:, :], func=AF.Sqrt), reads=[oss], writes=[ors])
                    perhead_scale(o32, o32, o32, ors, ors)
                    kb.op("pool", lambda e: e.tensor_tensor(out=o32[:, :], in0=o32[:, :], in1=nw8[:, :, :].rearrange("p h d -> p (h d)"), op=ALU.mult), reads=[o32, nw8], writes=[o32])
                    o_ = ob[cc % 2]
                    kb.op("dve", lambda e: e.tensor_tensor(out=o_[:, :], in0=o32[:, :], in1=gzl[:, cc, :], op=ALU.mult), reads=[o32, gzl], writes=[o_])
                    kb.dma("sp", D[f"MIX{s}"][tok0 + cc * 64:tok0 + (cc + 1) * 64, 512:1024], o_[:, :], reads=[o_], writes=[D[f"MIX{s}"]])
        kb.barrier()
```

```python
import numpy as np
from contextlib import ExitStack
import ml_dtypes
import concourse.bass as bass
import concourse.mybir as mybir
from concourse.bass_utils import run_bass_kernel_spmd

F32 = mybir.dt.float32
BF16 = mybir.dt.bfloat16
I32 = mybir.dt.int32
AF = mybir.ActivationFunctionType
ALU = mybir.AluOpType
AX = mybir.AxisListType

NCORES = 8
SEQ = 4096
DM = 1024
NSEQ = 2
NTOK = NSEQ * SEQ
MEM = 256
EPS = 1e-6
NDS = 8


class Tile:
    __slots__ = ("t", "w", "r", "name")

    def __init__(self, t, name=""):
        self.t = t
        self.w = None
        self.r = {}
        self.name = name

    def __getitem__(self, idx):
        return self.t[idx]


class Eng:
    def __init__(self, name, h, semid):
        self.name, self.h, self.semid = name, h, semid
        self.count = 0
        self.known = {}


class KB:
    def __init__(self):
        self.nc = bass.Bass("TRN2", target_bir_lowering=False)
        self.es = ExitStack()
        nc = self.nc
        self.sems = []
        self.E = {}
        for name, h in [("pe", nc.tensor), ("act", nc.scalar), ("dve", nc.vector),
                        ("pool", nc.gpsimd), ("sp", nc.sync)]:
            self.sems.append(self.es.enter_context(nc.semaphore("s_" + name)))
            self.E[name] = Eng(name, h, len(self.sems) - 1)
        self.dq = {}
        for q in ("sp", "pool", "act"):
            ids = []
            for i in range(NDS):
                self.sems.append(self.es.enter_context(nc.semaphore(f"d_{q}{i}")))
                ids.append(len(self.sems) - 1)
            self.dq[q] = dict(ids=ids, vals=[0] * NDS, idx=0)
        self.n_ins = 0
        self.n_wait = 0
        self.uid = 0

    def sb(self, stack, shape, dt, name=None):
        self.uid += 1
        name = f"{name or 't'}_{self.uid}"
        return Tile(stack.enter_context(self.nc.sbuf_tensor(name, list(shape), dt)), name)

    def ps(self, stack, shape, dt, name=None):
        self.uid += 1
        name = f"{name or 'p'}_{self.uid}"
        return Tile(stack.enter_context(self.nc.psum_tensor(name, list(shape), dt)), name)

    def dram(self, name, shape, dt, kind="Internal"):
        return Tile(self.nc.dram_tensor(name, list(shape), dt, kind=kind).ap(), name)

    def _wait(self, eng, s, v):
        if s == eng.semid or eng.known.get(s, 0) >= v:
            return
        eng.h.wait_ge(self.sems[s], v)
        eng.known[s] = v
        self.n_wait += 1

    def _deps(self, eng, reads, writes):
        deps = {}
        for t in reads:
            if t.w is not None and deps.get(t.w[0], 0) < t.w[1]:
                deps[t.w[0]] = t.w[1]
            if t.w is not None and t.w[0] == eng.semid and eng.name != "pe" and eng.known.get(eng.semid, 0) < t.w[1]:
                eng.h.wait_ge(self.sems[eng.semid], t.w[1])
                eng.known[eng.semid] = t.w[1]
                self.n_wait += 1
        for t in writes:
            if t.w is not None and deps.get(t.w[0], 0) < t.w[1]:
                deps[t.w[0]] = t.w[1]
            for s, v in t.r.items():
                if deps.get(s, 0) < v:
                    deps[s] = v
        for s, v in deps.items():
            self._wait(eng, s, v)

    def _mark(self, ev, reads, writes):
        s, v = ev
        for t in reads:
            if t.r.get(s, 0) < v:
                t.r[s] = v
        for t in writes:
            t.w = ev
            t.r = {}

    def op(self, en, fn, reads=(), writes=()):
        eng = self.E[en]
        self._deps(eng, reads, writes)
        ins = fn(eng.h)
        eng.count += 1
        ins.then_inc(self.sems[eng.semid], 1)
        self._mark((eng.semid, eng.count), reads, writes)
        self.n_ins += 1

    def dma(self, q, out, in_, reads=(), writes=(), **kw):
        eng = self.E[q]
        dq = self.dq[q]
        i = dq["idx"]
        dq["idx"] = (i + 1) % NDS
        semid = dq["ids"][i]
        prev = dq["vals"][i]
        self._deps(eng, reads, writes)
        if prev > 0:
            self._wait(eng, semid, prev)
        eng.h.dma_start(out=out, in_=in_, **kw).then_inc(self.sems[semid], 16)
        dq["vals"][i] = prev + 16
        self._mark((semid, prev + 16), reads, writes)
        self.n_ins += 1

    def barrier(self):
        cur = {}
        for e in self.E.values():
            cur[e.semid] = e.count
        for dq in self.dq.values():
            for sid, v in zip(dq["ids"], dq["vals"]):
                cur[sid] = v
        for e in self.E.values():
            for s, v in cur.items():
                if v > 0:
                    self._wait(e, s, v)

    def finish(self):
        self.barrier()
        self.es.close()


def load_w(kb, dst, src, src_ap, KC, cols, gain, stg, dst_col0=0, src_col0=0, scale=None, cnt=[0]):
    SC = stg[0].t.shape[1]
    for kc in range(KC):
        for c0 in range(0, cols, SC):
            cw = min(SC, cols - c0)
            st = stg[cnt[0] % len(stg)]
            kb.dma("sp", st[:, :cw], src_ap[kc * 128:(kc + 1) * 128, src_col0 + c0:src_col0 + c0 + cw],
                   reads=[src], writes=[st])
            o = dst[:, kc, dst_col0 + c0:dst_col0 + c0 + cw]
            if gain is not None:
                g = gain[:, kc:kc + 1]
                if scale is None:
                    if cnt[0] % 2 == 0:
                        kb.op("dve", lambda e: e.tensor_scalar(out=o, in0=st[:, :cw], scalar1=g, scalar2=None, op0=ALU.mult),
                              reads=[st, gain], writes=[dst])
                    else:
                        kb.op("act", lambda e: e.activation(out=o, in_=st[:, :cw], func=AF.Copy, scale=g),
                              reads=[st, gain], writes=[dst])
                else:
                    kb.op("dve", lambda e: e.tensor_scalar(out=o, in0=st[:, :cw], scalar1=g, scalar2=float(scale), op0=ALU.mult, op1=ALU.mult),
                          reads=[st, gain], writes=[dst])
            else:
                sc = 1.0 if scale is None else float(scale)
                if cnt[0] % 2 == 0:
                    kb.op("dve", lambda e: e.tensor_scalar(out=o, in0=st[:, :cw], scalar1=sc, scalar2=None, op0=ALU.mult),
                          reads=[st], writes=[dst])
                else:
                    kb.op("act", lambda e: e.activation(out=o, in_=st[:, :cw], func=AF.Copy, scale=sc),
                          reads=[st], writes=[dst])
            cnt[0] += 1


def load_gain_col(kb, dst, src, row):
    ap = src[row] if row is not None else src.t
    kb.dma("sp", dst[:, :], ap.rearrange("(k p) -> p k", p=128), reads=[src], writes=[dst],
           allow_slow_non_contiguous=True)


def rms_rstd(kb, x_ap, xt, junk, ss, rs, D):
    kb.op("act", lambda e: e.activation(out=junk[:, :D], in_=x_ap, func=AF.Square), reads=[xt], writes=[junk])
    kb.op("dve", lambda e: e.tensor_reduce(out=ss[:, 0:1], in_=junk[:, :D], axis=AX.X, op=ALU.add), reads=[junk], writes=[ss])
    kb.op("dve", lambda e: e.tensor_scalar(out=ss[:, 0:1], in0=ss[:, 0:1], scalar1=float(D * EPS), scalar2=None,
                                           op0=ALU.add), reads=[ss], writes=[ss])
    kb.op("dve", lambda e: e.reciprocal(out=ss[:, 0:1], in_=ss[:, 0:1]), reads=[ss], writes=[ss])
    kb.op("act", lambda e: e.activation(out=rs[:, 0:1], in_=ss[:, 0:1], func=AF.Sqrt), reads=[ss], writes=[rs])


def phase_mlp(kb, x_in, x_out, w1, w2, gain, layer, ident_d, final_gain=None, ntok=NTOK):
    nc = kb.nc
    G = 256
    NG = ntok // G
    with ExitStack() as st:
        W1 = kb.sb(st, [128, 8, 4096], BF16, "W1")
        W2 = kb.sb(st, [128, 32, 1024], BF16, "W2")
        gcol = kb.sb(st, [128, 8], F32, "gcol")
        ident = kb.sb(st, [128, 128], BF16, "ident")
        kb.dma("pool", ident[:, :], ident_d[:, :], reads=[ident_d], writes=[ident])
        load_gain_col(kb, gcol, gain, layer)
        with ExitStack() as st2:
            stg = [kb.sb(st2, [128, 2048], F32, "stg") for _ in range(2)]
            load_w(kb, W1, w1, w1[layer], 8, 4096, gcol, stg)
            load_w(kb, W2, w2, w2[layer], 32, 1024, None, stg)
            kb.barrier()
        xt = [kb.sb(st, [128, 2, 1024], F32, "xt") for _ in range(3)]
        xn = kb.sb(st, [128, 2, 1024], BF16, "xn")
        xnT = [kb.sb(st, [128, 8, G], BF16, "xnT") for _ in range(2)]
        hT = [kb.sb(st, [128, 8, G], BF16, "hT") for _ in range(4)]
        rr = [kb.sb(st, [128, 2, G], BF16, "rr") for _ in range(2)]
        junk = kb.sb(st, [128, 1024], F32, "junk")
        ss = [kb.sb(st, [128, 2], F32, "ss") for _ in range(2)]
        rs = [kb.sb(st, [128, 1], F32, "rs") for _ in range(2)]
        pT = [kb.ps(st, [128, 8, 128], BF16, "pT") for _ in range(2)]
        pH = [kb.ps(st, [128, 2, G], F32, "pH") for _ in range(2)]
        pO = [kb.ps(st, [128, 512], F32, "pO") for _ in range(2)]
        fg = None
        if final_gain is not None:
            fg = kb.sb(st, [128, 1024], F32, "fg")
            kb.dma("pool", fg[:, :], final_gain.t.partition_broadcast(128), reads=[final_gain], writes=[fg])
        x_in_v = x_in.t.rearrange("(g t p) d -> g p t d", t=2, p=128)
        x_out_v = x_out.t.rearrange("(g t p) d -> g p t d", t=2, p=128)
        cnt = [0]

        def prep(g):
            b = g % 2
            kb.dma("sp", xt[g % 3][:, :, :], x_in_v[g], reads=[x_in], writes=[xt[g % 3]])
            for t in range(2):
                i = cnt[0] % 2
                cnt[0] += 1
                rms_rstd(kb, xt[g % 3][:, t, :], xt[g % 3], junk, ss[i], rs[i], 1024)
                kb.op("dve", lambda e: e.tensor_scalar(out=xn[:, t, :], in0=xt[g % 3][:, t, :], scalar1=rs[i][:, 0:1], scalar2=32.0,
                                                       op0=ALU.mult, op1=ALU.mult), reads=[xt[g % 3], rs[i]], writes=[xn])
                p = pT[i]
                for kc in range(8):
                    kb.op("pe", lambda e: e.transpose(out=p[:, kc, :], in_=xn[:, t, kc * 128:(kc + 1) * 128], identity=ident[:, :]),
                          reads=[xn, ident], writes=[p])
                kb.op("act", lambda e: e.activation(out=xnT[b][:, :, t * 128:(t + 1) * 128], in_=p[:, :, :], func=AF.Copy),
                      reads=[p], writes=[xnT[b]])

        prep(0)
        hcnt = 0
        ocnt = 0
        for g in range(NG):
            b = g % 2
            for fp in range(16):
                ph = pH[hcnt % 2]
                r = rr[hcnt % 2]
                hcnt += 1
                for j in range(2):
                    f = fp * 2 + j
                    for kc in range(8):
                        kb.op("pe", lambda e: e.matmul(ph[:, j, :], lhsT=W1[:, kc, f * 128:(f + 1) * 128], rhs=xnT[b][:, kc, :],
                                                       start=(kc == 0), stop=(kc == 7)),
                              reads=[W1, xnT[b]], writes=[ph])
                kb.op("act", lambda e: e.activation(out=r[:, :, :], in_=ph[:, :, :], func=AF.Relu), reads=[ph], writes=[r])
                h = hT[fp // 4]
                kb.op("dve", lambda e: e.tensor_tensor(out=h[:, (fp % 4) * 2:(fp % 4) * 2 + 2, :], in0=r[:, :, :], in1=r[:, :, :], op=ALU.mult),
                      reads=[r], writes=[h])
            if g + 1 < NG:
                prep(g + 1)
            for t in range(2):
                for half in range(2):
                    po = pO[ocnt % 2]
                    ocnt += 1
                    for f in range(32):
                        h = hT[f // 8]
                        kb.op("pe", lambda e: e.matmul(po[:, :], lhsT=h[:, f % 8, t * 128:(t + 1) * 128], rhs=W2[:, f, half * 512:(half + 1) * 512],
                                                       start=(f == 0), stop=(f == 31)),
                              reads=[h, W2], writes=[po])
                    xs = xt[g % 3][:, t, half * 512:(half + 1) * 512]
                    kb.op("dve", lambda e: e.tensor_tensor(out=xs, in0=xs, in1=po[:, :], op=ALU.add), reads=[po, xt[g % 3]], writes=[xt[g % 3]])
                if fg is not None:
                    i = cnt[0] % 2
                    cnt[0] += 1
                    rms_rstd(kb, xt[g % 3][:, t, :], xt[g % 3], junk, ss[i], rs[i], 1024)
                    kb.op("dve", lambda e: e.tensor_scalar(out=xt[g % 3][:, t, :], in0=xt[g % 3][:, t, :], scalar1=rs[i][:, 0:1], scalar2=32.0,
                                                           op0=ALU.mult, op1=ALU.mult), reads=[xt[g % 3], rs[i]], writes=[xt[g % 3]])
                    kb.op("dve", lambda e: e.tensor_tensor(out=xt[g % 3][:, t, :], in0=xt[g % 3][:, t, :], in1=fg[:, :], op=ALU.mult),
                          reads=[xt[g % 3], fg], writes=[xt[g % 3]])
            kb.dma("sp", x_out_v[g], xt[g % 3][:, :, :], reads=[xt[g % 3]], writes=[x_out])
        kb.barrier()


C_Q, C_KCMP, C_VCMP, C_KSLC, C_VSLC, C_KWIN, C_VWIN, C_GATE = 0, 512, 640, 768, 896, 1024, 1152, 1280
C_GQ, C_GK, C_GV, C_GA, C_GB, C_GZ = 1304, 1816, 2328, 2840, 2848, 2856
IN_COLS = 3368
TM0 = 26 * 128
TMA = 296
WCOLS = TM0 + TMA + 512
PI = float(np.pi)


def make_scratch(kb, dbg=False):
    kind = "ExternalOutput" if dbg else "Internal"
    D = {}
    for s in range(NSEQ):
        D[f"QT{s}"] = kb.dram(f"QT{s}", [4, 128, SEQ], BF16, kind)
        D[f"KST{s}"] = kb.dram(f"KST{s}", [128, SEQ], BF16, kind)
        D[f"KWT{s}"] = kb.dram(f"KWT{s}", [128, SEQ], BF16, kind)
        D[f"KCT{s}"] = kb.dram(f"KCT{s}", [128, SEQ], BF16, kind)
        D[f"VCT{s}"] = kb.dram(f"VCT{s}", [128, SEQ], BF16, kind)
        D[f"VS{s}"] = kb.dram(f"VS{s}", [SEQ, 128], BF16, kind)
        D[f"VW{s}"] = kb.dram(f"VW{s}", [SEQ, 128], BF16, kind)
        D[f"GATE{s}"] = kb.dram(f"GATE{s}", [SEQ, 24], F32, kind)
        D[f"GQKV{s}"] = kb.dram(f"GQKV{s}", [12, 128, 4 + SEQ], F32, kind)
        D[f"GG{s}"] = kb.dram(f"GG{s}", [SEQ, 8], F32, kind)
        D[f"GB{s}"] = kb.dram(f"GB{s}", [SEQ, 8], F32, kind)
        D[f"GZ{s}"] = kb.dram(f"GZ{s}", [SEQ, 512], BF16, kind)
        D[f"MIX{s}"] = kb.dram(f"MIX{s}", [SEQ, 1024], BF16, kind)
    return D


def phase_a(kb, I, D, C, dbg_stop=99, dbg_groups=99):
    x, pos, w_in = I["x"], I["positions"], I["hyb_w_in"]
    with ExitStack() as st:
        W = kb.sb(st, [128, 8, WCOLS], BF16, "Win")
        gcol = kb.sb(st, [128, 8], F32, "gcol")
        ident = kb.sb(st, [128, 128], BF16, "ident")
        invf = kb.sb(st, [128, 1], F32, "invf")
        dtb = kb.sb(st, [128, 8], F32, "dtb")
        nea = kb.sb(st, [128, 8], F32, "nea")
        zero = kb.sb(st, [128, 4], F32, "zero")
        kb.dma("pool", ident[:, :], C["ident"][:, :], reads=[C["ident"]], writes=[ident])
        kb.dma("pool", invf[:, :], C["invf"][:, :], reads=[C["invf"]], writes=[invf])
        kb.dma("pool", dtb[:, :], I["hyb_gdn_dt_bias"][0].partition_broadcast(128), reads=[I["hyb_gdn_dt_bias"]], writes=[dtb])
        kb.dma("pool", nea[:, :], I["hyb_gdn_a_log"][0].partition_broadcast(128), reads=[I["hyb_gdn_a_log"]], writes=[nea])
        kb.op("act", lambda e: e.activation(out=nea[:, :], in_=nea[:, :], func=AF.Exp), reads=[nea], writes=[nea])
        kb.op("dve", lambda e: e.tensor_scalar(out=nea[:, :], in0=nea[:, :], scalar1=-1.0, scalar2=None, op0=ALU.mult), reads=[nea], writes=[nea])
        kb.op("dve", lambda e: e.memset(zero[:, :], 0.0), writes=[zero])
        for s in range(NSEQ):
            for c in range(12):
                kb.dma("pool", D[f"GQKV{s}"][c, :, 0:4], zero[:, :], reads=[zero], writes=[D[f"GQKV{s}"]])
        load_gain_col(kb, gcol, I["norm_mix"], 0)
        wi = w_in[0]
        with ExitStack() as st2:
            stg = [kb.sb(st2, [128, 512], F32, "stg") for _ in range(3)]
            load_w(kb, W, w_in, wi, 8, 512, gcol, stg, dst_col0=0, src_col0=C_Q, scale=0.125)
            load_w(kb, W, w_in, wi, 8, 128, gcol, stg, dst_col0=8 * 128, src_col0=C_KSLC)
            load_w(kb, W, w_in, wi, 8, 128, gcol, stg, dst_col0=10 * 128, src_col0=C_KWIN)
            load_w(kb, W, w_in, wi, 8, 128, gcol, stg, dst_col0=12 * 128, src_col0=C_KCMP)
            load_w(kb, W, w_in, wi, 8, 128, gcol, stg, dst_col0=13 * 128, src_col0=C_VCMP)
            load_w(kb, W, w_in, wi, 8, 512, gcol, stg, dst_col0=14 * 128, src_col0=C_GQ)
            load_w(kb, W, w_in, wi, 8, 512, gcol, stg, dst_col0=18 * 128, src_col0=C_GK)
            load_w(kb, W, w_in, wi, 8, 512, gcol, stg, dst_col0=22 * 128, src_col0=C_GV)
            load_w(kb, W, w_in, wi, 8, 128, gcol, stg, dst_col0=TM0, src_col0=C_VSLC)
            load_w(kb, W, w_in, wi, 8, 128, gcol, stg, dst_col0=TM0 + 128, src_col0=C_VWIN)
            load_w(kb, W, w_in, wi, 8, 24, gcol, stg, dst_col0=TM0 + 256, src_col0=C_GATE)
            load_w(kb, W, w_in, wi, 8, 16, gcol, stg, dst_col0=TM0 + 280, src_col0=C_GA)
            load_w(kb, W, w_in, wi, 8, 512, gcol, stg, dst_col0=TM0 + TMA, src_col0=C_GZ)
            k = 0
            for (src0, ncol, dst0, sc) in [(C_Q, 512, 4 * 128, 0.125), (C_KSLC, 128, 9 * 128, 1.0), (C_KWIN, 128, 11 * 128, 1.0)]:
                nh = ncol // 64
                for kc in range(8):
                    sg = stg[k % 3]
                    k += 1
                    kb.dma("sp", sg[:, :ncol], wi[kc * 128:(kc + 1) * 128, src0:src0 + ncol], reads=[w_in], writes=[sg])
                    sv = sg[:, :ncol].rearrange("p (h t c) -> p h t c", h=nh, t=2)
                    dv = W[:, kc, dst0:dst0 + ncol].rearrange("p (h t c) -> p h t c", h=nh, t=2)
                    g = gcol[:, kc:kc + 1]
                    kb.op("dve", lambda e: e.tensor_scalar(out=dv[:, :, 0, :], in0=sv[:, :, 1, :], scalar1=g, scalar2=-sc, op0=ALU.mult, op1=ALU.mult),
                          reads=[sg, gcol], writes=[W])
                    kb.op("dve", lambda e: e.tensor_scalar(out=dv[:, :, 1, :], in0=sv[:, :, 0, :], scalar1=g, scalar2=sc, op0=ALU.mult, op1=ALU.mult),
                          reads=[sg, gcol], writes=[W])
            kb.barrier()
        G = 512
        if dbg_stop <= 0:
            return
        xt = [kb.sb(st, [128, 4, 1024], F32, "xt") for _ in range(2)]
        xn = kb.sb(st, [128, 1024], BF16, "xn")
        xnT = [kb.sb(st, [128, 8, G], BF16, "xnT") for _ in range(2)]
        junk = kb.sb(st, [128, 1024], F32, "junk")
        ss = [kb.sb(st, [128, 2], F32, "ss") for _ in range(2)]
        rs = [kb.sb(st, [128, 1], F32, "rs") for _ in range(2)]
        posi = kb.sb(st, [128, G], I32, "posi")
        posl = [kb.sb(st, [128, G], I32, "posl") for _ in range(2)]
        posf = kb.sb(st, [128, G], F32, "posf")
        ang = kb.sb(st, [128, G], F32, "ang")
        kf = kb.sb(st, [128, G], F32, "kf")
        sinT = kb.sb(st, [128, G], F32, "sinT")
        cosT = kb.sb(st, [128, G], F32, "cosT")
        t1 = [kb.sb(st, [128, G], F32, "t1") for _ in range(2)]
        t2 = [kb.sb(st, [128, G], F32, "t2") for _ in range(2)]
        ob = [kb.sb(st, [128, G], BF16, "ob") for _ in range(3)]
        of = [kb.sb(st, [128, G], F32, "of") for _ in range(3)]
        tma = [kb.sb(st, [128, 256], BF16, "tma") for _ in range(2)]
        tmg = [kb.sb(st, [128, 40], F32, "tmg") for _ in range(2)]
        tmz = [kb.sb(st, [128, 512], BF16, "tmz") for _ in range(2)]
        pT = [kb.ps(st, [128, 8, 128], BF16, "pT") for _ in range(2)]
        pA = [kb.ps(st, [128, 512], F32, "pA") for _ in range(4)]
        pB = [kb.ps(st, [128, 512], F32, "pB") for _ in range(2)]
        cnt = [0]
        ca = [0]
        cb = [0]
        co = [0]
        cf = [0]

        def fm(c, b):
            p = pA[ca[0] % 4]
            ca[0] += 1
            for kc in range(8):
                kb.op("pe", lambda e: e.matmul(p[:, :], lhsT=W[:, kc, c * 128:(c + 1) * 128], rhs=xnT[b][:, kc, :], start=(kc == 0), stop=(kc == 7)),
                      reads=[W, xnT[b]], writes=[p])
            return p

        def load_a(g_):
            s_, gi_ = divmod(g_, SEQ // G)
            kb.dma("sp", xt[g_ % 2][:, :, :], x.t[s_, gi_ * G:(gi_ + 1) * G, :].rearrange("(t p) d -> p t d", p=128), reads=[x], writes=[xt[g_ % 2]])
            kb.dma("sp", posl[g_ % 2][:, :], pos.t[s_, gi_ * G:(gi_ + 1) * G].partition_broadcast(128), reads=[pos], writes=[posl[g_ % 2]])

        for s in range(NSEQ):
            for gi in range(SEQ // G):
                g = s * (SEQ // G) + gi
                b = g % 2
                tok0 = gi * G
                if g >= dbg_groups:
                    continue
                if g == 0:
                    load_a(0)
                if g + 1 < min(dbg_groups, NSEQ * (SEQ // G)):
                    load_a(g + 1)
                for t in range(4):
                    i = cnt[0] % 2
                    cnt[0] += 1
                    rms_rstd(kb, xt[b][:, t, :], xt[b], junk, ss[i], rs[i], 1024)
                    kb.op("dve", lambda e: e.tensor_scalar(out=xn[:, :], in0=xt[b][:, t, :], scalar1=rs[i][:, 0:1], scalar2=32.0,
                                                           op0=ALU.mult, op1=ALU.mult), reads=[xt[b], rs[i]], writes=[xn])
                    p = pT[i]
                    for kc in range(8):
                        kb.op("pe", lambda e: e.transpose(out=p[:, kc, :], in_=xn[:, kc * 128:(kc + 1) * 128], identity=ident[:, :]),
                              reads=[xn, ident], writes=[p])
                    kb.op("act", lambda e: e.activation(out=xnT[b][:, :, t * 128:(t + 1) * 128], in_=p[:, :, :], func=AF.Copy),
                          reads=[p], writes=[xnT[b]])
                if dbg_stop <= 1:
                    continue
                kb.op("dve", lambda e: e.tensor_copy(out=posf[:, :], in_=posl[b][:, :]), reads=[posl[b]], writes=[posf])
                kb.op("dve", lambda e: e.tensor_scalar(out=ang[:, :], in0=posf[:, :], scalar1=invf[:, 0:1], scalar2=None, op0=ALU.mult),
                      reads=[posf, invf], writes=[ang])
                for (dst, shift) in [(sinT, 0.0), (cosT, 0.5 * PI)]:
                    kb.op("dve", lambda e: e.tensor_scalar(out=posf[:, :], in0=ang[:, :], scalar1=float(shift), scalar2=None, op0=ALU.add),
                          reads=[ang], writes=[posf])
                    kb.op("dve", lambda e: e.tensor_scalar(out=posi[:, :], in0=posf[:, :], scalar1=float(1.0 / (2 * PI)), scalar2=None, op0=ALU.mult),
                          reads=[posf], writes=[posi])
                    kb.op("dve", lambda e: e.tensor_copy(out=kf[:, :], in_=posi[:, :]), reads=[posi], writes=[kf])
                    kb.op("dve", lambda e: e.scalar_tensor_tensor(out=posf[:, :], in0=kf[:, :], scalar=float(-2 * PI), in1=posf[:, :], op0=ALU.mult, op1=ALU.add),
                          reads=[kf, posf], writes=[posf])
                    kb.op("dve", lambda e: e.tensor_scalar(out=kf[:, :], in0=posf[:, :], scalar1=PI, scalar2=float(-2 * PI), op0=ALU.is_gt, op1=ALU.mult),
                          reads=[posf], writes=[kf])
                    kb.op("dve", lambda e: e.tensor_tensor(out=posf[:, :], in0=posf[:, :], in1=kf[:, :], op=ALU.add), reads=[posf, kf], writes=[posf])
                    kb.op("dve", lambda e: e.tensor_scalar(out=posf[:, :], in0=posf[:, :], scalar1=PI, scalar2=-PI, op0=ALU.min, op1=ALU.max),
                          reads=[posf], writes=[posf])
                    kb.op("act", lambda e: e.activation(out=dst[:, :], in_=posf[:, :], func=AF.Sin), reads=[posf], writes=[dst])
                if dbg_stop <= 2:
                    continue
                ropes = [(c, c + 4, D[f"QT{s}"], D[f"QT{s}"][c, :, tok0:tok0 + G]) for c in range(4)]
                ropes += [(8, 9, D[f"KST{s}"], D[f"KST{s}"][:, tok0:tok0 + G]), (10, 11, D[f"KWT{s}"], D[f"KWT{s}"][:, tok0:tok0 + G])]
                for (c1, c2, dtile, dap) in ropes:
                    p1 = fm(c1, b)
                    p2 = fm(c2, b)
                    j = cb[0] % 2
                    cb[0] += 1
                    o = ob[co[0] % 3]
                    co[0] += 1
                    kb.op("dve", lambda e: e.tensor_tensor(out=t1[j][:, :], in0=p1[:, :], in1=cosT[:, :], op=ALU.mult), reads=[p1, cosT], writes=[t1[j]])
                    kb.op("dve", lambda e: e.tensor_tensor(out=t2[j][:, :], in0=p2[:, :], in1=sinT[:, :], op=ALU.mult), reads=[p2, sinT], writes=[t2[j]])
                    kb.op("pool", lambda e: e.tensor_tensor(out=o[:, :], in0=t1[j][:, :], in1=t2[j][:, :], op=ALU.add), reads=[t1[j], t2[j]], writes=[o])
                    kb.dma("sp", dap, o[:, :], reads=[o], writes=[dtile])
                for (c, dname) in [(12, "KCT"), (13, "VCT")]:
                    p1 = fm(c, b)
                    o = ob[co[0] % 3]
                    co[0] += 1
                    kb.op("act", lambda e: e.activation(out=o[:, :], in_=p1[:, :], func=AF.Copy), reads=[p1], writes=[o])
                    kb.dma("sp", D[f"{dname}{s}"][:, tok0:tok0 + G], o[:, :], reads=[o], writes=[D[f"{dname}{s}"]])
                for c in range(12):
                    p1 = fm(14 + c, b)
                    o = of[cf[0] % 3]
                    cf[0] += 1
                    kb.op("act", lambda e: e.activation(out=o[:, :], in_=p1[:, :], func=AF.Copy), reads=[p1], writes=[o])
                    kb.dma("sp", D[f"GQKV{s}"][c, :, 4 + tok0:4 + tok0 + G], o[:, :], reads=[o], writes=[D[f"GQKV{s}"]])
                if dbg_stop <= 3:
                    continue
                for t in range(4):
                    r0 = tok0 + t * 128
                    j = t % 2
                    pa = pB[0]
                    pz = pB[1]
                    for kc in range(8):
                        kb.op("pe", lambda e: e.matmul(pa[:, :TMA], lhsT=xnT[b][:, kc, t * 128:(t + 1) * 128], rhs=W[:, kc, TM0:TM0 + TMA], start=(kc == 0), stop=(kc == 7)),
                              reads=[W, xnT[b]], writes=[pa])
                    for kc in range(8):
                        kb.op("pe", lambda e: e.matmul(pz[:, :], lhsT=xnT[b][:, kc, t * 128:(t + 1) * 128], rhs=W[:, kc, TM0 + TMA:TM0 + TMA + 512], start=(kc == 0), stop=(kc == 7)),
                              reads=[W, xnT[b]], writes=[pz])
                    kb.op("act", lambda e: e.activation(out=tma[j][:, :], in_=pa[:, 0:256], func=AF.Copy), reads=[pa], writes=[tma[j]])
                    kb.dma("sp", D[f"VS{s}"][r0:r0 + 128, :], tma[j][:, 0:128], reads=[tma[j]], writes=[D[f"VS{s}"]])
                    kb.dma("sp", D[f"VW{s}"][r0:r0 + 128, :], tma[j][:, 128:256], reads=[tma[j]], writes=[D[f"VW{s}"]])
                    tg = tmg[j]
                    kb.op("act", lambda e: e.activation(out=tg[:, 0:24], in_=pa[:, 256:280], func=AF.Sigmoid), reads=[pa], writes=[tg])
                    kb.op("act", lambda e: e.activation(out=tg[:, 24:32], in_=pa[:, 288:296], func=AF.Sigmoid), reads=[pa], writes=[tg])
                    kb.op("dve", lambda e: e.tensor_tensor(out=tg[:, 32:40], in0=pa[:, 280:288], in1=dtb[:, :], op=ALU.add), reads=[pa, dtb], writes=[tg])
                    kb.op("act", lambda e: e.activation(out=tg[:, 32:40], in_=tg[:, 32:40], func=AF.Exp), reads=[tg], writes=[tg])
                    kb.op("dve", lambda e: e.tensor_scalar(out=tg[:, 32:40], in0=tg[:, 32:40], scalar1=1.0, scalar2=None, op0=ALU.add), reads=[tg], writes=[tg])
                    kb.op("act", lambda e: e.activation(out=tg[:, 32:40], in_=tg[:, 32:40], func=AF.Ln), reads=[tg], writes=[tg])
                    kb.op("dve", lambda e: e.tensor_tensor(out=tg[:, 32:40], in0=tg[:, 32:40], in1=nea[:, :], op=ALU.mult), reads=[tg, nea], writes=[tg])
                    kb.dma("sp", D[f"GATE{s}"][r0:r0 + 128, :], tg[:, 0:24], reads=[tg], writes=[D[f"GATE{s}"]])
                    kb.dma("sp", D[f"GB{s}"][r0:r0 + 128, :], tg[:, 24:32], reads=[tg], writes=[D[f"GB{s}"]])
                    kb.dma("sp", D[f"GG{s}"][r0:r0 + 128, :], tg[:, 32:40], reads=[tg], writes=[D[f"GG{s}"]])
                    kb.op("act", lambda e: e.activation(out=tmz[j][:, :], in_=pz[:, :], func=AF.Silu), reads=[pz], writes=[tmz[j]])
                    kb.dma("sp", D[f"GZ{s}"][r0:r0 + 128, :], tmz[j][:, :], reads=[tmz[j]], writes=[D[f"GZ{s}"]])
        kb.barrier()


IN_SHAPES = {
    "x": ([NSEQ, SEQ, DM], F32), "mem": ([NSEQ, MEM, DM], F32), "positions": ([NSEQ, SEQ], I32),
    "norm_mix": ([2, DM], F32), "norm_xattn": ([2, DM], F32), "norm_mlp": ([2, DM], F32),
    "hyb_w_in": ([1, DM, IN_COLS], F32), "hyb_cmp_k_pos": ([1, 32, 64], F32), "hyb_cmp_k_w1": ([1, 2048, 128], F32),
    "hyb_cmp_k_w2": ([1, 128, 64], F32), "hyb_cmp_v_pos": ([1, 32, 64], F32), "hyb_cmp_v_w1": ([1, 2048, 128], F32),
    "hyb_cmp_v_w2": ([1, 128, 64], F32), "hyb_gdn_conv": ([1, 4, 1536], F32), "hyb_gdn_a_log": ([1, 8], F32),
    "hyb_gdn_dt_bias": ([1, 8], F32), "hyb_gdn_norm": ([1, 64], F32), "hyb_w_out": ([1, 1024, 1024], F32),
    "sc_w_in": ([1, 1024, 3072], F32), "sc_conv": ([1, 3, 1024], F32), "sc_w_out": ([1, 1024, 1024], F32),
    "mem_norm": ([DM], F32), "xa_wq": ([2, 1024, 512], F32), "xa_wkv": ([2, 1024, 1024], F32), "xa_wo": ([2, 512, 1024], F32),
    "mlp_w1": ([2, 1024, 4096], F32), "mlp_w2": ([2, 4096, 1024], F32), "final_norm": ([DM], F32),
}


def make_consts():
    C = {}
    C["ident"] = np.eye(128, dtype=np.float32).astype(ml_dtypes.bfloat16)
    C["identf"] = np.eye(128, dtype=np.float32)
    inv = (10000.0 ** (-np.arange(0, 64, 2, dtype=np.float32) / 64)).astype(np.float32)
    C["invf"] = np.tile(inv, 4).reshape(128, 1).astype(np.float32)
    C.update(nsa_consts())
    C.update(gdn_consts())
    return C


CONST_DT = {"ident": BF16, "identf": F32, "invf": F32, "winmask": BF16, "cmpmask": BF16, "EE": BF16, "ovaug": BF16, "c01": F32, "addc": F32, "LT": F32, "ones64": F32, "trilI8": F32, "trilS8": F32, "I8": F32, "Bones": BF16}


def declare(kb):
    I = {k: kb.dram(k, shp, dt, "ExternalInput") for k, (shp, dt) in IN_SHAPES.items()}
    C = {k: kb.dram("c_" + k, list(v.shape), CONST_DT[k], "ExternalInput") for k, v in make_consts().items()}
    return I, C


def core_inputs(inputs, core):
    m = {}
    for k in IN_SHAPES:
        v = np.asarray(inputs[k])
        if k in ("x", "mem", "positions"):
            v = v[core * NSEQ:(core + 1) * NSEQ]
        m[k] = np.ascontiguousarray(v)
    for k, v in make_consts().items():
        m["c_" + k] = v
    return m


def phase_memkv(kb, I, C, KT, VV):
    with ExitStack() as st:
        W = kb.sb(st, [128, 8, 1024], BF16, "Wkv")
        gcol = kb.sb(st, [128, 8], F32, "gcol")
        ident = kb.sb(st, [128, 128], BF16, "ident")
        stg = [kb.sb(st, [128, 1024], F32, "stg") for _ in range(2)]
        mt = kb.sb(st, [128, 2, 1024], F32, "mt")
        mn = kb.sb(st, [128, 1024], BF16, "mn")
        mT = kb.sb(st, [128, 8, 256], BF16, "mT")
        junk = kb.sb(st, [128, 1024], F32, "junk")
        ss = kb.sb(st, [128, 2], F32, "ss")
        rs = kb.sb(st, [128, 1], F32, "rs")
        pT = kb.ps(st, [128, 8, 128], BF16, "pT")
        pK = [kb.ps(st, [128, 512], F32, "pK") for _ in range(2)]
        kb.dma("sp", ident[:, :], C["ident"][:, :], reads=[C["ident"]], writes=[ident])
        load_gain_col(kb, gcol, I["mem_norm"], None)
        k = 0
        for l in range(2):
            load_w(kb, W, I["xa_wkv"], I["xa_wkv"][l], 8, 1024, gcol, stg)
            for s in range(NSEQ):
                kb.dma("sp", mt[:, :, :], I["mem"].t[s].rearrange("(t p) d -> p t d", p=128), reads=[I["mem"]], writes=[mt])
                for t in range(2):
                    rms_rstd(kb, mt[:, t, :], mt, junk, ss, rs, 1024)
                    kb.op("dve", lambda e: e.tensor_scalar(out=mn[:, :], in0=mt[:, t, :], scalar1=rs[:, 0:1], scalar2=32.0, op0=ALU.mult, op1=ALU.mult),
                          reads=[mt, rs], writes=[mn])
                    for kc in range(8):
                        kb.op("pe", lambda e: e.transpose(out=pT[:, kc, :], in_=mn[:, kc * 128:(kc + 1) * 128], identity=ident[:, :]), reads=[mn, ident], writes=[pT])
                    kb.op("act", lambda e: e.activation(out=mT[:, :, t * 128:(t + 1) * 128], in_=pT[:, :, :], func=AF.Copy), reads=[pT], writes=[mT])
                for h in range(4):
                    p = pK[k % 2]
                    k += 1
                    for kc in range(8):
                        kb.op("pe", lambda e: e.matmul(p[:, :256], lhsT=W[:, kc, h * 128:(h + 1) * 128], rhs=mT[:, kc, :], start=(kc == 0), stop=(kc == 7)),
                              reads=[W, mT], writes=[p])
                    kb.op("act", lambda e: e.activation(out=KT[l][s][:, h, :], in_=p[:, :256], func=AF.Copy), reads=[p], writes=[KT[l][s]])
                for t in range(2):
                    p = pK[k % 2]
                    k += 1
                    for kc in range(8):
                        kb.op("pe", lambda e: e.matmul(p[:, :], lhsT=mT[:, kc, t * 128:(t + 1) * 128], rhs=W[:, kc, 512:1024], start=(kc == 0), stop=(kc == 7)),
                              reads=[W, mT], writes=[p])
                    kb.op("act", lambda e: e.activation(out=VV[l][s][:, t, :], in_=p[:, :], func=AF.Copy), reads=[p], writes=[VV[l][s]])
        kb.barrier()


def phase_b(kb, I, C, D, layer, x_in_of, x_out, KT, VV, dbg_groups=99):
    G = 512
    NGS = SEQ // G
    with ExitStack() as st:
        ident = kb.sb(st, [128, 128], BF16, "ident")
        ones = kb.sb(st, [128, 128], BF16, "ones")
        kb.dma("sp", ident[:, :], C["ident"][:, :], reads=[C["ident"]], writes=[ident])
        kb.op("dve", lambda e: e.memset(ones[:, :], 1.0), writes=[ones])
        Wout = kb.sb(st, [128, 8, 1024], BF16, "Wout")
        Wq = kb.sb(st, [128, 8, 512], BF16, "Wq")
        Wo = kb.sb(st, [128, 4, 1024], BF16, "Wo")
        gx = kb.sb(st, [128, 8], F32, "gx")
        load_gain_col(kb, gx, I["norm_xattn"], layer)
        if layer == 1:
            Wsc = kb.sb(st, [128, 8, 3072], BF16, "Wsc")
            gm = kb.sb(st, [128, 8], F32, "gm")
            cw = kb.sb(st, [128, 8, 3], F32, "cw")
            load_gain_col(kb, gm, I["norm_mix"], 1)
            for j in range(3):
                kb.dma("sp", cw[:, :, j], I["sc_conv"][0, j].rearrange("(k p) -> p k", p=128), reads=[I["sc_conv"]], writes=[cw], allow_slow_non_contiguous=True)
        with ExitStack() as st2:
            stg = [kb.sb(st2, [128, 1024], F32, "stg") for _ in range(2)]
            if layer == 0:
                load_w(kb, Wout, I["hyb_w_out"], I["hyb_w_out"][0], 8, 1024, None, stg)
            else:
                load_w(kb, Wout, I["sc_w_out"], I["sc_w_out"][0], 8, 1024, None, stg)
                load_w(kb, Wsc, I["sc_w_in"], I["sc_w_in"][0], 8, 3072, gm, stg)
            load_w(kb, Wq, I["xa_wq"], I["xa_wq"][layer], 8, 512, gx, stg, scale=float(128 ** -0.5))
            load_w(kb, Wo, I["xa_wo"], I["xa_wo"][layer], 4, 1024, None, stg)
            kb.barrier()
        xt = [kb.sb(st, [128, 4, 1024], F32, "xt") for _ in range(2)]
        xn = kb.sb(st, [128, 1024], BF16, "xn")
        xnT = kb.sb(st, [128, 8, G], BF16, "xnT")
        junk = kb.sb(st, [128, 1024], F32, "junk")
        ss = [kb.sb(st, [128, 2], F32, "ss") for _ in range(2)]
        rs = [kb.sb(st, [128, 1], F32, "rs") for _ in range(2)]
        qT = kb.sb(st, [128, 4, G], BF16, "qT")
        eT = [kb.sb(st, [128, 2, G], BF16, "eT") for _ in range(2)]
        rden = kb.sb(st, [128, G], F32, "rden")
        oT = kb.sb(st, [128, 4, G], BF16, "oT")
        pT = [kb.ps(st, [128, 8, 128], BF16, "pT") for _ in range(2)]
        pA = [kb.ps(st, [128, 512], F32, "pA") for _ in range(4)]
        pO = [kb.ps(st, [128, 512], F32, "pO") for _ in range(2)]
        if layer == 0:
            mx = [kb.sb(st, [128, 4, 1024], BF16, "mx") for _ in range(2)]
            mT = kb.sb(st, [128, 8, G], BF16, "mT")
        else:
            bT = [kb.sb(st, [128, G], F32, "bT") for _ in range(2)]
            halo = kb.sb(st, [128, 8, 2], F32, "halo")
            cu = [kb.sb(st, [128, G + 2], F32, "cu") for _ in range(2)]
            cT = kb.sb(st, [128, G], F32, "cT")
            acc = [kb.sb(st, [128, G], F32, "acc") for _ in range(2)]
            mT = kb.sb(st, [128, 8, G], BF16, "mT")
        cnt = [0]
        ca = [0]
        co = [0]

        def load_b(g_):
            s_, gi_ = divmod(g_, NGS)
            xin, xtile = x_in_of(s_)
            kb.dma("sp", xt[g_ % 2][:, :, :], xin[gi_ * G:(gi_ + 1) * G, :].rearrange("(t p) d -> p t d", p=128), reads=[xtile], writes=[xt[g_ % 2]])
            if layer == 0:
                kb.dma("sp", mx[g_ % 2][:, :, :], D[f"MIX{s_}"][gi_ * G:(gi_ + 1) * G, :].rearrange("(t p) d -> p t d", p=128), reads=[D[f"MIX{s_}"]], writes=[mx[g_ % 2]])

        def norm_T(b, dstT):
            for t in range(4):
                i = cnt[0] % 2
                cnt[0] += 1
                rms_rstd(kb, xt[b][:, t, :], xt[b], junk, ss[i], rs[i], 1024)
                kb.op("dve", lambda e: e.tensor_scalar(out=xn[:, :], in0=xt[b][:, t, :], scalar1=rs[i][:, 0:1], scalar2=32.0, op0=ALU.mult, op1=ALU.mult),
                      reads=[xt[b], rs[i]], writes=[xn])
                p = pT[i]
                for kc in range(8):
                    kb.op("pe", lambda e: e.transpose(out=p[:, kc, :], in_=xn[:, kc * 128:(kc + 1) * 128], identity=ident[:, :]), reads=[xn, ident], writes=[p])
                kb.op("act", lambda e: e.activation(out=dstT[:, :, t * 128:(t + 1) * 128], in_=p[:, :, :], func=AF.Copy), reads=[p], writes=[dstT])

        def proj_add(b, lT, nk, Wt):
            for t in range(4):
                for half in range(2):
                    po = pO[co[0] % 2]
                    co[0] += 1
                    for kc in range(nk):
                        kb.op("pe", lambda e: e.matmul(po[:, :], lhsT=lT[:, kc, t * 128:(t + 1) * 128], rhs=Wt[:, kc, half * 512:(half + 1) * 512], start=(kc == 0), stop=(kc == nk - 1)),
                              reads=[lT, Wt], writes=[po])
                    xs = xt[b][:, t, half * 512:(half + 1) * 512]
                    kb.op("dve", lambda e: e.tensor_tensor(out=xs, in0=xs, in1=po[:, :], op=ALU.add), reads=[po, xt[b]], writes=[xt[b]])

        ng = min(dbg_groups, NSEQ * NGS)
        for g in range(ng):
            s, gi = divmod(g, NGS)
            b = g % 2
            if g == 0:
                load_b(0)
            if g + 1 < ng:
                load_b(g + 1)
            if layer == 0:
                for t in range(4):
                    p = pT[t % 2]
                    for kc in range(8):
                        kb.op("pe", lambda e: e.transpose(out=p[:, kc, :], in_=mx[b][:, t, kc * 128:(kc + 1) * 128], identity=ident[:, :]), reads=[mx[b], ident], writes=[p])
                    kb.op("act", lambda e: e.activation(out=mT[:, :, t * 128:(t + 1) * 128], in_=p[:, :, :], func=AF.Copy), reads=[p], writes=[mT])
            else:
                norm_T(b, xnT)
                if gi == 0:
                    kb.op("dve", lambda e: e.memset(halo[:, :, :], 0.0), writes=[halo])
                for c in range(8):
                    pb = pA[ca[0] % 4]
                    pc = pA[(ca[0] + 1) % 4]
                    pu = pA[(ca[0] + 2) % 4]
                    ca[0] += 3
                    cc = cu[c % 2]
                    ac = acc[c % 2]
                    bb = bT[c % 2]
                    for (pp, off) in [(pb, 0), (pc, 1024), (pu, 2048)]:
                        for kc in range(8):
                            kb.op("pe", lambda e: e.matmul(pp[:, :], lhsT=Wsc[:, kc, off + c * 128:off + (c + 1) * 128], rhs=xnT[:, kc, :], start=(kc == 0), stop=(kc == 7)),
                                  reads=[Wsc, xnT], writes=[pp])
                    kb.op("act", lambda e: e.activation(out=bb[:, :], in_=pb[:, :], func=AF.Copy), reads=[pb], writes=[bb])
                    kb.op("act", lambda e: e.activation(out=cT[:, :], in_=pc[:, :], func=AF.Copy), reads=[pc], writes=[cT])
                    kb.op("dve", lambda e: e.tensor_copy(out=cc[:, 0:2], in_=halo[:, c, :]), reads=[halo], writes=[cc])
                    kb.op("dve", lambda e: e.tensor_tensor(out=cc[:, 2:G + 2], in0=cT[:, :], in1=pu[:, :], op=ALU.mult), reads=[cT, pu], writes=[cc])
                    kb.op("dve", lambda e: e.tensor_copy(out=halo[:, c, :], in_=cc[:, G:G + 2]), reads=[cc], writes=[halo])
                    kb.op("dve", lambda e: e.tensor_scalar(out=ac[:, :], in0=cc[:, 0:G], scalar1=cw[:, c, 0:1], scalar2=None, op0=ALU.mult), reads=[cc, cw], writes=[ac])
                    kb.op("dve", lambda e: e.scalar_tensor_tensor(out=ac[:, :], in0=cc[:, 1:G + 1], scalar=cw[:, c, 1:2], in1=ac[:, :], op0=ALU.mult, op1=ALU.add),
                          reads=[cc, cw, ac], writes=[ac])
                    kb.op("dve", lambda e: e.scalar_tensor_tensor(out=ac[:, :], in0=cc[:, 2:G + 2], scalar=cw[:, c, 2:3], in1=ac[:, :], op0=ALU.mult, op1=ALU.add),
                          reads=[cc, cw, ac], writes=[ac])
                    kb.op("pool", lambda e: e.tensor_tensor(out=mT[:, c, :], in0=ac[:, :], in1=bb[:, :], op=ALU.mult), reads=[ac, bb], writes=[mT])
            proj_add(b, mT, 8, Wout)
            norm_T(b, xnT)
            for h in range(4):
                p = pA[ca[0] % 4]
                ca[0] += 1
                for kc in range(8):
                    kb.op("pe", lambda e: e.matmul(p[:, :], lhsT=Wq[:, kc, h * 128:(h + 1) * 128], rhs=xnT[:, kc, :], start=(kc == 0), stop=(kc == 7)), reads=[Wq, xnT], writes=[p])
                kb.op("act", lambda e: e.activation(out=qT[:, h, :], in_=p[:, :], func=AF.Copy), reads=[p], writes=[qT])
            for h in range(4):
                e_ = eT[h % 2]
                for mc in range(2):
                    p = pA[ca[0] % 4]
                    ca[0] += 1
                    kb.op("pe", lambda e: e.matmul(p[:, :], lhsT=KT[layer][s][:, h, mc * 128:(mc + 1) * 128], rhs=qT[:, h, :], start=True, stop=True), reads=[KT[layer][s], qT], writes=[p])
                    kb.op("act", lambda e: e.activation(out=e_[:, mc, :], in_=p[:, :], func=AF.Exp), reads=[p], writes=[e_])
                pd = pA[ca[0] % 4]
                po = pA[(ca[0] + 1) % 4]
                ca[0] += 2
                for mc in range(2):
                    kb.op("pe", lambda e: e.matmul(pd[:, :], lhsT=ones[:, :], rhs=e_[:, mc, :], start=(mc == 0), stop=(mc == 1)), reads=[ones, e_], writes=[pd])
                for mc in range(2):
                    kb.op("pe", lambda e: e.matmul(po[:, :], lhsT=VV[layer][s][:, mc, h * 128:(h + 1) * 128], rhs=e_[:, mc, :], start=(mc == 0), stop=(mc == 1)), reads=[VV[layer][s], e_], writes=[po])
                kb.op("dve", lambda e: e.reciprocal(out=rden[:, :], in_=pd[:, :]), reads=[pd], writes=[rden])
                kb.op("dve", lambda e: e.tensor_tensor(out=oT[:, h, :], in0=po[:, :], in1=rden[:, :], op=ALU.mult), reads=[po, rden], writes=[oT])
            proj_add(b, oT, 4, Wo)
            kb.dma("sp", x_out.t[s * SEQ + gi * G:s * SEQ + (gi + 1) * G, :].rearrange("(t p) d -> p t d", p=128), xt[b][:, :, :], reads=[xt[b]], writes=[x_out])
        kb.barrier()


NEGB = -30000.0


def nsa_consts():
    C = {}
    k = np.arange(128)[:, None]
    q = np.arange(512)[None, :]
    wm = np.zeros((8, 128, 512), np.float32)
    for r in range(8):
        d = q - k + 512 - 128 * r
        wm[r] = np.where((d >= 0) & (d <= 511), 0.0, NEGB)
    C["winmask"] = wm.astype(ml_dtypes.bfloat16)
    cm = np.zeros((4, 128, 512), np.float32)
    for g in range(4):
        cm[g] = np.where(q - 16 * k >= 31 - 512 * g, 0.0, NEGB)
    C["cmpmask"] = cm.astype(ml_dtypes.bfloat16)
    C["EE"] = (np.arange(4096)[None, :] // 64 == np.arange(64)[:, None]).astype(np.float32).astype(ml_dtypes.bfloat16)
    cs = np.arange(255)[:, None] * 16
    js = np.arange(64)[None, :] * 64
    ov = np.clip(np.minimum(cs + 32, js + 64) - np.maximum(cs, js), 0, None) / 32.0
    oa = np.zeros((2, 128, 65), np.float32)
    oa[0, :, :64] = ov[:128]
    oa[1, :127, :64] = ov[128:]
    oa[:, :, 64] = 1.0
    C["ovaug"] = oa.astype(ml_dtypes.bfloat16)
    t = (np.arange(32)[:, None, None] * 128 + np.arange(128)[None, :, None])
    j = np.arange(64)[None, None, :]
    cur = t // 64
    forced = (j == 0) | (j == cur) | (j == cur - 1)
    causal = (j * 64 <= t)
    c01 = causal.astype(np.float32)
    addc = np.where(causal, np.where(forced, 1000.0, 0.0), -1.0).astype(np.float32)
    C["c01"] = np.ascontiguousarray(c01.transpose(1, 0, 2))
    C["addc"] = np.ascontiguousarray(addc.transpose(1, 0, 2))
    return C


def phase_nsa(kb, I, C, D, dbg_sg=99, dbg_G=99):
    with ExitStack() as st:
        identb = kb.sb(st, [128, 128], BF16, "identb")
        winm = kb.sb(st, [128, 8, 512], BF16, "winm")
        cmpm = kb.sb(st, [128, 4, 512], BF16, "cmpm")
        EE = kb.sb(st, [64, 4096], BF16, "EE")
        ovaug = kb.sb(st, [128, 2, 65], BF16, "ovaug")
        c01 = kb.sb(st, [128, 32, 64], F32, "c01")
        addc = kb.sb(st, [128, 32, 64], F32, "addc")
        kb.dma("sp", identb[:, :], C["ident"][:, :], reads=[C["ident"]], writes=[identb])
        kb.dma("sp", winm[:, :, :], C["winmask"].t.rearrange("r p q -> p r q"), reads=[C["winmask"]], writes=[winm])
        kb.dma("sp", cmpm[:, :, :], C["cmpmask"].t.rearrange("r p q -> p r q"), reads=[C["cmpmask"]], writes=[cmpm])
        kb.dma("sp", EE[:, :], C["EE"][:, :], reads=[C["EE"]], writes=[EE])
        kb.dma("sp", ovaug[:, :, :], C["ovaug"].t.rearrange("c p j -> p c j"), reads=[C["ovaug"]], writes=[ovaug])
        kb.dma("sp", c01[:, :, :], C["c01"][:, :, :], reads=[C["c01"]], writes=[c01])
        kb.dma("sp", addc[:, :, :], C["addc"][:, :, :], reads=[C["addc"]], writes=[addc])
        W1 = [kb.sb(st, [64, 32, 128], BF16, "cW1") for _ in range(2)]
        W2 = [kb.sb(st, [128, 64], BF16, "cW2") for _ in range(2)]
        posT = [kb.sb(st, [64, 32], BF16, "cpos") for _ in range(2)]
        with ExitStack() as st2:
            sg1 = kb.sb(st2, [64, 32, 128], F32, "sg1")
            sg2 = kb.sb(st2, [128, 64], F32, "sg2")
            sg3 = kb.sb(st2, [64, 32], F32, "sg3")
            for i, nm in enumerate(["k", "v"]):
                w1 = I[f"hyb_cmp_{nm}_w1"]
                kb.dma("sp", sg1[:, :, :], w1[0].rearrange("(l d) h -> d l h", d=64), reads=[w1], writes=[sg1])
                kb.op("dve", lambda e: e.tensor_copy(out=W1[i][:, :, :], in_=sg1[:, :, :]), reads=[sg1], writes=[W1[i]])
                w2 = I[f"hyb_cmp_{nm}_w2"]
                kb.dma("sp", sg2[:, :], w2[0], reads=[w2], writes=[sg2])
                kb.op("dve", lambda e: e.tensor_copy(out=W2[i][:, :], in_=sg2[:, :]), reads=[sg2], writes=[W2[i]])
                pp = I[f"hyb_cmp_{nm}_pos"]
                kb.dma("sp", sg3[:, :], pp[0].rearrange("l d -> d l"), reads=[pp], writes=[sg3], allow_slow_non_contiguous=True)
                kb.op("dve", lambda e: e.tensor_copy(out=posT[i][:, :], in_=sg3[:, :]), reads=[sg3], writes=[posT[i]])
            kb.barrier()
        kct = kb.sb(st, [64, SEQ], BF16, "kct")
        kst = kb.sb(st, [64, SEQ], BF16, "kst")
        kwt = kb.sb(st, [64, SEQ], BF16, "kwt")
        vsa = kb.sb(st, [128, 32, 65], BF16, "vsa")
        vwa = kb.sb(st, [128, 32, 65], BF16, "vwa")
        vca = kb.sb(st, [128, 2, 65], BF16, "vca")
        kcT = kb.sb(st, [64, 256], BF16, "kcT")
        hsil = kb.sb(st, [128, 256], BF16, "hsil")
        posc = kb.sb(st, [128, 1], F32, "posc")
        qTh = [kb.sb(st, [64, 512], BF16, "qTh") for _ in range(4)]
        gat = kb.sb(st, [128, 4, 24], F32, "gat")
        eT = [kb.sb(st, [128, 512], BF16, "eT") for _ in range(3)]
        ocmp = kb.sb(st, [128, 4, 4, 65], F32, "ocmp")
        eall = [kb.sb(st, [128, 8, 512], BF16, "eall") for _ in range(4)]
        ewin = kb.sb(st, [128, 8, 512], BF16, "ewin")
        imp = kb.sb(st, [128, 4, 64], F32, "imp")
        imp2 = kb.sb(st, [128, 64], F32, "imp2")
        m8 = kb.sb(st, [128, 16], F32, "m8")
        rd = kb.sb(st, [128, 8], F32, "rd")
        nsel = kb.sb(st, [128, 64], BF16, "nsel")
        nselT = kb.sb(st, [64, 512], BF16, "nselT")
        oacc = kb.sb(st, [128, 64], F32, "oacc")
        onsa = [kb.sb(st, [128, 4, 256], BF16, "onsa") for _ in range(2)]
        pS = [kb.ps(st, [128, 512], F32, "pS") for _ in range(2)]
        pOs = kb.ps(st, [128, 4, 128], F32, "pOs")
        pOw = kb.ps(st, [128, 4, 128], F32, "pOw")
        pOc = kb.ps(st, [128, 4, 128], F32, "pOc")
        pU = kb.ps(st, [128, 4, 128], F32, "pU")
        pTr = kb.ps(st, [64, 512], BF16, "pTr")
        cs = [0]
        ce = [0]
        cg = 0
        for s in range(NSEQ):
            for g in range(2):
                sg = s * 2 + g
                if sg >= dbg_sg:
                    continue
                r64 = slice(g * 64, (g + 1) * 64)
                kb.dma("sp", kst[:, :], D[f"KST{s}"][r64, :], reads=[D[f"KST{s}"]], writes=[kst])
                kb.dma("sp", kwt[:, :], D[f"KWT{s}"][r64, :], reads=[D[f"KWT{s}"]], writes=[kwt])
                kb.op("dve", lambda e: e.memset(vsa[:, :, 64:65], 1.0), writes=[vsa])
                kb.op("dve", lambda e: e.memset(vwa[:, :, 64:65], 1.0), writes=[vwa])
                kb.op("dve", lambda e: e.memset(vca[:, :, :], 0.0), writes=[vca])
                kb.op("dve", lambda e: e.memset(vca[:, :, 64:65], 1.0), writes=[vca])
                kb.dma("sp", vsa[:, :, 0:64], D[f"VS{s}"][:, r64].rearrange("(t p) d -> p t d", p=128), reads=[D[f"VS{s}"]], writes=[vsa])
                kb.dma("sp", vwa[:, :, 0:64], D[f"VW{s}"][:, r64].rearrange("(t p) d -> p t d", p=128), reads=[D[f"VW{s}"]], writes=[vwa])
                for i, nm in enumerate(["KCT", "VCT"]):
                    kb.dma("sp", kct[:, :], D[f"{nm}{s}"][r64, :], reads=[D[f"{nm}{s}"]], writes=[kct])
                    ph = pS[cs[0] % 2]
                    cs[0] += 1
                    pp_ = pS[cs[0] % 2]
                    cs[0] += 1
                    for l in range(32):
                        kb.op("pe", lambda e: e.matmul(ph[:, 0:255], lhsT=W1[i][:, l, :], rhs=kct[:, l:l + 16 * 254 + 1:16], start=(l == 0), stop=(l == 31)),
                              reads=[W1[i], kct], writes=[ph])
                    for l in range(32):
                        kb.op("pe", lambda e: e.matmul(pp_[:, 0:1], lhsT=W1[i][:, l, :], rhs=posT[i][:, l:l + 1], start=(l == 0), stop=(l == 31)),
                              reads=[W1[i], posT[i]], writes=[pp_])
                    kb.op("dve", lambda e: e.tensor_copy(out=posc[:, :], in_=pp_[:, 0:1]), reads=[pp_], writes=[posc])
                    kb.op("act", lambda e: e.activation(out=hsil[:, 0:255], in_=ph[:, 0:255], func=AF.Silu, bias=posc[:, 0:1]), reads=[ph, posc], writes=[hsil])
                    if i == 0:
                        p2 = pS[cs[0] % 2]
                        cs[0] += 1
                        kb.op("pe", lambda e: e.matmul(p2[0:64, 0:255], lhsT=W2[0][:, :], rhs=hsil[:, 0:255], start=True, stop=True), reads=[W2[0], hsil], writes=[p2])
                        kb.op("act", lambda e: e.activation(out=kcT[:, 0:255], in_=p2[0:64, 0:255], func=AF.Copy), reads=[p2], writes=[kcT])
                    else:
                        for c, nn in [(0, 128), (1, 127)]:
                            p2 = pS[cs[0] % 2]
                            cs[0] += 1
                            kb.op("pe", lambda e: e.matmul(p2[0:nn, 0:64], lhsT=hsil[:, c * 128:c * 128 + nn], rhs=W2[1][:, :], start=True, stop=True), reads=[W2[1], hsil], writes=[p2])
                            kb.op("act", lambda e: e.activation(out=vca[0:nn, c, 0:64], in_=p2[0:nn, 0:64], func=AF.Copy), reads=[p2], writes=[vca])
                for G in range(8):
                    if G >= dbg_G:
                        continue
                    q0 = G * 512
                    on = onsa[cg % 2]
                    cg += 1
                    for h in range(4):
                        hh = g * 4 + h
                        kb.dma("sp", qTh[h][:, :], D[f"QT{s}"][hh // 2, (hh % 2) * 64:(hh % 2) * 64 + 64, q0:q0 + 512], reads=[D[f"QT{s}"]], writes=[qTh[h]])
                    kb.dma("sp", gat[:, :, :], D[f"GATE{s}"][q0:q0 + 512, :].rearrange("(t p) c -> p t c", p=128), reads=[D[f"GATE{s}"]], writes=[gat])
                    chunks = [(0, 128, G)] if G < 4 else [(0, 128, None), (1, 127, G - 4)]
                    for h in range(4):
                        ets = []
                        for (c, nn, mi) in chunks:
                            p = pS[cs[0] % 2]
                            cs[0] += 1
                            kb.op("pe", lambda e: e.matmul(p[0:nn, :], lhsT=kcT[:, c * 128:c * 128 + nn], rhs=qTh[h][:, :], start=True, stop=(mi is None)), reads=[kcT, qTh[h]], writes=[p])
                            if mi is not None:
                                kb.op("pe", lambda e: e.matmul(p[0:nn, :], lhsT=identb[0:nn, 0:nn], rhs=cmpm[0:nn, mi, :], start=False, stop=True), reads=[identb, cmpm], writes=[p])
                            e_ = eT[ce[0] % 3]
                            ce[0] += 1
                            kb.op("act", lambda e: e.activation(out=e_[0:nn, :], in_=p[0:nn, :], func=AF.Exp), reads=[p], writes=[e_])
                            ets.append((e_, c, nn))
                        for qt in range(4):
                            for k_, (e_, c, nn) in enumerate(ets):
                                kb.op("pe", lambda e: e.matmul(pOc[:, qt, 0:65], lhsT=e_[0:nn, qt * 128:(qt + 1) * 128], rhs=vca[0:nn, c, :], start=(k_ == 0), stop=(k_ == len(ets) - 1)),
                                      reads=[e_, vca], writes=[pOc])
                            for k_, (e_, c, nn) in enumerate(ets):
                                kb.op("pe", lambda e: e.matmul(pU[:, qt, 0:65], lhsT=e_[0:nn, qt * 128:(qt + 1) * 128], rhs=ovaug[0:nn, c, :], start=(k_ == 0), stop=(k_ == len(ets) - 1)),
                                      reads=[e_, ovaug], writes=[pU])
                        kb.op("act", lambda e: e.activation(out=ocmp[:, h, :, :], in_=pOc[:, :, 0:65], func=AF.Copy), reads=[pOc], writes=[ocmp])
                        kb.op("dve", lambda e: e.tensor_scalar(out=rd[:, 0:4], in0=pU[:, :, 64], scalar1=1e-30, scalar2=None, op0=ALU.max), reads=[pU], writes=[rd])
                        kb.op("dve", lambda e: e.reciprocal(out=rd[:, 4:8], in_=rd[:, 0:4]), reads=[rd], writes=[rd])
                        for qt in range(4):
                            if h == 0:
                                kb.op("dve", lambda e: e.tensor_scalar(out=imp[:, qt, :], in0=pU[:, qt, 0:64], scalar1=rd[:, 4 + qt:5 + qt], scalar2=None, op0=ALU.mult),
                                      reads=[pU, rd], writes=[imp])
                            else:
                                kb.op("dve", lambda e: e.scalar_tensor_tensor(out=imp[:, qt, :], in0=pU[:, qt, 0:64], scalar=rd[:, 4 + qt:5 + qt], in1=imp[:, qt, :], op0=ALU.mult, op1=ALU.add),
                                      reads=[pU, rd, imp], writes=[imp])
                    for qt in range(4):
                        qg = G * 4 + qt
                        kb.op("dve", lambda e: e.tensor_tensor(out=imp[:, qt, :], in0=imp[:, qt, :], in1=c01[:, qg, :], op=ALU.mult), reads=[imp, c01], writes=[imp])
                        kb.op("dve", lambda e: e.tensor_tensor(out=imp[:, qt, :], in0=imp[:, qt, :], in1=addc[:, qg, :], op=ALU.add), reads=[imp, addc], writes=[imp])
                        kb.op("dve", lambda e: e.max(out=m8[:, 0:8], in_=imp[:, qt, :]), reads=[imp], writes=[m8])
                        kb.op("dve", lambda e: e.match_replace(out=imp2[:, :], in_to_replace=m8[:, 0:8], in_values=imp[:, qt, :], imm_value=-1e9), reads=[imp, m8], writes=[imp2])
                        kb.op("dve", lambda e: e.max(out=m8[:, 8:16], in_=imp2[:, :]), reads=[imp2], writes=[m8])
                        kb.op("dve", lambda e: e.tensor_scalar(out=imp2[:, :], in0=imp[:, qt, :], scalar1=m8[:, 15:16], scalar2=-NEGB, op0=ALU.is_ge, op1=ALU.mult), reads=[imp, m8], writes=[imp2])
                        kb.op("dve", lambda e: e.tensor_scalar(out=nsel[:, :], in0=imp2[:, :], scalar1=NEGB, scalar2=None, op0=ALU.add), reads=[imp2], writes=[nsel])
                        kb.op("pe", lambda e: e.transpose(out=pTr[:, qt * 128:(qt + 1) * 128], in_=nsel[:, :], identity=identb[:, :]), reads=[nsel, identb], writes=[pTr])
                    kb.op("act", lambda e: e.activation(out=nselT[:, :], in_=pTr[:, :], func=AF.Copy), reads=[pTr], writes=[nselT])
                    for h in range(4):
                        nk = 4 * G + 4
                        for kt in range(nk):
                            r = kt - 4 * G
                            qlo = max(r, 0)
                            qs = slice(qlo * 128, 512)
                            p = pS[cs[0] % 2]
                            cs[0] += 1
                            kb.op("pe", lambda e: e.matmul(p[:, qs], lhsT=kst[:, kt * 128:(kt + 1) * 128], rhs=qTh[h][:, qs], start=True, stop=False), reads=[kst, qTh[h]], writes=[p])
                            kb.op("pe", lambda e: e.matmul(p[:, qs], lhsT=EE[:, kt * 128:(kt + 1) * 128], rhs=nselT[:, qs], start=False, stop=(r < 0)), reads=[EE, nselT], writes=[p])
                            if r >= 0:
                                kb.op("pe", lambda e: e.matmul(p[:, qs], lhsT=identb[:, :], rhs=winm[:, r + 4, qs], start=False, stop=True), reads=[identb, winm], writes=[p])
                            kb.op("act", lambda e: e.activation(out=eall[kt // 8][:, kt % 8, qs], in_=p[:, qs], func=AF.Exp), reads=[p], writes=[eall[kt // 8]])
                        for qt in range(4):
                            last = 4 * G + qt
                            for kt in range(last + 1):
                                kb.op("pe", lambda e: e.matmul(pOs[:, qt, 0:65], lhsT=eall[kt // 8][:, kt % 8, qt * 128:(qt + 1) * 128], rhs=vsa[:, kt, :], start=(kt == 0), stop=(kt == last)),
                                      reads=[eall[kt // 8], vsa], writes=[pOs])
                        for r in range(8):
                            kt = 4 * G - 4 + r
                            if kt < 0:
                                continue
                            lo, hi = max(0, r - 4), min(3, r)
                            qs = slice(lo * 128, (hi + 1) * 128)
                            p = pS[cs[0] % 2]
                            cs[0] += 1
                            kb.op("pe", lambda e: e.matmul(p[:, qs], lhsT=kwt[:, kt * 128:(kt + 1) * 128], rhs=qTh[h][:, qs], start=True, stop=False), reads=[kwt, qTh[h]], writes=[p])
                            kb.op("pe", lambda e: e.matmul(p[:, qs], lhsT=identb[:, :], rhs=winm[:, r, qs], start=False, stop=True), reads=[identb, winm], writes=[p])
                            kb.op("act", lambda e: e.activation(out=ewin[:, r, qs], in_=p[:, qs], func=AF.Exp), reads=[p], writes=[ewin])
                        for qt in range(4):
                            rr_ = [r for r in range(qt, qt + 5) if 4 * G - 4 + r >= 0]
                            for r in rr_:
                                kt = 4 * G - 4 + r
                                kb.op("pe", lambda e: e.matmul(pOw[:, qt, 0:65], lhsT=ewin[:, r, qt * 128:(qt + 1) * 128], rhs=vwa[:, kt, :], start=(r == rr_[0]), stop=(r == rr_[-1])),
                                      reads=[ewin, vwa], writes=[pOw])
                        for qt in range(4):
                            kb.op("dve", lambda e: e.tensor_copy(out=rd[:, 0:1], in_=ocmp[:, h, qt, 64:65]), reads=[ocmp], writes=[rd])
                            kb.op("dve", lambda e: e.tensor_copy(out=rd[:, 1:2], in_=pOs[:, qt, 64:65]), reads=[pOs], writes=[rd])
                            kb.op("dve", lambda e: e.tensor_copy(out=rd[:, 2:3], in_=pOw[:, qt, 64:65]), reads=[pOw], writes=[rd])
                            kb.op("dve", lambda e: e.tensor_scalar(out=rd[:, 0:3], in0=rd[:, 0:3], scalar1=1e-30, scalar2=None, op0=ALU.max), reads=[rd], writes=[rd])
                            kb.op("dve", lambda e: e.reciprocal(out=rd[:, 4:7], in_=rd[:, 0:3]), reads=[rd], writes=[rd])
                            kb.op("dve", lambda e: e.tensor_tensor(out=rd[:, 4:7], in0=rd[:, 4:7], in1=gat[:, qt, h * 3 + g * 12:h * 3 + g * 12 + 3], op=ALU.mult), reads=[rd, gat], writes=[rd])
                            kb.op("dve", lambda e: e.tensor_scalar(out=oacc[:, :], in0=ocmp[:, h, qt, 0:64], scalar1=rd[:, 4:5], scalar2=None, op0=ALU.mult), reads=[ocmp, rd], writes=[oacc])
                            kb.op("dve", lambda e: e.scalar_tensor_tensor(out=oacc[:, :], in0=pOs[:, qt, 0:64], scalar=rd[:, 5:6], in1=oacc[:, :], op0=ALU.mult, op1=ALU.add), reads=[pOs, rd, oacc], writes=[oacc])
                            kb.op("dve", lambda e: e.scalar_tensor_tensor(out=on[:, qt, h * 64:(h + 1) * 64], in0=pOw[:, qt, 0:64], scalar=rd[:, 6:7], in1=oacc[:, :], op0=ALU.mult, op1=ALU.add), reads=[pOw, rd, oacc], writes=[on])
                    kb.dma("sp", D[f"MIX{s}"][q0:q0 + 512, g * 256:(g + 1) * 256].rearrange("(t p) c -> p t c", p=128), on[:, :, :], reads=[on], writes=[D[f"MIX{s}"]])
        kb.barrier()


def phase_zero_gdn(kb, D):
    with ExitStack() as st:
        z = kb.sb(st, [128, 512], BF16, "z")
        kb.op("dve", lambda e: e.memset(z[:, :], 0.0), writes=[z])
        for s in range(NSEQ):
            for t in range(SEQ // 128):
                kb.dma("sp", D[f"MIX{s}"][t * 128:(t + 1) * 128, 512:1024], z[:, :], reads=[z], writes=[D[f"MIX{s}"]])
        kb.barrier()


def build_all(dbg=False):
    kb = KB()
    I, C = declare(kb)
    D = make_scratch(kb, dbg=dbg)
    kind = "ExternalOutput" if dbg else "Internal"
    xa = kb.dram("xa", [NTOK, DM], F32, kind)
    xb = kb.dram("xb", [NTOK, DM], F32, kind)
    xc = kb.dram("xc", [NTOK, DM], F32, kind)
    out = kb.dram("out", [NTOK, DM], F32, "ExternalOutput")
    with ExitStack() as top:
        KT = [[kb.sb(top, [128, 4, 256], BF16, "KT") for s in range(NSEQ)] for l in range(2)]
        VV = [[kb.sb(top, [128, 2, 512], BF16, "VV") for s in range(NSEQ)] for l in range(2)]
        phase_memkv(kb, I, C, KT, VV)
        phase_a(kb, I, D, C)
        phase_nsa(kb, I, C, D)
        phase_gdn(kb, I, C, D)
        phase_b(kb, I, C, D, 0, lambda s: (I["x"].t[s], I["x"]), xa, KT, VV)
        phase_mlp(kb, xa, xb, I["mlp_w1"], I["mlp_w2"], I["norm_mlp"], 0, C["ident"])
        phase_b(kb, I, C, D, 1, lambda s: (xb.t[s * SEQ:(s + 1) * SEQ], xb), xc, KT, VV)
        phase_mlp(kb, xc, out, I["mlp_w1"], I["mlp_w2"], I["norm_mlp"], 1, C["ident"], final_gain=I["final_norm"])
        kb.finish()
    return kb


def kernel(**inputs):
    kb = build_all()
    in_maps = [core_inputs(inputs, c) for c in range(NCORES)]
    res = run_bass_kernel_spmd(kb.nc, in_maps, core_ids=list(range(NCORES)))
    outs = [np.asarray(r["out"], dtype=np.float32).reshape(NSEQ, SEQ, DM) for r in res.results]
    return np.concatenate(outs, axis=0)


def gdn_consts():
    C = {}
    i = np.arange(64)
    lt = (i[:, None] <= i[None, :]).astype(np.float32)
    C["LT"] = lt
    C["ones64"] = np.ones((64, 128), np.float32)
    tri = (i[None, :] <= i[:, None]).astype(np.float32)
    tris = (i[None, :] < i[:, None]).astype(np.float32)
    C["trilI8"] = np.tile(tri[:, None, :], (1, 8, 1)).reshape(64, 512)
    C["trilS8"] = np.tile(tris[:, None, :], (1, 8, 1)).reshape(64, 512)
    C["I8"] = np.tile(np.eye(64, dtype=np.float32)[:, None, :], (1, 8, 1)).reshape(64, 512)
    bo = np.zeros((128, 128), np.float32)
    bo[:64, :64] = 1
    bo[64:, 64:] = 1
    C["Bones"] = bo.astype(ml_dtypes.bfloat16)
    return C


def phase_gdn(kb, I, C, D, dbg_seq=99, dbg_groups=99, dbg_stop=99):
    G = 512
    with ExitStack() as st:
        identb = kb.sb(st, [128, 128], BF16, "identb")
        LT = kb.sb(st, [64, 64], F32, "LT")
        ones64 = kb.sb(st, [64, 128], F32, "ones64")
        trilI = kb.sb(st, [64, 512], F32, "trilI")
        trilS = kb.sb(st, [64, 512], F32, "trilS")
        I8 = kb.sb(st, [64, 512], F32, "I8")
        Bones = kb.sb(st, [128, 128], BF16, "Bones")
        nw8 = kb.sb(st, [64, 8, 64], F32, "nw8")
        cwg = kb.sb(st, [128, 12, 4], F32, "cwg")
        for (t_, nm) in [(identb, "ident"), (LT, "LT"), (ones64, "ones64"), (trilI, "trilI8"), (trilS, "trilS8"), (I8, "I8"), (Bones, "Bones")]:
            kb.dma("sp", t_[:, :], C[nm][:, :], reads=[C[nm]], writes=[t_])
        for h in range(8):
            kb.dma("sp", nw8[:, h, :], I["hyb_gdn_norm"][0].partition_broadcast(64), reads=[I["hyb_gdn_norm"]], writes=[nw8])
        for j in range(4):
            kb.dma("sp", cwg[:, :, j], I["hyb_gdn_conv"][0, j].rearrange("(k p) -> p k", p=128), reads=[I["hyb_gdn_conv"]], writes=[cwg], allow_slow_non_contiguous=True)
        pre = kb.sb(st, [128, 12, G + 4], F32, "pre")
        acc = [kb.sb(st, [128, G], F32, "acc") for _ in range(2)]
        sl = [kb.sb(st, [128, G], F32, "sl") for _ in range(2)]
        sqb = [kb.sb(st, [128, G], BF16, "sqb") for _ in range(2)]
        rn = [kb.sb(st, [128, G], F32, "rn") for _ in range(2)]
        qkn = kb.sb(st, [128, 8, G], BF16, "qkn")
        vn = kb.sb(st, [128, 4, G], BF16, "vn")
        qodd = kb.sb(st, [64, 8, G], BF16, "qodd")
        gtl = kb.sb(st, [64, 8, 8], F32, "gtl")
        btl = kb.sb(st, [64, 8, 8], F32, "btl")
        gzl = kb.sb(st, [64, 8, 512], BF16, "gzl")
        def two(shape, dt, nm):
            return [kb.sb(st, shape, dt, nm) for _ in range(2)]
        PARL = [two([64, 2, 512], BF16, "kvtm"), two([64, 8], F32, "gcs"), two([64, 8], F32, "ngcs"), two([64, 8], F32, "eg"), two([64, 8], F32, "egl"),
                two([64, 8], F32, "eglast"), two([64, 512], F32, "gU"), two([64, 512], F32, "arg"), two([64, 512], F32, "Dm"), two([64, 512], F32, "Ds"),
                two([64, 512], BF16, "Dib"), two([64, 512], BF16, "DiT"), two([64, 512], F32, "A32"), two([64, 512], F32, "PT32"), two([64, 512], BF16, "PTb"),
                two([64, 512], BF16, "Vb"), two([64, 512], BF16, "Kbg"), two([64, 512], BF16, "ktil"), two([64, 512], F32, "u32"), two([64, 512], BF16, "wTb"),
                two([64, 512], BF16, "qkT")]
        Ypar = [[kb.sb(st, [64, 512], BF16, "Y") for _ in range(2)] for _ in range(2)]
        YTpar = [[kb.sb(st, [64, 512], BF16, "YT") for _ in range(2)] for _ in range(2)]
        vnew = kb.sb(st, [64, 512], BF16, "vnew")
        S32 = kb.sb(st, [64, 512], F32, "S32")
        Sb = kb.sb(st, [64, 512], BF16, "Sb")
        o2 = kb.sb(st, [64, 512], F32, "o2")
        o32 = kb.sb(st, [64, 512], F32, "o32")
        osq = kb.sb(st, [64, 512], F32, "osq")
        oss = kb.sb(st, [64, 8], F32, "oss")
        ors = kb.sb(st, [64, 8], F32, "ors")
        ob = [kb.sb(st, [64, 512], BF16, "ob") for _ in range(2)]
        pA = [kb.ps(st, [128, 512], F32, "pgA") for _ in range(5)]
        pT = [kb.ps(st, [64, 1024], BF16, "pgT") for _ in range(2)]
        ca = [0]
        ct = [0]

        def psA():
            p = pA[ca[0] % 5]
            ca[0] += 1
            return p

        def psT():
            p = pT[ct[0] % 2]
            ct[0] += 1
            return p

        def hs(h):
            return slice(h * 64, (h + 1) * 64)

        def perhead_scale(out_t, in_t, in_tile, sc1, sc1_tile, sc2=None, sc2_tile=None, eng="dve"):
            for h in range(8):
                if sc2 is None:
                    kb.op(eng, lambda e: e.tensor_scalar(out=out_t[:, hs(h)], in0=in_t[:, hs(h)], scalar1=sc1[:, h:h + 1], scalar2=None, op0=ALU.mult),
                          reads=[in_tile, sc1_tile], writes=[out_t])
                elif isinstance(sc2, float):
                    kb.op(eng, lambda e: e.tensor_scalar(out=out_t[:, hs(h)], in0=in_t[:, hs(h)], scalar1=sc1[:, h:h + 1], scalar2=sc2, op0=ALU.mult, op1=ALU.mult),
                          reads=[in_tile, sc1_tile], writes=[out_t])
                else:
                    kb.op(eng, lambda e: e.tensor_scalar(out=out_t[:, hs(h)], in0=in_t[:, hs(h)], scalar1=sc1[:, h:h + 1], scalar2=sc2[:, h:h + 1], op0=ALU.mult, op1=ALU.mult),
                          reads=[in_tile, sc1_tile, sc2_tile], writes=[out_t])

        for s in range(NSEQ):
            if s >= dbg_seq:
                continue
            kb.op("dve", lambda e: e.memset(S32[:, :], 0.0), writes=[S32])
            kb.op("dve", lambda e: e.memset(Sb[:, :], 0.0), writes=[Sb])
            for gi in range(SEQ // G):
                if gi >= dbg_groups:
                    continue
                tok0 = gi * G
                for c in range(12):
                    kb.dma("sp", pre[:, c, :], D[f"GQKV{s}"][c, :, tok0:tok0 + G + 4], reads=[D[f"GQKV{s}"]], writes=[pre])
                kb.dma("sp", gtl[:, :, :], D[f"GG{s}"][tok0:tok0 + G, :].rearrange("(c p) h -> p c h", p=64), reads=[D[f"GG{s}"]], writes=[gtl])
                kb.dma("sp", btl[:, :, :], D[f"GB{s}"][tok0:tok0 + G, :].rearrange("(c p) h -> p c h", p=64), reads=[D[f"GB{s}"]], writes=[btl])
                kb.dma("sp", gzl[:, :, :], D[f"GZ{s}"][tok0:tok0 + G, :].rearrange("(c p) z -> p c z", p=64), reads=[D[f"GZ{s}"]], writes=[gzl])
                for c in range(12):
                    a_ = acc[c % 2]
                    kb.op("dve", lambda e: e.tensor_scalar(out=a_[:, :], in0=pre[:, c, 1:G + 1], scalar1=cwg[:, c, 0:1], scalar2=None, op0=ALU.mult), reads=[pre, cwg], writes=[a_])
                    for j in range(1, 4):
                        kb.op("dve", lambda e: e.scalar_tensor_tensor(out=a_[:, :], in0=pre[:, c, 1 + j:1 + j + G], scalar=cwg[:, c, j:j + 1], in1=a_[:, :], op0=ALU.mult, op1=ALU.add),
                              reads=[pre, cwg, a_], writes=[a_])
                    if c >= 8:
                        kb.op("act", lambda e: e.activation(out=vn[:, c - 8, :], in_=a_[:, :], func=AF.Silu), reads=[a_], writes=[vn])
                        continue
                    s_ = sl[c % 2]
                    q_ = sqb[c % 2]
                    r_ = rn[c % 2]
                    kb.op("act", lambda e: e.activation(out=s_[:, :], in_=a_[:, :], func=AF.Silu), reads=[a_], writes=[s_])
                    kb.op("pool", lambda e: e.tensor_tensor(out=q_[:, :], in0=s_[:, :], in1=s_[:, :], op=ALU.mult), reads=[s_], writes=[q_])
                    p = psA()
                    kb.op("pe", lambda e: e.matmul(p[:, :], lhsT=Bones[:, :], rhs=q_[:, :], start=True, stop=True), reads=[Bones, q_], writes=[p])
                    kb.op("dve", lambda e: e.tensor_scalar(out=r_[:, :], in0=p[:, :], scalar1=1e-6, scalar2=None, op0=ALU.add), reads=[p], writes=[r_])
                    kb.op("dve", lambda e: e.reciprocal(out=r_[:, :], in_=r_[:, :]), reads=[r_], writes=[r_])
                    kb.op("act", lambda e: e.activation(out=r_[:, :], in_=r_[:, :], func=AF.Sqrt), reads=[r_], writes=[r_])
                    kb.op("dve", lambda e: e.scalar_tensor_tensor(out=qkn[:, c, :], in0=s_[:, :], scalar=(0.125 if c < 4 else 1.0), in1=r_[:, :], op0=ALU.mult, op1=ALU.mult),
                          reads=[s_, r_], writes=[qkn])
                if dbg_stop <= 1:
                    continue
                kb.dma("sp", qodd[:, :, :], qkn[64:128, :, :], reads=[qkn], writes=[qodd])
                def pre_gen(cc, par):
                    (kvtm, gcs, ngcs, eg, egl, eglast, gU, arg, Dm, Ds, Dib, DiT, A32, PT32, PTb, Vb, Kbg, ktil, u32, wTb, qkT) = [L_[par] for L_ in PARL]
                    Y = Ypar[par]
                    YT = YTpar[par]
                    yi = 0
                    cs_ = slice(cc * 64, (cc + 1) * 64)
                    gt = gtl[:, cc, :]
                    bt = btl[:, cc, :]

                    def kT(h):
                        return qkn[0:64, 4 + h // 2, cs_] if h % 2 == 0 else qodd[:, 4 + h // 2, cs_]

                    def qT(h):
                        return qkn[0:64, h // 2, cs_] if h % 2 == 0 else qodd[:, h // 2, cs_]

                    p = psT()
                    for hp in range(4):
                        kb.op("pe", lambda e: e.transpose(out=p[:, hp * 128:(hp + 1) * 128], in_=qkn[:, 4 + hp, cs_], identity=identb[:, :]), reads=[qkn, identb], writes=[p])
                        kb.op("pe", lambda e: e.transpose(out=p[:, 512 + hp * 128:512 + (hp + 1) * 128], in_=vn[:, hp, cs_], identity=identb[:, :]), reads=[vn, identb], writes=[p])
                    kb.op("act", lambda e: e.activation(out=kvtm[:, :, :], in_=p[:, :].rearrange("p (a b) -> p a b", a=2), func=AF.Copy), reads=[p], writes=[kvtm])
                    Ktm = kvtm[:, 0, :]
                    Vtm = kvtm[:, 1, :]
                    yield
                    p = psA()
                    kb.op("pe", lambda e: e.matmul(p[0:64, 0:8], lhsT=LT[:, :], rhs=gt, start=True, stop=True), reads=[LT, gtl], writes=[p])
                    kb.op("dve", lambda e: e.tensor_copy(out=gcs[:, :], in_=p[0:64, 0:8]), reads=[p], writes=[gcs])
                    for h in range(8):
                        kb.op("pool", lambda e: e.tensor_scalar(out=gU[:, hs(h)], in0=LT[:, :], scalar1=gt[:, h:h + 1], scalar2=None, op0=ALU.mult), reads=[LT, gtl], writes=[gU])
                    pR = psA()
                    kb.op("pe", lambda e: e.matmul(pR[0:64, :], lhsT=ones64[:, 0:64], rhs=gU[:, :], start=True, stop=True), reads=[ones64, gU], writes=[pR])
                    kb.op("act", lambda e: e.activation(out=eg[:, :], in_=gcs[:, :], func=AF.Exp), reads=[gcs], writes=[eg])
                    kb.op("act", lambda e: e.activation(out=eglast[:, :], in_=pR[0:64, 63:512:64], func=AF.Exp), reads=[pR], writes=[eglast])
                    kb.op("dve", lambda e: e.tensor_tensor(out=egl[:, :], in0=pR[0:64, 63:512:64], in1=gcs[:, :], op=ALU.subtract), reads=[pR, gcs], writes=[egl])
                    kb.op("act", lambda e: e.activation(out=egl[:, :], in_=egl[:, :], func=AF.Exp), reads=[egl], writes=[egl])
                    kb.op("dve", lambda e: e.tensor_scalar(out=ngcs[:, :], in0=gcs[:, :], scalar1=-1.0, scalar2=None, op0=ALU.mult), reads=[gcs], writes=[ngcs])
                    for h in range(8):
                        kb.op("dve", lambda e: e.tensor_scalar(out=arg[:, hs(h)], in0=pR[0:64, hs(h)], scalar1=ngcs[:, h:h + 1], scalar2=-1.0, op0=ALU.add, op1=ALU.mult),
                              reads=[pR, ngcs], writes=[arg])
                    kb.op("dve", lambda e: e.tensor_scalar(out=arg[:, :], in0=arg[:, :], scalar1=0.0, scalar2=None, op0=ALU.min), reads=[arg], writes=[arg])
                    kb.op("act", lambda e: e.activation(out=Dm[:, :], in_=arg[:, :], func=AF.Exp), reads=[arg], writes=[Dm])
                    kb.op("dve", lambda e: e.tensor_tensor(out=Ds[:, :], in0=Dm[:, :], in1=trilS[:, :], op=ALU.mult), reads=[Dm, trilS], writes=[Ds])
                    kb.op("dve", lambda e: e.tensor_tensor(out=Dib[:, :], in0=Dm[:, :], in1=trilI[:, :], op=ALU.mult), reads=[Dm, trilI], writes=[Dib])
                    yield
                    p = psT()
                    for h in range(8):
                        kb.op("pe", lambda e: e.transpose(out=p[:, hs(h)], in_=Dib[:, hs(h)], identity=identb[0:64, 0:64]), reads=[Dib, identb], writes=[p])
                    kb.op("act", lambda e: e.activation(out=DiT[:, :], in_=p[:, 0:512], func=AF.Copy), reads=[p], writes=[DiT])
                    yield
                    pM1 = psA()
                    pM2 = psA()
                    for h in range(8):
                        kb.op("pe", lambda e: e.matmul(pM1[0:64, hs(h)], lhsT=kT(h), rhs=kT(h), start=True, stop=True), reads=[qkn, qodd], writes=[pM1])
                    for h in range(8):
                        kb.op("pe", lambda e: e.matmul(pM2[0:64, hs(h)], lhsT=kT(h), rhs=qT(h), start=True, stop=True), reads=[qkn, qodd], writes=[pM2])
                    kb.op("dve", lambda e: e.tensor_tensor(out=A32[:, :], in0=pM1[0:64, :], in1=Ds[:, :], op=ALU.mult), reads=[pM1, Ds], writes=[A32])
                    kb.op("dve", lambda e: e.tensor_tensor(out=qkT[:, :], in0=pM2[0:64, :], in1=DiT[:, :], op=ALU.mult), reads=[pM2, DiT], writes=[qkT])
                    y0 = Y[yi % 2]
                    yt0 = YT[yi % 2]
                    yi += 1
                    perhead_scale(y0, A32, A32, bt, btl, -1.0)
                    p = psT()
                    for h in range(8):
                        kb.op("pe", lambda e: e.transpose(out=p[:, hs(h)], in_=y0[:, hs(h)], identity=identb[0:64, 0:64]), reads=[y0, identb], writes=[p])
                    kb.op("act", lambda e: e.activation(out=yt0[:, :], in_=p[:, 0:512], func=AF.Copy), reads=[p], writes=[yt0])
                    yield
                    kb.op("dve", lambda e: e.tensor_tensor(out=PT32[:, :], in0=yt0[:, :], in1=I8[:, :], op=ALU.add), reads=[yt0, I8], writes=[PT32])
                    kb.op("act", lambda e: e.activation(out=PTb[:, :], in_=PT32[:, :], func=AF.Copy), reads=[PT32], writes=[PTb])
                    yc, ytc = y0, yt0
                    for it in range(5):
                        yn = Y[yi % 2]
                        ytn = YT[yi % 2]
                        yi += 1
                        p1 = psA()
                        p2 = psA()
                        for h in range(8):
                            kb.op("pe", lambda e: e.matmul(p1[0:64, hs(h)], lhsT=ytc[:, hs(h)], rhs=yc[:, hs(h)], start=True, stop=True), reads=[ytc, yc], writes=[p1])
                        for h in range(8):
                            kb.op("pe", lambda e: e.matmul(p2[0:64, hs(h)], lhsT=yc[:, hs(h)], rhs=ytc[:, hs(h)], start=True, stop=True), reads=[ytc, yc], writes=[p2])
                        kb.op("act", lambda e: e.activation(out=yn[:, :], in_=p1[0:64, :], func=AF.Copy), reads=[p1], writes=[yn])
                        kb.op("dve", lambda e: e.tensor_copy(out=ytn[:, :], in_=p2[0:64, :]), reads=[p2], writes=[ytn])
                        yield
                        p3 = psA()
                        for h in range(8):
                            kb.op("pe", lambda e: e.matmul(p3[0:64, hs(h)], lhsT=yn[:, hs(h)], rhs=PTb[:, hs(h)], start=True, stop=True), reads=[yn, PTb], writes=[p3])
                        kb.op("dve", lambda e: e.tensor_tensor(out=PT32[:, :], in0=PT32[:, :], in1=p3[0:64, :], op=ALU.add), reads=[PT32, p3], writes=[PT32])
                        kb.op("act", lambda e: e.activation(out=PTb[:, :], in_=PT32[:, :], func=AF.Copy), reads=[PT32], writes=[PTb])
                        yc, ytc = yn, ytn
                        yield
                    yield
                    perhead_scale(Vb, Vtm, kvtm, bt, btl, eng="pool")
                    perhead_scale(Kbg, Ktm, kvtm, bt, btl, eg, eg)
                    perhead_scale(ktil, Ktm, kvtm, egl, egl, eng="pool")
                    pu = psA()
                    pw = psA()
                    for h in range(8):
                        kb.op("pe", lambda e: e.matmul(pu[0:64, hs(h)], lhsT=PTb[:, hs(h)], rhs=Vb[:, hs(h)], start=True, stop=True), reads=[PTb, Vb], writes=[pu])
                    for h in range(8):
                        kb.op("pe", lambda e: e.matmul(pw[0:64, hs(h)], lhsT=Kbg[:, hs(h)], rhs=PTb[:, hs(h)], start=True, stop=True), reads=[PTb, Kbg], writes=[pw])
                    kb.op("act", lambda e: e.activation(out=u32[:, :], in_=pu[0:64, :], func=AF.Copy), reads=[pu], writes=[u32])
                    kb.op("act", lambda e: e.activation(out=wTb[:, :], in_=pw[0:64, :], func=AF.Copy), reads=[pw], writes=[wTb])

                def rec_fn(cc, par):
                    (kvtm, gcs, ngcs, eg, egl, eglast, gU, arg, Dm, Ds, Dib, DiT, A32, PT32, PTb, Vb, Kbg, ktil, u32, wTb, qkT) = [L_[par] for L_ in PARL]
                    cs_ = slice(cc * 64, (cc + 1) * 64)

                    def qT(h):
                        return qkn[0:64, h // 2, cs_] if h % 2 == 0 else qodd[:, h // 2, cs_]
                    pws = psA()
                    for h in range(8):
                        kb.op("pe", lambda e: e.matmul(pws[0:64, hs(h)], lhsT=wTb[:, hs(h)], rhs=Sb[:, hs(h)], start=True, stop=True), reads=[wTb, Sb], writes=[pws])
                    kb.op("dve", lambda e: e.tensor_tensor(out=vnew[:, :], in0=u32[:, :], in1=pws[0:64, :], op=ALU.subtract), reads=[u32, pws], writes=[vnew])
                    po1 = psA()
                    po2 = psA()
                    for h in range(8):
                        kb.op("pe", lambda e: e.matmul(po1[0:64, hs(h)], lhsT=qT(h), rhs=Sb[:, hs(h)], start=True, stop=True), reads=[qkn, qodd, Sb], writes=[po1])
                    for h in range(8):
                        kb.op("pe", lambda e: e.matmul(po2[0:64, hs(h)], lhsT=qkT[:, hs(h)], rhs=vnew[:, hs(h)], start=True, stop=True), reads=[qkT, vnew], writes=[po2])
                    pS_ = psA()
                    for h in range(8):
                        kb.op("pe", lambda e: e.matmul(pS_[0:64, hs(h)], lhsT=ktil[:, hs(h)], rhs=vnew[:, hs(h)], start=True, stop=True), reads=[ktil, vnew], writes=[pS_])
                    kb.op("act", lambda e: e.activation(out=o2[:, :], in_=po2[0:64, :], func=AF.Copy), reads=[po2], writes=[o2])
                    for h in range(8):
                        kb.op("dve", lambda e: e.scalar_tensor_tensor(out=o32[:, hs(h)], in0=po1[0:64, hs(h)], scalar=eg[:, h:h + 1], in1=o2[:, hs(h)], op0=ALU.mult, op1=ALU.add),
                              reads=[po1, eg, o2], writes=[o32])
                    for h in range(8):
                        kb.op("dve", lambda e: e.scalar_tensor_tensor(out=S32[:, hs(h)], in0=S32[:, hs(h)], scalar=eglast[:, h:h + 1], in1=pS_[0:64, hs(h)], op0=ALU.mult, op1=ALU.add),
                              reads=[S32, eglast, pS_], writes=[S32])
                    kb.op("act", lambda e: e.activation(out=Sb[:, :], in_=S32[:, :], func=AF.Copy), reads=[S32], writes=[Sb])
                    kb.op("pool", lambda e: e.tensor_tensor(out=osq[:, :], in0=o32[:, :], in1=o32[:, :], op=ALU.mult), reads=[o32], writes=[osq])
                    kb.op("dve", lambda e: e.tensor_reduce(out=oss[:, :], in_=osq[:, :].rearrange("p (h d) -> p h d", h=8), axis=AX.X, op=ALU.add), reads=[osq], writes=[oss])
                    kb.op("dve", lambda e: e.tensor_scalar(out=oss[:, :], in0=oss[:, :], scalar1=1.0 / 64, scalar2=EPS, op0=ALU.mult, op1=ALU.add), reads=[oss], writes=[oss])
                    kb.op("dve", lambda e: e.reciprocal(out=oss[:, :], in_=oss[:, :]), reads=[oss], writes=[oss])
                    kb.op("act", lambda e: e.activation(out=ors[:, :], in_=oss[:, :], func=AF.Sqrt), reads=[oss], writes=[ors])
                    perhead_scale(o32, o32, o32, ors, ors)
                    kb.op("pool", lambda e: e.tensor_tensor(out=o32[:, :], in0=o32[:, :], in1=nw8[:, :, :].rearrange("p h d -> p (h d)"), op=ALU.mult), reads=[o32, nw8], writes=[o32])
                    o_ = ob[cc % 2]
                    kb.op("dve", lambda e: e.tensor_tensor(out=o_[:, :], in0=o32[:, :], in1=gzl[:, cc, :], op=ALU.mult), reads=[o32, gzl], writes=[o_])
                    kb.dma("sp", D[f"MIX{s}"][tok0 + cc * 64:tok0 + (cc + 1) * 64, 512:1024], o_[:, :], reads=[o_], writes=[D[f"MIX{s}"]])

                for pair in range(4):
                    gens = [pre_gen(2 * pair, 0), pre_gen(2 * pair + 1, 1)]
                    while gens:
                        for g_ in list(gens):
                            try:
                                next(g_)
                            except StopIteration:
                                gens.remove(g_)
                    rec_fn(2 * pair, 0)
                    rec_fn(2 * pair + 1, 1)
        kb.barrier()
```
